# Optimizing a Trainium2 kernel written in Bass

```python
import jax, jax.numpy as jnp
from jax import lax
import numpy as np

D_MODEL = 1024
BATCH = 32
SEQ = 2048
DEPTH = 4
DEC_BATCH = 8
DEC_SEQ = 4096
PAST_LEN = 128

GRID_W = 64
HEAD_DIM = 64
A_HEADS = 8
A_KV_HEADS = 2
A_GROUP = A_HEADS // A_KV_HEADS
B_HEADS = 8
NA_ROWS_MAX = 8
NA_COLS = 16
Q_BLOCK = 128
D_FF = 4 * D_MODEL
ROPE_THETA = 10000.0
ALPHA = (2 * DEPTH) ** 0.25
BETA = (8 * DEPTH) ** -0.25
EPS = 1e-6
A_Q = A_HEADS * HEAD_DIM
A_KV = A_KV_HEADS * HEAD_DIM
B_W = B_HEADS * HEAD_DIM
N_IN = A_Q + 2 * A_KV + 3 * B_W + 2 * D_MODEL
SPLIT_POINTS = [A_Q, A_Q + A_KV, A_Q + 2 * A_KV, A_Q + 2 * A_KV + B_W,
                A_Q + 2 * A_KV + 2 * B_W, A_Q + 2 * A_KV + 3 * B_W,
                A_Q + 2 * A_KV + 3 * B_W + D_MODEL]
RPB_R = 2 * NA_ROWS_MAX - 1
RPB_C = 2 * NA_COLS - 1

kernel_name = "hybrid_gqa_natten_deepnorm_encoder"


def _layer_norm(x):
    xf = x.astype(jnp.float32)
    mu = jnp.mean(xf, -1, keepdims=True)
    var = jnp.mean(jnp.square(xf - mu), -1, keepdims=True)
    return ((xf - mu) * lax.rsqrt(var + EPS)).astype(x.dtype)


def _layer_norm_affine(x, g, b):
    xf = x.astype(jnp.float32)
    mu = jnp.mean(xf, -1, keepdims=True)
    var = jnp.mean(jnp.square(xf - mu), -1, keepdims=True)
    y = (xf - mu) * lax.rsqrt(var + EPS) * g.astype(jnp.float32) + b.astype(jnp.float32)
    return y.astype(x.dtype)


def _rms_norm(x, g):
    xf = x.astype(jnp.float32)
    y = xf * lax.rsqrt(jnp.mean(jnp.square(xf), -1, keepdims=True) + EPS) * g.astype(jnp.float32)
    return y.astype(x.dtype)


def _axial_rope_tables(n_tok):
    t = np.arange(n_tok)
    n_pairs_axis = HEAD_DIM // 4
    inv_freq = 1.0 / (ROPE_THETA ** (np.arange(n_pairs_axis) * 2.0 / (HEAD_DIM // 2)))
    ang = np.concatenate([(t // GRID_W)[:, None] * inv_freq[None, :],
                          (t % GRID_W)[:, None] * inv_freq[None, :]], -1)
    return jnp.asarray(np.cos(ang), jnp.float32), jnp.asarray(np.sin(ang), jnp.float32)


def _apply_rope(x, cos, sin):
    xf = x.astype(jnp.float32).reshape(x.shape[:-1] + (HEAD_DIM // 2, 2))
    x0, x1 = xf[..., 0], xf[..., 1]
    c = cos[None, :, None, :]
    s = sin[None, :, None, :]
    out = jnp.stack([x0 * c - x1 * s, x0 * s + x1 * c], -1).reshape(x.shape)
    return out.astype(x.dtype)


def _gqa_attention(q, k, v):
    bn, s_len = q.shape[0], q.shape[1]
    nblk = s_len // Q_BLOCK
    qb = q.reshape(bn, nblk, Q_BLOCK, A_KV_HEADS, A_GROUP, HEAD_DIM).transpose(1, 0, 2, 3, 4, 5)
    scale = HEAD_DIM ** -0.5

    def one_block(qi):
        s = jnp.einsum('bqkgd,bskd->bkgqs', qi, k).astype(jnp.float32) * scale
        p = jax.nn.softmax(s, axis=-1).astype(v.dtype)
        return jnp.einsum('bkgqs,bskd->bqkgd', p, v)

    o = lax.map(one_block, qb)
    return o.transpose(1, 0, 2, 3, 4, 5).reshape(bn, s_len, A_Q)


def _neighbourhood_tables(n_tok):
    rows = n_tok // GRID_W
    kr = min(NA_ROWS_MAX, rows)
    t = np.arange(n_tok)
    r, c = t // GRID_W, t % GRID_W
    rs = np.clip(r - kr // 2, 0, rows - kr)
    cs = np.clip(c - NA_COLS // 2, 0, GRID_W - NA_COLS)
    kr_idx = rs[:, None] + np.arange(kr)[None, :]
    kc_idx = cs[:, None] + np.arange(NA_COLS)[None, :]
    key_idx = (kr_idx[:, :, None] * GRID_W + kc_idx[:, None, :]).reshape(n_tok, -1)
    dr = kr_idx - r[:, None] + (NA_ROWS_MAX - 1)
    dc = kc_idx - c[:, None] + (NA_COLS - 1)
    bias_idx = (dr[:, :, None] * RPB_C + dc[:, None, :]).reshape(n_tok, -1)
    return key_idx.astype(np.int32), bias_idx.astype(np.int32)


def _neighbourhood_attention(q, k, v, rpb):
    bn, s_len = q.shape[0], q.shape[1]
    key_idx, bias_idx = _neighbourhood_tables(s_len)
    nblk = s_len // GRID_W
    n_win = key_idx.shape[1]
    qb = q.reshape(bn, nblk, GRID_W, B_HEADS, HEAD_DIM).transpose(1, 0, 2, 3, 4)
    kidx = jnp.asarray(key_idx.reshape(nblk, GRID_W, n_win))
    bias = rpb.reshape(B_HEADS, -1)[:, jnp.asarray(bias_idx)].astype(jnp.float32)
    bias = bias.reshape(B_HEADS, nblk, GRID_W, n_win).transpose(1, 0, 2, 3)
    scale = HEAD_DIM ** -0.5

    def one_block(args):
        qi, ki, bi = args
        kg = k[:, ki]
        vg = v[:, ki]
        s = jnp.einsum('bqhd,bqwhd->bhqw', qi, kg).astype(jnp.float32) * scale + bi[None]
        p = jax.nn.softmax(s, axis=-1).astype(v.dtype)
        return jnp.einsum('bhqw,bqwhd->bqhd', p, vg)

    o = lax.map(one_block, (qb, kidx, bias))
    return o.transpose(1, 0, 2, 3, 4).reshape(bn, s_len, B_W)


def _encoder_layer(x, c, cos, sin, w_ada, b_ada, w_in, q_norm_a, k_norm_a, rpb_b,
                   w_out_a, w_out_b, w_out, ln_mix_g, ln_mix_b,
                   w_ff1, w_ff2, ln_ff_g, ln_ff_b):
    bn, s_len, _ = x.shape
    mod = jnp.dot(jax.nn.silu(c), w_ada) + b_ada
    sh1, sc1, g1, sh2, sc2, g2 = jnp.split(mod[:, None, :], 6, axis=-1)

    u = _layer_norm(x) * (1.0 + sc1) + sh1
    proj = jnp.dot(u, w_in)
    qa, ka, va, qb, kb, vb, ga, gb = jnp.split(proj, SPLIT_POINTS, axis=-1)
    qa = _apply_rope(_rms_norm(qa.reshape(bn, s_len, A_HEADS, HEAD_DIM), q_norm_a), cos, sin)
    ka = _apply_rope(_rms_norm(ka.reshape(bn, s_len, A_KV_HEADS, HEAD_DIM), k_norm_a), cos, sin)
    va = va.reshape(bn, s_len, A_KV_HEADS, HEAD_DIM)
    oa = jnp.dot(_gqa_attention(qa, ka, va), w_out_a)
    ob = jnp.dot(_neighbourhood_attention(
        qb.reshape(bn, s_len, B_HEADS, HEAD_DIM),
        kb.reshape(bn, s_len, B_HEADS, HEAD_DIM),
        vb.reshape(bn, s_len, B_HEADS, HEAD_DIM), rpb_b), w_out_b)
    mixed = jnp.dot(jax.nn.sigmoid(ga) * oa + jax.nn.sigmoid(gb) * ob, w_out)
    x = _layer_norm_affine(ALPHA * x + (1.0 + g1) * mixed, ln_mix_g, ln_mix_b)

    u = _layer_norm(x) * (1.0 + sc2) + sh2
    h = jnp.dot(jnp.square(jax.nn.relu(jnp.dot(u, w_ff1))), w_ff2)
    x = _layer_norm_affine(ALPHA * x + (1.0 + g2) * h, ln_ff_g, ln_ff_b)
    return x


def setup_inputs(seed: int = 0) -> dict:
    key = jax.random.key(seed)
    ks = jax.random.split(key, 20)
    f32 = jnp.float32
    nrm = lambda k, shape, s: jax.random.normal(k, shape, f32) * s
    return {
        "x_prompt": nrm(ks[0], (BATCH, SEQ, D_MODEL), 1.0),
        "x_sample": nrm(ks[1], (DEC_BATCH, DEC_SEQ, D_MODEL), 1.0),
        "c_prompt": nrm(ks[2], (BATCH, D_MODEL), 1.0),
        "c_sample": nrm(ks[3], (DEC_BATCH, D_MODEL), 1.0),
        "w_ada": nrm(ks[4], (DEPTH, D_MODEL, 6 * D_MODEL), 0.5 * D_MODEL ** -0.5),
        "b_ada": nrm(ks[5], (DEPTH, 6 * D_MODEL), 0.02),
        "w_in": nrm(ks[6], (DEPTH, D_MODEL, N_IN), D_MODEL ** -0.5),
        "q_norm_a": 1.0 + nrm(ks[7], (DEPTH, HEAD_DIM), 0.05),
        "k_norm_a": 1.0 + nrm(ks[8], (DEPTH, HEAD_DIM), 0.05),
        "rpb_b": nrm(ks[9], (DEPTH, B_HEADS, RPB_R, RPB_C), 0.1),
        "w_out_a": nrm(ks[10], (DEPTH, A_Q, D_MODEL), A_Q ** -0.5),
        "w_out_b": nrm(ks[11], (DEPTH, B_W, D_MODEL), B_W ** -0.5),
        "w_out": nrm(ks[12], (DEPTH, D_MODEL, D_MODEL), BETA * D_MODEL ** -0.5),
        "ln_mix_g": 1.0 + nrm(ks[13], (DEPTH, D_MODEL), 0.05),
        "ln_mix_b": nrm(ks[14], (DEPTH, D_MODEL), 0.02),
        "w_ff1": nrm(ks[15], (DEPTH, D_MODEL, D_FF), D_MODEL ** -0.5),
        "w_ff2": nrm(ks[16], (DEPTH, D_FF, D_MODEL), BETA * D_FF ** -0.5),
        "ln_ff_g": 1.0 + nrm(ks[17], (DEPTH, D_MODEL), 0.05),
        "ln_ff_b": nrm(ks[18], (DEPTH, D_MODEL), 0.02),
    }


def reference(x_prompt, x_sample, c_prompt, c_sample, w_ada, b_ada, w_in, q_norm_a, k_norm_a,
              rpb_b, w_out_a, w_out_b, w_out, ln_mix_g, ln_mix_b, w_ff1, w_ff2, ln_ff_g, ln_ff_b):
    cos_p, sin_p = _axial_rope_tables(x_prompt.shape[1])
    cos_s, sin_s = _axial_rope_tables(x_sample.shape[1])
    y_prompt = x_prompt
    y_sample = x_sample
    for l in range(DEPTH):
        layer_params = (w_ada[l], b_ada[l], w_in[l], q_norm_a[l], k_norm_a[l], rpb_b[l],
                        w_out_a[l], w_out_b[l], w_out[l], ln_mix_g[l], ln_mix_b[l],
                        w_ff1[l], w_ff2[l], ln_ff_g[l], ln_ff_b[l])
        y_prompt = _encoder_layer(y_prompt, c_prompt, cos_p, sin_p, *layer_params)
        y_sample = _encoder_layer(y_sample, c_sample, cos_s, sin_s, *layer_params)
    return (y_prompt, y_sample)
```

```python
import numpy as np
from contextlib import ExitStack
from collections import deque

import concourse.bass as bass
import concourse.mybir as mybir
from concourse.bass_utils import run_bass_kernel_spmd

F32 = mybir.dt.float32
BF16 = mybir.dt.bfloat16
ALU = mybir.AluOpType
AF = mybir.ActivationFunctionType
AX = mybir.AxisListType

D = 1024
NIN = 4352
DFF = 4096
KC = 8
GRID_W = 64
ALPHA = 8.0 ** 0.25
EPS = 1e-6
TB = 512
NT = 4
C_QA, C_KA, C_VA, C_QB, C_KB, C_VB, C_GA, C_GB = 0, 512, 640, 768, 1280, 1792, 2304, 3328
UNIT = 2048
NSLOT = 7
NEG_FILL = -30000.0

def unit_table():
    units = []
    units.append(("kvA", 0, "w_in", 0, 8, C_KA, 256))
    for i in range(2):
        units.append(("vB", i, "w_in", 4 * i, 4, C_VB, 512))
    for i in range(2):
        units.append(("kB", i, "w_in", 0, 8, C_KB + 256 * i, 256))
    for i in range(2):
        units.append(("qA", i, "w_in", 4 * i, 4, C_QA, 512))
    for i in range(2):
        units.append(("qB", i, "w_in", 0, 8, C_QB + 256 * i, 256))
    for i in range(4):
        units.append(("ga", i, "w_in", 0, 8, C_GA + 256 * i, 256))
    for i in range(4):
        units.append(("gb", i, "w_in", 0, 8, C_GB + 256 * i, 256))
    for i in range(2):
        units.append(("oa", i, "w_out_a", 0, 4, 512 * i, 512))
    for i in range(2):
        units.append(("ob", i, "w_out_b", 0, 4, 512 * i, 512))
    for h in range(2):
        for i in range(2):
            units.append(("wo", 2 * h + i, "w_out", 4 * i, 4, 512 * h, 512))
    for i in range(16):
        units.append(("f1", i, "w_ff1", 0, 8, 256 * i, 256))
    for h in range(2):
        for g in range(8):
            units.append(("f2", 8 * h + g, "w_ff2", 4 * g, 4, 512 * h, 512))
    return units


UNITS = unit_table()
UIDX = {(u[0], u[1]): i for i, u in enumerate(UNITS)}
NU = len(UNITS)


class Tok:
    __slots__ = ("sem", "val", "eng")

    def __init__(self, sem, val, eng):
        self.sem, self.val, self.eng = sem, val, eng


class Region:
    __slots__ = ("name", "w", "r")

    def __init__(self, name):
        self.name, self.w, self.r = name, None, {}


class Chan:
    def __init__(self, sem):
        self.sem, self.n, self.last = sem, 0, None


ENGS = ("pe", "act", "dve", "pool", "sp")


class _Recorder:
    def __init__(self):
        self.call = None

    def __getattr__(self, name):
        def f(*args, **kwargs):
            assert self.call is None
            self.call = (name, args, kwargs)
            return None
        return f


def _eager(fn):
    rec = _Recorder()
    fn(rec)
    name, args, kwargs = rec.call

    def replay(e):
        return getattr(e, name)(*args, **kwargs)
    return replay


class Prog:
    def __init__(self, nc, es):
        self.nc = nc
        self.es = es
        self.ops = {e: [] for e in ENGS}
        self.sem = {e: es.enter_context(nc.semaphore("s_" + e)) for e in ENGS}
        self.cnt = {e: 0 for e in ENGS}
        self.pending = {e: [] for e in ENGS}
        self.last_arena = {}
        self.guard = []
        self.nsem = 5

    def chan(self, name):
        self.nsem += 1
        return Chan(self.es.enter_context(self.nc.semaphore(name)))

    def _deps(self, eng, reads, writes, extra):
        waits = []
        for r in reads:
            if r.w is not None and not (r.w.eng == eng and eng == "pe"):
                waits.append(r.w)
        for w in writes:
            if w.w is not None and not (w.w.eng == eng and eng == "pe"):
                waits.append(w.w)
            for e, t in w.r.items():
                if not (e == eng and eng == "pe"):
                    waits.append(t)
        waits.extend(t for t in extra if t is not None)
        return waits

    def op(self, eng, fn, reads=(), writes=(), extra=(), signal=True, arena=False):
        if arena:
            extra = list(extra) + self.guard
        waits = self._deps(eng, reads, writes, extra)
        if signal:
            self.cnt[eng] += 1
            tok = Tok(self.sem[eng], self.cnt[eng], eng)
            for p in self.pending[eng]:
                p.val = self.cnt[eng]
            self.pending[eng] = []
        else:
            tok = Tok(self.sem[eng], None, eng)
            self.pending[eng].append(tok)
        self.ops[eng].append((_eager(fn), waits, tok if signal else None, None))
        for r in reads:
            r.r[eng] = tok
        for w in writes:
            w.w = tok
            w.r = {}
        if arena:
            self.last_arena[eng] = tok
        return tok

    def dma(self, q, out_ap, in_ap, chan, reads=(), writes=(), extra=(), arena=False, serialize=True):
        if arena:
            extra = list(extra) + self.guard
        waits = self._deps("dma", reads, writes, list(extra) + ([chan.last] if serialize else []))
        chan.n += 1
        tok = Tok(chan.sem, 16 * chan.n, None)
        chan.last = tok

        def fn(e, out_ap=out_ap, in_ap=in_ap):
            return e.dma_start(out=out_ap, in_=in_ap)

        self.ops[q].append((fn, waits, None, tok))
        for r in reads:
            r.r["dma" + str(id(chan))] = tok
        for w in writes:
            w.w = tok
            w.r = {}
        if arena:
            self.last_arena["dma" + str(id(chan))] = tok
        return tok

    def switch_mode(self):
        self.guard = [t for t in self.last_arena.values()]
        self.last_arena = {}

    def check(self):
        for e in ENGS:
            assert not self.pending[e], f"unresolved pending tokens on {e}"
        ptr = {e: 0 for e in ENGS}
        semv = {}
        total = sum(len(v) for v in self.ops.values())
        done = 0
        progress = True
        while progress:
            progress = False
            for e in ENGS:
                ops = self.ops[e]
                while ptr[e] < len(ops):
                    fn, waits, tok, dtok = ops[ptr[e]]
                    ok = True
                    for w in waits:
                        assert w.val is not None
                        if semv.get(id(w.sem), 0) < w.val:
                            ok = False
                            break
                    if not ok:
                        break
                    if tok is not None:
                        semv[id(tok.sem)] = semv.get(id(tok.sem), 0) + 1
                        assert semv[id(tok.sem)] == tok.val
                    if dtok is not None:
                        semv[id(dtok.sem)] = semv.get(id(dtok.sem), 0) + 16
                        assert semv[id(dtok.sem)] == dtok.val
                    ptr[e] += 1
                    done += 1
                    progress = True
        if done != total:
            msg = {e: (ptr[e], len(self.ops[e])) for e in ENGS}
            raise RuntimeError(f"static deadlock: {msg}")

    def emit(self, eng_name, e):
        waited = {}
        for fn, waits, tok, dtok in self.ops[eng_name]:
            for w in waits:
                k = id(w.sem)
                if waited.get(k, 0) < w.val:
                    e.wait_ge(w.sem, w.val)
                    waited[k] = w.val
            ins = fn(e)
            if tok is not None:
                ins.then_inc(tok.sem, 1)
            if dtok is not None:
                ins.then_inc(dtok.sem, 16)


def mk(tensor, off, dims):
    return bass.AP(tensor, off, [list(d) for d in dims])


class _Stop(Exception):
    pass


def build_program(seq_lens, depth, stop_at=None):
    NS = len(seq_lens)
    L = depth
    SMAX = max(seq_lens)
    lens_set = sorted(set(seq_lens))
    nc = bass.Bass("TRN2", target_bir_lowering=False)

    def dram_in(name, shape, dt=F32):
        return nc.dram_tensor(name, list(shape), dt, kind="ExternalInput")

    xs = [dram_in(f"x{s}", [seq_lens[s], D]) for s in range(NS)]
    ys = [nc.dram_tensor(f"y{s}", [seq_lens[s], D], F32, kind="ExternalOutput") for s in range(NS)]
    cT = dram_in("cT", [128, KC * NS])
    w_ada = dram_in("w_ada", [L, D, 6 * D])
    b_ada = dram_in("b_ada", [L, 6 * D])
    b_ada_pm = dram_in("b_ada_pm", [128, L * 48])
    wsrc = {
        "w_in": dram_in("w_in", [L, D, NIN]),
        "w_out_a": dram_in("w_out_a", [L, 512, D]),
        "w_out_b": dram_in("w_out_b", [L, 512, D]),
        "w_out": dram_in("w_out", [L, D, D]),
        "w_ff1": dram_in("w_ff1", [L, D, DFF]),
        "w_ff2": dram_in("w_ff2", [L, DFF, D]),
    }
    qkg = dram_in("qkg", [L, 640])
    lnv = {k: dram_in(k, [L, D]) for k in ("ln_mix_g", "ln_mix_b", "ln_ff_g", "ln_ff_b")}
    bt = dram_in("bt", [L, 5, 8, 128, 640])
    cstab = {S: dram_in(f"cs{S}", [S, 64]) for S in lens_set}

    ws = [nc.dram_tensor(f"ws{l}", [NU, 128, UNIT], BF16) for l in range(L)]
    ebs = nc.dram_tensor("ebs", [L, 5, 8, 128, 640], BF16)
    kbs = nc.dram_tensor("kbs", [128, 4, SMAX], BF16)
    vbs = nc.dram_tensor("vbs", [SMAX // 128, 128, 520], BF16)
    grow = nc.dram_tensor("grow", [L, NS, 2 * D], F32)

    es = ExitStack()
    with es:
        P = Prog(nc, es)

        def sb(name, shape, dt):
            return es.enter_context(nc.sbuf_tensor(name, list(shape), dt))

        ident = sb("ident", [128, 128], BF16)
        negh = sb("negh", [128, 16], F32)
        MODS = sb("MODS", [128, L * NS * 4 * KC], F32)
        G12 = sb("G12", [128, 2, D], F32)
        LNV = sb("LNV", [128, 4, D], F32)
        QKG = sb("QKG", [128, 640], F32)
        CSr = sb("CSr", [128, 2, 64], F32)
        kAT = sb("kAT", [128, SMAX], BF16)
        vAug = sb("vAug", [128, SMAX // 128, 256], BF16)
        kBr = sb("kBr", [128, 4, 6, 128], BF16)
        vBr = sb("vBr", [128, 6, 520], BF16)
        EBr = sb("EBr", [128, 3, 640], BF16)
        wring = sb("wring", [128, NSLOT, UNIT], BF16)
        xt = sb("xt", [128, NT, D], F32)
        xn = sb("xn", [128, 2, D], BF16)
        uT = sb("uT", [128, KC, TB], BF16)
        WK = sb("WK", [128, 2, D], F32)
        st6 = sb("st6", [128, 4, 12], F32)
        mv = sb("mv", [128, 4, 4], F32)
        ssq = sb("ssq", [128, 2, 16], F32)
        RT = sb("RT", [128, 2, 256], F32)
        QR = sb("QR", [128, 512], BF16)
        kBst = sb("kBst", [128, 4, TB], BF16)
        vBst = sb("vBst", [128, 2, 520], BF16)
        silc = sb("silc", [128, KC * NS], BF16)
        A_QAT, A_QBT, A_PA, A_PB, A_REC, A_AOT, A_OB, A_RECB, A_OBT, A_TA, A_TB_, A_T12, A_MIT = (
            0, 2048, 4096, 6144, 7424, 9472, 11520, 12544, 12608, 14656, 16704, 18752, 20800)
        A_END = 20800 + 4096
        A_HT, A_R = 0, 16384
        ARENA = max(A_END, A_R + 1024)
        arena = sb("arena", [128, ARENA], BF16)

        def abf(off, dims):
            return mk(arena, off, [[ARENA, 128]] + dims)

        banks = [es.enter_context(nc.psum_tensor(f"bank{i}", [128, 512], F32)) for i in range(8)]
        banks_bf = [b.bitcast(BF16) for b in banks]
        bank_r = [Region(f"bank{i}") for i in range(8)]
        free_banks = deque(range(8))

        def acq():
            assert free_banks, "out of PSUM banks"
            return free_banks.popleft()

        def rel(b):
            free_banks.append(b)

        R = {}

        def reg(name):
            if name not in R:
                R[name] = Region(name)
            return R[name]

        ch_x = [P.chan(f"chx{i}") for i in range(NT)]
        ch_w = [P.chan(f"chw{i}") for i in range(NSLOT)]
        ch_st = [P.chan(f"chst{i}") for i in range(4)]
        ch_kb = [P.chan(f"chkb{i}") for i in range(6)]
        ch_vb = [P.chan(f"chvb{i}") for i in range(6)]
        ch_eb = [P.chan(f"cheb{i}") for i in range(3)]
        ch_cs = [P.chan(f"chcs{i}") for i in range(2)]
        ch_misc = P.chan("chmisc")
        ch_miscp = P.chan("chmiscp")
        ch_wp = [P.chan(f"chwp{i}") for i in range(NSLOT)]
        ch_pre = [P.chan(f"chpre{l}") for l in range(L)]
        ch_kst = [P.chan(f"chkst{i}") for i in range(2)]
        st_rr = [0]

        def store_chan():
            c = ch_st[st_rr[0] % len(ch_st)]
            st_rr[0] += 1
            return c

        wslot_r = [Region(f"wslot{i}") for i in range(NSLOT)]
        wfree = deque(range(NSLOT))
        wsched = []
        wstate = {"next": 0, "ptr": 0, "loaded": {}}

        def wpump():
            while wstate["next"] < len(wsched) and wfree:
                l, u = wsched[wstate["next"]]
                slot = wfree.popleft()
                src = mk(ws[l], u * 128 * UNIT, [[UNIT, 128], [1, UNIT]])
                P.dma("sp", wring[:, slot, :], src, ch_w[slot],
                      reads=[reg(f"ws{l}")], writes=[wslot_r[slot]])
                wstate["loaded"][wstate["next"]] = slot
                wstate["next"] += 1

        def wget(l, name, idx):
            i = wstate["ptr"]
            assert wsched[i] == (l, UIDX[(name, idx)]), (wsched[i], l, name, idx)
            wpump()
            assert i in wstate["loaded"], "weight ring exhausted (would deadlock)"
            wstate["ptr"] += 1
            return wstate["loaded"][i]

        def wdone(slot):
            wfree.append(slot)
            wpump()

        def wap(slot, nk, ncols, k, c0, n):
            return mk(wring, slot * UNIT + k * ncols + c0, [[NSLOT * UNIT, 128], [1, n]])

        for s in range(NS):
            nb = seq_lens[s] // TB
            for l in range(L):
                for b in range(nb):
                    wsched += [(l, UIDX[("kvA", 0)]), (l, UIDX[("vB", 0)]), (l, UIDX[("vB", 1)]),
                               (l, UIDX[("kB", 0)]), (l, UIDX[("kB", 1)])]
                for b in range(nb):
                    seq = [("qA", 0), ("qA", 1), ("qB", 0), ("qB", 1)]
                    for g in range(2):
                        seq += [("ga", 2 * g), ("gb", 2 * g), ("oa", g), ("ob", g), ("ga", 2 * g + 1), ("gb", 2 * g + 1)]
                    seq += [("wo", i) for i in range(4)]
                    seq += [("f1", i) for i in range(16)]
                    seq += [("f2", i) for i in range(16)]
                    wsched += [(l, UIDX[k]) for k in seq]

        def ckpt(name):
            if stop_at == name:
                raise _Stop()

        def body():
            r_ident = reg("ident")
            P.op("pool", lambda e: e.memset(ident[:], 0.0), writes=[r_ident])
            P.op("pool", lambda e: e.affine_select(out=ident[:], in_=ident[:], pattern=[[-1, 128]],
                                                   compare_op=ALU.not_equal, fill=1.0, base=0,
                                                   channel_multiplier=1), reads=[r_ident], writes=[r_ident])
            P.op("pool", lambda e: e.memset(negh[:], -0.5), writes=[reg("negh")])
            P.op("pool", lambda e: e.memset(vAug[:], 1.0), writes=[reg("vAugAll")])
            P.op("pool", lambda e: e.memset(vBst[:], 1.0), writes=[reg("vBst0"), reg("vBst1")])

            def precast(l):
                for ui, (name, idx, src, k0, nk, c0, ncols) in enumerate(UNITS):
                    w = wsrc[src]
                    rows = w.shape[1]
                    cols = w.shape[2]
                    base = l * rows * cols + (k0 * 128) * cols + c0
                    sap = mk(w, base, [[cols, 128], [128 * cols, nk], [1, ncols]])
                    dap = mk(ws[l], ui * 128 * UNIT, [[UNIT, 128], [ncols, nk], [1, ncols]])
                    P.dma("pool", dap, sap, ch_pre[l], writes=[], serialize=False)
                reg(f"ws{l}").w = ch_pre[l].last

            ckpt('consts')
            precast(0)
            ckpt('precast0')

            ebreg = reg("ebs")
            ebtoks = []
            for l in range(L):
                for cfg in range(5):
                    for h in range(8):
                        k = (l * 5 + cfg) * 8 + h
                        slot = k % 2
                        wr = reg(f"WK{slot}")
                        er = reg(f"EBr{slot}")
                        off = ((l * 5 + cfg) * 8 + h) * 128 * 640
                        P.dma("sp", WK[:, slot, 0:640], mk(bt, off, [[640, 128], [1, 640]]), ch_x[slot], writes=[wr])
                        P.op("act", lambda e, slot=slot: e.activation(out=EBr[:, slot, :], in_=WK[:, slot, 0:640], func=AF.Exp),
                             reads=[wr], writes=[er])
                        t = P.dma("pool", mk(ebs, off, [[640, 128], [1, 640]]), EBr[:, slot, :], ch_st[slot], reads=[er])
                        ebtoks.append(t)
            ebreg.w = None
            ckpt('eb')
            eb_guard = [ch_st[0].last, ch_st[1].last]

            r_silc = reg("silc")
            P.dma("sp", WK[:, 0, 0:KC * NS], cT[:, :], ch_misc, writes=[reg("WK0")])
            P.op("act", lambda e: e.activation(out=silc[:], in_=WK[:, 0, 0:KC * NS], func=AF.Silu),
                 reads=[reg("WK0")], writes=[r_silc])
            r_bpm = reg("bpm")
            bpm = sb("bpm", [128, L * 48], F32)
            P.dma("sp", bpm[:], b_ada_pm[:, :], ch_misc, writes=[r_bpm])
            brow = mk(G12, 0, [[2 * D, NS], [1, 2 * D]])
            grow_sb = mk(LNV, 0, [[4 * D, NS], [1, 2 * D]])
            KIND_OF = {1: 0, 0: 1, 4: 2, 3: 3}
            for l in range(L):
                rb = reg("G12")
                P.dma("sp", brow[:, 0:D], mk(b_ada, l * 6 * D + 2 * D, [[0, NS], [1, D]]), ch_misc, writes=[rb])
                P.dma("sp", brow[:, D:2 * D], mk(b_ada, l * 6 * D + 5 * D, [[0, NS], [1, D]]), ch_misc, writes=[rb])
                for blk in range(6):
                    for half in range(2):
                        slots = []
                        for kh in range(2):
                            slot = wfree.popleft()
                            base = l * D * 6 * D + (kh * 4 * 128) * 6 * D + blk * D + half * 512
                            sap = mk(w_ada, base, [[6 * D, 128], [128 * 6 * D, 4], [1, 512]])
                            dap = mk(wring, slot * UNIT, [[NSLOT * UNIT, 128], [512, 4], [1, 512]])
                            P.dma("pool", dap, sap, ch_wp[slot], writes=[wslot_r[slot]])
                            slots.append(slot)
                        if blk in KIND_OF:
                            kind = KIND_OF[blk]
                            b = acq()
                            for cc in range(4):
                                for kc in range(KC):
                                    slot = slots[kc // 4]
                                    lhsT = mk(wring, slot * UNIT + (kc % 4) * 512 + cc * 128, [[NSLOT * UNIT, 128], [1, 128]])
                                    rhs = silc[:, kc * NS:(kc + 1) * NS]
                                    P.op("pe", lambda e, o=banks[b][:, cc * NS:(cc + 1) * NS], lhsT=lhsT, rhs=rhs, kc=kc:
                                         e.matmul(o, lhsT=lhsT, rhs=rhs, start=(kc == 0), stop=(kc == KC - 1)),
                                         reads=[wslot_r[slot], r_silc], writes=[bank_r[b]] if kc in (0, KC - 1) else [],
                                         signal=(kc == KC - 1))
                            for cc in range(4):
                                kc = half * 4 + cc
                                bcol = l * 48 + blk * 8 + kc
                                oap = mk(MODS, (l * NS * 4 + kind) * KC + kc, [[L * NS * 4 * KC, 128], [4 * KC, NS]])
                                addc = 1.0 if kind in (0, 2) else 0.0
                                P.op("dve", lambda e, oap=oap, i=banks[b][:, cc * NS:(cc + 1) * NS], sc=bpm[:, bcol:bcol + 1], addc=addc:
                                     e.tensor_scalar(out=oap, in0=i, scalar1=sc, scalar2=addc, op0=ALU.add, op1=ALU.add),
                                     reads=[bank_r[b], r_bpm], writes=[reg("MODS")])
                            rel(b)
                        else:
                            gi = 0 if blk == 2 else 1
                            b = acq()
                            for kc in range(KC):
                                slot = slots[kc // 4]
                                rhs = mk(wring, slot * UNIT + (kc % 4) * 512, [[NSLOT * UNIT, 128], [1, 512]])
                                lhsT = silc[:, kc * NS:(kc + 1) * NS]
                                P.op("pe", lambda e, o=banks[b][0:NS, :], lhsT=lhsT, rhs=rhs, kc=kc:
                                     e.matmul(o, lhsT=lhsT, rhs=rhs, start=(kc == 0), stop=(kc == KC - 1)),
                                     reads=[wslot_r[slot], r_silc], writes=[bank_r[b]] if kc in (0, KC - 1) else [],
                                     signal=(kc == KC - 1))
                            c0 = gi * D + half * 512
                            P.op("dve", lambda e, o=grow_sb[:, c0:c0 + 512], i=banks[b][0:NS, :], bb=brow[:, c0:c0 + 512]:
                                 e.tensor_tensor(out=o, in0=i, in1=bb, op=ALU.add),
                                 reads=[bank_r[b], rb], writes=[reg("LNV")])
                            P.op("dve", lambda e, o=grow_sb[:, c0:c0 + 512], gi=gi:
                                 e.tensor_scalar(out=o, in0=o, scalar1=1.0, scalar2=((0.5 if gi == 0 else 1.0) / ALPHA), op0=ALU.add, op1=ALU.mult),
                                 reads=[reg("LNV")], writes=[reg("LNV")])
                            rel(b)
                        for slot in slots:
                            wfree.append(slot)
                P.dma("pool", mk(grow, l * NS * 2 * D, [[2 * D, NS], [1, 2 * D]]), grow_sb[:, :], ch_miscp,
                      reads=[reg("LNV")], writes=[reg("grow")])

            ckpt('mods')
            for l in range(1, L):
                precast(l)
            ckpt('precast')

            def mods_ap(l, s, kind, kc):
                col = ((l * NS + s) * 4 + kind) * KC + kc
                return MODS[:, col:col + 1]

            def ln_stats(src_ap, src_reg, j, eps=EPS):
                rs, rm = reg(f"st6_{j}"), reg(f"mv_{j}")
                P.op("dve", lambda e: e.bn_stats(out=st6[:, j, 0:6], in_=src_ap[:, 0:512]), reads=[src_reg], writes=[rs])
                P.op("dve", lambda e: e.bn_stats(out=st6[:, j, 6:12], in_=src_ap[:, 512:1024]), reads=[src_reg], writes=[rs])
                P.op("dve", lambda e: e.bn_aggr(out=mv[:, j, 0:2], in_=st6[:, j, :]), reads=[rs], writes=[rm])
                P.op("pool", lambda e: e.tensor_scalar(out=mv[:, j, 2:3], in0=mv[:, j, 1:2], scalar1=eps, scalar2=None, op0=ALU.add),
                     reads=[rm], writes=[rm])
                P.op("pool", lambda e: e.tensor_tensor(out=mv[:, j, 2:3], in0=mv[:, j, 2:3], in1=negh[:, 0:1], op=ALU.pow),
                     reads=[rm, reg("negh")], writes=[rm])
                P.op("dve", lambda e: e.scalar_tensor_tensor(out=mv[:, j, 3:4], in0=mv[:, j, 0:1], scalar=-1.0, in1=mv[:, j, 2:3],
                                                             op0=ALU.mult, op1=ALU.mult), reads=[rm], writes=[rm])
                return rm

            def ln_to_uT(tiles_ap, tiles_reg, l, s, kind_sc, kind_sh):
                tb = [acq() for _ in range(4)]
                for t in range(NT):
                    rm = ln_stats(tiles_ap[t], tiles_reg[t], t)
                    ckpt('p1a1')
                    sl = t % 2
                    rx = reg(f"xn{sl}")
                    P.op("act", lambda e, t=t, sl=sl: e.activation(out=xn[:, sl, :], in_=tiles_ap[t], func=AF.Identity,
                                                                   scale=mv[:, t, 2:3], bias=mv[:, t, 3:4]),
                         reads=[tiles_reg[t], rm], writes=[rx])
                    ckpt('p1a2')
                    for kc in range(KC):
                        oap = banks_bf[tb[kc // 2]][:, (kc % 2) * 512 + t * 128:(kc % 2) * 512 + (t + 1) * 128]
                        last = (kc == KC - 1)
                        P.op("pe", lambda e, oap=oap, i=xn[:, sl, kc * 128:(kc + 1) * 128]: e.transpose(oap, i, ident[:]),
                             reads=[rx, r_ident], writes=[bank_r[x] for x in tb] if (last or kc == 0) else [], signal=last)
                    ckpt('p1a3')
                ckpt('p1a4')
                for kc in range(KC):
                    eng = "act"
                    iap = banks_bf[tb[kc // 2]][:, (kc % 2) * 512:(kc % 2) * 512 + 512]
                    sc, sh = mods_ap(l, s, kind_sc, kc), mods_ap(l, s, kind_sh, kc)
                    if eng == "act":
                        P.op("act", lambda e, kc=kc, iap=iap, sc=sc, sh=sh: e.activation(out=uT[:, kc, :], in_=iap, func=AF.Identity, scale=sc, bias=sh),
                             reads=[bank_r[tb[kc // 2]], reg("MODS")], writes=[reg(f"uT{kc}")])
                    else:
                        P.op("dve", lambda e, kc=kc, iap=iap, sc=sc, sh=sh: e.tensor_scalar(out=uT[:, kc, :], in0=iap, scalar1=sc, scalar2=sh, op0=ALU.mult, op1=ALU.add),
                             reads=[bank_r[tb[kc // 2]], reg("MODS")], writes=[reg(f"uT{kc}")])
                    ckpt('p1a5' if kc == 0 else 'p1a6' if kc == 1 else 'zz')
                for b in tb:
                    rel(b)

            uT_regs = [reg(f"uT{kc}") for kc in range(KC)]

            def proj_tok(slots, nk_per, ncols, c0, n, tile, b, col0=0):
                for kc in range(KC):
                    slot = slots[kc // nk_per]
                    rhs = wap(slot, nk_per, ncols, kc % nk_per, c0, n)
                    lhsT = uT[:, kc, tile * 128:(tile + 1) * 128]
                    P.op("pe", lambda e, o=banks[b][:, col0:col0 + n], lhsT=lhsT, rhs=rhs, kc=kc:
                         e.matmul(o, lhsT=lhsT, rhs=rhs, start=(kc == 0), stop=(kc == KC - 1)),
                         reads=[wslot_r[slot], uT_regs[kc]], writes=[bank_r[b]] if kc in (0, KC - 1) else [],
                         signal=(kc == KC - 1))

            def proj_feat(slot, cc, b, rhs_fn, rhs_regs, nk=KC, ncols=256):
                for kc in range(nk):
                    lhsT = wap(slot, nk, ncols, kc, cc * 128, 128)
                    P.op("pe", lambda e, o=banks[b][:, :], lhsT=lhsT, rhs=rhs_fn(kc), kc=kc:
                         e.matmul(o, lhsT=lhsT, rhs=rhs, start=(kc == 0), stop=(kc == nk - 1)),
                         reads=[wslot_r[slot], rhs_regs[kc]], writes=[bank_r[b]] if kc in (0, nk - 1) else [],
                         signal=(kc == nk - 1))

            def qk_norm_rope(b, ncols, nh, gcol0, cs_slot, cs_reg, out_ap, out_reg, arena_out=False):
                w0, w1 = reg("WK0"), reg("WK1")
                ps = banks[b][:, 0:ncols]
                P.op("act", lambda e: e.activation(out=WK[:, 0, 0:ncols], in_=ps, func=AF.Square), reads=[bank_r[b]], writes=[w0])
                rs = reg("ssq")
                P.op("dve", lambda e: e.tensor_reduce(out=ssq[:, 0, 0:nh], in_=mk(WK, 0, [[2 * D, 128], [64, nh], [1, 64]]), axis=AX.X, op=ALU.add),
                     reads=[w0], writes=[rs])
                P.op("pool", lambda e: e.tensor_scalar(out=ssq[:, 1, 0:nh], in0=ssq[:, 0, 0:nh], scalar1=1.0 / 64.0, scalar2=EPS, op0=ALU.mult, op1=ALU.add),
                     reads=[rs], writes=[rs])
                P.op("pool", lambda e: e.tensor_tensor(out=ssq[:, 1, 0:nh], in0=ssq[:, 1, 0:nh], in1=negh[:, 0:nh], op=ALU.pow),
                     reads=[rs, reg("negh")], writes=[rs])
                P.op("dve", lambda e: e.tensor_tensor(out=mk(WK, D, [[2 * D, 128], [64, nh], [1, 64]]),
                                                      in0=mk(banks[b], 0, [[512, 128], [64, nh], [1, 64]]),
                                                      in1=mk(ssq, 16, [[32, 128], [1, nh], [0, 64]]), op=ALU.mult),
                     reads=[bank_r[b], rs], writes=[w1])
                P.op("pool", lambda e: e.tensor_tensor(out=WK[:, 1, 0:ncols], in0=WK[:, 1, 0:ncols], in1=QKG[:, gcol0:gcol0 + ncols], op=ALU.mult),
                     reads=[w1, reg("QKG")], writes=[w1])
                ev = mk(WK, D, [[2 * D, 128], [64, nh], [1, 32]])
                od = mk(WK, D + 32, [[2 * D, 128], [64, nh], [1, 32]])
                Cb = mk(CSr, cs_slot * 64, [[128, 128], [0, nh], [1, 32]])
                Sb = mk(CSr, cs_slot * 64 + 32, [[128, 128], [0, nh], [1, 32]])
                T1 = mk(RT, 0, [[512, 128], [32, nh], [1, 32]])
                T2 = mk(RT, 256, [[512, 128], [32, nh], [1, 32]])
                oe = mk(out_ap.tensor, out_ap.offset, [[out_ap.ap[0][0], 128], [64, nh], [1, 32]])
                oo = mk(out_ap.tensor, out_ap.offset + 32, [[out_ap.ap[0][0], 128], [64, nh], [1, 32]])
                r1, r2 = reg("RT0"), reg("RT1")
                P.op("dve", lambda e: e.tensor_tensor(out=T1, in0=ev, in1=Cb, op=ALU.mult), reads=[w1, cs_reg], writes=[r1])
                P.op("pool", lambda e: e.tensor_tensor(out=T2, in0=od, in1=Sb, op=ALU.mult), reads=[w1, cs_reg], writes=[r2])
                P.op("dve", lambda e: e.tensor_tensor(out=oe, in0=T1, in1=T2, op=ALU.subtract), reads=[r1, r2], writes=[out_reg], arena=arena_out)
                P.op("pool", lambda e: e.tensor_tensor(out=T1, in0=ev, in1=Sb, op=ALU.mult), reads=[w1, cs_reg], writes=[r1])
                P.op("dve", lambda e: e.tensor_tensor(out=T2, in0=od, in1=Cb, op=ALU.mult), reads=[w1, cs_reg], writes=[r2])
                P.op("pool", lambda e: e.tensor_tensor(out=oo, in0=T1, in1=T2, op=ALU.add), reads=[r1, r2], writes=[out_reg], arena=arena_out)

            def post_ln(t, half_banks, gi, lg, lb, wk):
                rw = reg(f"WK{wk}")
                rx = reg(f"xt{t}")
                for h in range(2):
                    P.op("dve", lambda e, h=h: e.tensor_tensor(out=WK[:, wk, h * 512:(h + 1) * 512], in0=banks[half_banks[h]][:, :],
                                                              in1=G12[:, gi, h * 512:(h + 1) * 512], op=ALU.mult),
                         reads=[bank_r[half_banks[h]], reg("G12")], writes=[rw])
                P.op("pool", lambda e: e.tensor_tensor(out=WK[:, wk, :], in0=WK[:, wk, :], in1=xt[:, t, :], op=ALU.add),
                     reads=[rx, rw], writes=[rw])
                rm = ln_stats(WK[:, wk, :], rw, t, eps=EPS / (ALPHA * ALPHA))
                P.op("act", lambda e: e.activation(out=WK[:, wk, :], in_=WK[:, wk, :], func=AF.Identity, scale=mv[:, t, 2:3], bias=mv[:, t, 3:4]),
                     reads=[rw, rm], writes=[rw])
                P.op("pool", lambda e: e.tensor_tensor(out=WK[:, wk, :], in0=WK[:, wk, :], in1=LNV[:, lg, :], op=ALU.mult),
                     reads=[rw, reg("LNV")], writes=[rw])
                P.op("dve", lambda e: e.tensor_tensor(out=xt[:, t, :], in0=WK[:, wk, :], in1=LNV[:, lb, :], op=ALU.add),
                     reads=[rw, reg("LNV")], writes=[rx])

            xt_regs = [reg(f"xt{t}") for t in range(NT)]
            xt_aps = [xt[:, t, :] for t in range(NT)]

            def load_x(src, s, blk):
                for t in range(NT):
                    row0 = blk * TB + t * 128
                    P.dma("sp", xt[:, t, :], mk(src, row0 * D, [[D, 128], [1, D]]), ch_x[t],
                          reads=[reg(f"y{s}_{blk * NT + t}")], writes=[xt_regs[t]])

            def load_cs(S, tile, slot):
                P.dma("sp", CSr[:, slot, :], mk(cstab[S], tile * 128 * 64, [[64, 128], [1, 64]]), ch_cs[slot], writes=[reg(f"CS{slot}")])
                return reg(f"CS{slot}")

            cs_ctr = [0]

            first_layer_loaded = [False]
            for s in range(NS):
                S = seq_lens[s]
                NB = S // TB
                NTIL = S // 128
                for l in range(L):
                    src = xs[s] if l == 0 else ys[s]
                    misc_extra = eb_guard if not first_layer_loaded[0] else []
                    first_layer_loaded[0] = True
                    P.dma("sp", G12[:, 0, :], mk(grow, (l * NS + s) * 2 * D, [[0, 128], [1, D]]), ch_misc, reads=[reg("grow")], writes=[reg("G12")], extra=misc_extra)
                    P.dma("sp", G12[:, 1, :], mk(grow, (l * NS + s) * 2 * D + D, [[0, 128], [1, D]]), ch_misc, reads=[reg("grow")], writes=[reg("G12")])
                    for i, k in enumerate(("ln_mix_g", "ln_mix_b", "ln_ff_g", "ln_ff_b")):
                        P.dma("sp", LNV[:, i, :], mk(lnv[k], l * D, [[0, 128], [1, D]]), ch_misc, writes=[reg("LNV")])
                    P.dma("sp", QKG[:, :], mk(qkg, l * 640, [[0, 128], [1, 640]]), ch_misc, writes=[reg("QKG")])

                    P.switch_mode()
                    for blk in range(NB):
                        load_x(src, s, blk)
                        ckpt('p1x')
                        ln_to_uT(xt_aps, xt_regs, l, s, 0, 1)
                        ckpt('p1a')
                        s_kv = wget(l, "kvA", 0)
                        s_vb = [wget(l, "vB", 0), wget(l, "vB", 1)]
                        for t in range(NT):
                            tile = blk * NT + t
                            b = acq()
                            proj_tok([s_kv], 8, 256, 0, 256, t, b)
                            csl = cs_ctr[0] % 2
                            cs_ctr[0] += 1
                            csreg = load_cs(S, tile, csl)
                            rv = reg(f"vAug{tile}")
                            P.op("act", lambda e, b=b, tile=tile: e.activation(
                                out=mk(vAug, tile * 256, [[(SMAX // 128) * 256, 128], [192, 2], [1, 64]]),
                                in_=mk(banks[b], 128, [[512, 128], [64, 2], [1, 64]]), func=AF.Copy),
                                reads=[bank_r[b]], writes=[rv], extra=[reg("vAugAll").w])
                            ckpt('p1b')
                            rqr = reg("QR")
                            qk_norm_rope(b, 128, 2, 512, csl, csreg, QR[:, 0:128], rqr)
                            rel(b)
                            b2 = acq()
                            P.op("pe", lambda e, b2=b2: e.transpose(banks_bf[b2][:, 0:128], QR[:, 0:128], ident[:]),
                                 reads=[rqr, r_ident], writes=[bank_r[b2]])
                            P.op("dve", lambda e, b2=b2, tile=tile: e.tensor_copy(out=kAT[:, tile * 128:(tile + 1) * 128], in_=banks_bf[b2][:, 0:128]),
                                 reads=[bank_r[b2]], writes=[reg(f"kAT{tile}")])
                            rel(b2)
                            ckpt('p1c')
                            b3 = acq()
                            proj_tok(s_vb, 4, 512, 0, 512, t, b3)
                            vsl = tile % 2
                            rvs = reg(f"vBst{vsl}")
                            P.op("act", lambda e, b3=b3, vsl=vsl: e.activation(
                                out=mk(vBst, vsl * 520, [[2 * 520, 128], [65, 8], [1, 64]]),
                                in_=mk(banks[b3], 0, [[512, 128], [64, 8], [1, 64]]), func=AF.Copy),
                                reads=[bank_r[b3]], writes=[rvs])
                            rel(b3)
                            P.dma("pool", mk(vbs, tile * 128 * 520, [[520, 128], [1, 520]]), vBst[:, vsl, :], ch_kst[vsl],
                                  reads=[rvs], writes=[reg(f"vbs{tile}")])
                        wdone(s_kv)
                        for x in s_vb:
                            wdone(x)
                        ckpt('p1d')
                        rks = reg("kBst")
                        for i in range(2):
                            sl = wget(l, "kB", i)
                            for cc in range(2):
                                hp = 2 * i + cc
                                b = acq()
                                proj_feat(sl, cc, b, lambda kc: uT[:, kc, :], uT_regs)
                                P.op("dve", lambda e, b=b, hp=hp: e.tensor_copy(out=kBst[:, hp, :], in_=banks[b][:, :]),
                                     reads=[bank_r[b]], writes=[rks])
                                rel(b)
                            wdone(sl)
                        P.dma("pool", mk(kbs, blk * TB, [[4 * SMAX, 128], [SMAX, 4], [1, TB]]), kBst[:, :, :], store_chan(),
                              reads=[rks], writes=[reg(f"kbs{blk}")])

                    ckpt('p1')
                    kb_slot_of = {}
                    kb_state = {"next": 0}

                    def na_window(T):
                        return min(max(T - 2, 0), NTIL - 5)

                    def ensure_kv(upto):
                        while kb_state["next"] <= upto:
                            c = kb_state["next"]
                            slot = c % 6
                            P.dma("sp", kBr[:, :, slot, :], mk(kbs, c * 128, [[4 * SMAX, 128], [SMAX, 4], [1, 128]]), ch_kb[slot],
                                  reads=[reg(f"kbs{c // NT}")], writes=[reg(f"kBr{slot}")])
                            P.dma("sp", vBr[:, slot, :], mk(vbs, c * 128 * 520, [[520, 128], [1, 520]]), ch_vb[slot],
                                  reads=[reg(f"vbs{c}")], writes=[reg(f"vBr{slot}")])
                            kb_state["next"] += 1

                    eb_ctr = [0]
                    for blk in range(NB):
                        load_x(src, s, blk)
                        ln_to_uT(xt_aps, xt_regs, l, s, 0, 1)
                        P.switch_mode()
                        ckpt('A')
                        s_qa = [wget(l, "qA", 0), wget(l, "qA", 1)]
                        for t in range(NT):
                            tile = blk * NT + t
                            b = acq()
                            proj_tok(s_qa, 4, 512, 0, 512, t, b)
                            csl = cs_ctr[0] % 2
                            cs_ctr[0] += 1
                            csreg = load_cs(S, tile, csl)
                            rqr = reg("QR")
                            qk_norm_rope(b, 512, 8, 0, csl, csreg, QR[:, 0:512], rqr)
                            rel(b)
                            b2 = acq()
                            for c in range(4):
                                P.op("pe", lambda e, b2=b2, c=c: e.transpose(banks_bf[b2][:, c * 128:(c + 1) * 128], QR[:, c * 128:(c + 1) * 128], ident[:]),
                                     reads=[rqr, r_ident], writes=[bank_r[b2]] if c in (0, 3) else [], signal=(c == 3))
                            P.op("act", lambda e, b2=b2, t=t: e.activation(
                                out=abf(A_QAT + t * 128, [[512, 4], [1, 128]]),
                                in_=mk(banks_bf[b2], 0, [[1024, 128], [128, 4], [1, 128]]), func=AF.Copy),
                                reads=[bank_r[b2]], writes=[reg(f"qAT{c}") for c in range(4)], arena=True)
                            rel(b2)
                        for x in s_qa:
                            wdone(x)
                        for i in range(2):
                            sl = wget(l, "qB", i)
                            for cc in range(2):
                                hp = 2 * i + cc
                                b = acq()
                                proj_feat(sl, cc, b, lambda kc: uT[:, kc, :], uT_regs)
                                P.op("dve", lambda e, b=b, hp=hp: e.tensor_copy(out=abf(A_QBT + hp * 512, [[1, 512]]), in_=banks[b][:, :]),
                                     reads=[bank_r[b]], writes=[reg(f"qBT{hp}")], arena=True)
                                rel(b)
                            wdone(sl)

                        ckpt('B')
                        NK = NTIL
                        for c in range(4):
                            ob_ = [acq(), acq()]
                            prev = None
                            pa_ctr = 0

                            def pv(prev):
                                kt, slots_ = prev
                                for side in range(2):
                                    lhsT = mk(vAug, kt * 256 + side * 128, [[(SMAX // 128) * 256, 128], [1, 128]])
                                    rhs = abf(A_PA + slots_[side] * 512, [[1, 512]])
                                    last = (kt == NK - 1)
                                    P.op("pe", lambda e, o=banks[ob_[side]][:, :], lhsT=lhsT, rhs=rhs, kt=kt:
                                         e.matmul(o, lhsT=lhsT, rhs=rhs, start=(kt == 0), stop=(kt == NK - 1)),
                                         reads=[reg(f"vAug{kt}"), reg(f"pA{slots_[side]}")],
                                         writes=[bank_r[ob_[side]]] if (kt == 0 or last) else [], signal=True, arena=True)

                            for kt in range(NK):
                                sb_ = [acq(), acq()]
                                slots_ = [(pa_ctr * 2) % 4, (pa_ctr * 2 + 1) % 4]
                                pa_ctr += 1
                                for side in range(2):
                                    lhsT = kAT[side * 64:(side + 1) * 64, kt * 128:(kt + 1) * 128]
                                    rhs = mk(arena, side * 64 * ARENA + A_QAT + c * 512, [[ARENA, 64], [1, 512]])
                                    P.op("pe", lambda e, o=banks[sb_[side]][:, :], lhsT=lhsT, rhs=rhs:
                                         e.matmul(o, lhsT=lhsT, rhs=rhs, start=True, stop=True),
                                         reads=[reg(f"kAT{kt}"), reg(f"qAT{c}")], writes=[bank_r[sb_[side]]], arena=True)
                                if prev is not None:
                                    pv(prev)
                                for side in range(2):
                                    P.op("act", lambda e, i=banks[sb_[side]][:, :], o=abf(A_PA + slots_[side] * 512, [[1, 512]]):
                                         e.activation(out=o, in_=i, func=AF.Exp, scale=0.125),
                                         reads=[bank_r[sb_[side]]], writes=[reg(f"pA{slots_[side]}")], arena=True)
                                    rel(sb_[side])
                                prev = (kt, slots_)
                            pv(prev)
                            for side in range(2):
                                b = ob_[side]
                                lo, hi = (0, 64) if side == 0 else (64, 128)
                                dlo, dhi = (64, 128) if side == 0 else (0, 64)
                                rr = reg(f"rec{side}")
                                rec_f = mk(arena, lo * ARENA + A_REC + side * 1024, [[ARENA, 64], [1, 1024]]).bitcast(F32)
                                P.op("dve", lambda e, b=b, rec_f=rec_f, dlo=dlo, dhi=dhi: e.reciprocal(out=rec_f, in_=banks[b][dlo:dhi, :]),
                                     reads=[bank_r[b]], writes=[rr], arena=True)
                                P.op("dve", lambda e, b=b, rec_f=rec_f, lo=lo, hi=hi, c=c:
                                     e.tensor_tensor(out=mk(arena, lo * ARENA + A_AOT + c * 512, [[ARENA, 64], [1, 512]]),
                                                     in0=banks[b][lo:hi, :], in1=rec_f, op=ALU.mult),
                                     reads=[bank_r[b], rr], writes=[reg(f"aoT{c}")], arena=True)
                                rel(b)

                        ckpt('C')
                        for t in range(NT):
                            T = blk * NT + t
                            c0 = na_window(T)
                            cfg = 0 if T == 0 else 1 if T == 1 else 3 if T == NTIL - 2 else 4 if T == NTIL - 1 else 2
                            ensure_kv(c0 + 4)
                            obk = [acq(), acq()]
                            for h in range(8):
                                hp, side = h // 2, h % 2
                                esl = eb_ctr[0] % 3
                                eb_ctr[0] += 1
                                off = ((l * 5 + cfg) * 8 + h) * 128 * 640
                                P.dma("sp", EBr[:, esl, :], mk(ebs, off, [[640, 128], [1, 640]]), ch_eb[esl], writes=[reg(f"EBr{esl}")])
                                sa, sb2 = acq(), acq()
                                rhs = mk(arena, side * 64 * ARENA + A_QBT + hp * 512 + t * 128, [[ARENA, 64], [1, 128]])
                                for j in range(5):
                                    slot = (c0 + j) % 6
                                    lhsT = mk(kBr, side * 64 * (4 * 6 * 128) + (hp * 6 + slot) * 128, [[4 * 6 * 128, 64], [1, 128]])
                                    o = banks[sa][:, j * 128:(j + 1) * 128] if j < 4 else banks[sb2][:, 0:128]
                                    P.op("pe", lambda e, o=o, lhsT=lhsT, rhs=rhs: e.matmul(o, lhsT=lhsT, rhs=rhs, start=True, stop=True),
                                         reads=[reg(f"kBr{slot}"), reg(f"qBT{hp}")],
                                         writes=[bank_r[sa]] if j in (0, 3) else [bank_r[sb2]] if j == 4 else [],
                                         signal=(j >= 3), arena=True)
                                psl = h % 2
                                rp = reg(f"pB{psl}")
                                P.op("act", lambda e, sa=sa, psl=psl: e.activation(out=abf(A_PB + psl * 640, [[1, 512]]), in_=banks[sa][:, :], func=AF.Exp, scale=0.125),
                                     reads=[bank_r[sa]], writes=[rp], arena=True)
                                P.op("act", lambda e, sb2=sb2, psl=psl: e.activation(out=abf(A_PB + psl * 640 + 512, [[1, 128]]), in_=banks[sb2][:, 0:128], func=AF.Exp, scale=0.125),
                                     reads=[bank_r[sb2]], writes=[rp], arena=True)
                                rel(sa)
                                rel(sb2)
                                P.op("pool", lambda e, psl=psl, esl=esl: e.tensor_tensor(out=abf(A_PB + psl * 640, [[1, 640]]), in0=abf(A_PB + psl * 640, [[1, 640]]),
                                                                                         in1=EBr[:, esl, :], op=ALU.mult),
                                     reads=[rp, reg(f"EBr{esl}")], writes=[rp], arena=True)
                                bk = obk[h // 4]
                                for j in range(5):
                                    slot = (c0 + j) % 6
                                    lhsT = abf(A_PB + psl * 640 + j * 128, [[1, 128]])
                                    rhs2 = mk(vBr, slot * 520 + h * 65, [[6 * 520, 128], [1, 65]])
                                    o = banks[bk][:, (h % 4) * 65:(h % 4) * 65 + 65]
                                    P.op("pe", lambda e, o=o, lhsT=lhsT, rhs2=rhs2, j=j: e.matmul(o, lhsT=lhsT, rhs=rhs2, start=(j == 0), stop=(j == 4)),
                                         reads=[rp, reg(f"vBr{slot}")], writes=[bank_r[bk]] if j in (0, 4) else [], signal=(j == 4), arena=True)
                            osl = t % 2
                            rob = reg(f"ob{osl}")
                            for g in range(2):
                                bk = obk[g]
                                rb_ap = mk(arena, A_RECB + osl * 32 + g * 8, [[ARENA, 128], [1, 8]]).bitcast(F32)
                                P.op("dve", lambda e, bk=bk, rb_ap=rb_ap: e.reciprocal(out=rb_ap, in_=mk(banks[bk], 64, [[512, 128], [65, 4]])),
                                     reads=[bank_r[bk]], writes=[reg(f"recB{osl}")], arena=True)
                                P.op("dve", lambda e, bk=bk, rb_ap=rb_ap, g=g, osl=osl: e.tensor_tensor(
                                    out=abf(A_OB + osl * 512 + g * 256, [[64, 4], [1, 64]]),
                                    in0=mk(banks[bk], 0, [[512, 128], [65, 4], [1, 64]]),
                                    in1=mk(rb_ap.tensor, rb_ap.offset, [[rb_ap.ap[0][0], 128], [1, 4], [0, 64]]), op=ALU.mult),
                                    reads=[bank_r[bk], reg(f"recB{osl}")], writes=[rob], arena=True)
                                rel(bk)
                            b2 = acq()
                            for hp in range(4):
                                P.op("pe", lambda e, b2=b2, hp=hp, osl=osl: e.transpose(banks_bf[b2][:, hp * 128:(hp + 1) * 128],
                                                                                        abf(A_OB + osl * 512 + hp * 128, [[1, 128]]), ident[:]),
                                     reads=[rob, r_ident], writes=[bank_r[b2]] if hp in (0, 3) else [], signal=(hp == 3), arena=True)
                            P.op("dve", lambda e, b2=b2, t=t: e.tensor_copy(out=abf(A_OBT + t * 128, [[512, 4], [1, 128]]),
                                                                            in_=mk(banks_bf[b2], 0, [[1024, 128], [128, 4], [1, 128]])),
                                 reads=[bank_r[b2]], writes=[reg(f"obT{hp}") for hp in range(4)], arena=True)
                            rel(b2)

                        ckpt('D')
                        for g in range(2):
                            sl_ga = [None, None]
                            sl_gb = [None, None]
                            sl_ga[0] = wget(l, "ga", 2 * g)
                            sl_gb[0] = wget(l, "gb", 2 * g)
                            sl_oa = wget(l, "oa", g)
                            sl_ob = wget(l, "ob", g)
                            for q in range(4):
                                n = 4 * g + q
                                if q == 2:
                                    sl_ga[1] = wget(l, "ga", 2 * g + 1)
                                    sl_gb[1] = wget(l, "gb", 2 * g + 1)
                                bga, bgb, boa, bob = acq(), acq(), acq(), acq()
                                proj_feat(sl_ga[q // 2], q % 2, bga, lambda kc: uT[:, kc, :], uT_regs)
                                proj_feat(sl_gb[q // 2], q % 2, bgb, lambda kc: uT[:, kc, :], uT_regs)
                                proj_feat(sl_oa, q, boa, lambda kc: abf(A_AOT + kc * 512, [[1, 512]]), [reg(f"aoT{c}") for c in range(4)], nk=4, ncols=512)
                                proj_feat(sl_ob, q, bob, lambda kc: abf(A_OBT + kc * 512, [[1, 512]]), [reg(f"obT{c}") for c in range(4)], nk=4, ncols=512)
                                if q == 1:
                                    wdone(sl_ga[0])
                                    wdone(sl_gb[0])
                                ta = abf(A_TA + q * 512, [[1, 512]])
                                tb_ = abf(A_TB_ + q * 512, [[1, 512]])
                                P.op("act", lambda e, ta=ta, bga=bga: e.activation(out=ta, in_=banks[bga][:, :], func=AF.Tanh, scale=0.5),
                                     reads=[bank_r[bga]], writes=[reg(f"ta{q}")], arena=True)
                                P.op("act", lambda e, tb_=tb_, bgb=bgb: e.activation(out=tb_, in_=banks[bgb][:, :], func=AF.Tanh, scale=0.5),
                                     reads=[bank_r[bgb]], writes=[reg(f"tb{q}")], arena=True)
                                rel(bga)
                                rel(bgb)
                                t1 = abf(A_T12, [[1, 1024]]).bitcast(F32)
                                t2 = abf(A_T12 + 1024, [[1, 1024]]).bitcast(F32)
                                P.op("dve", lambda e, t1=t1, ta=ta, boa=boa: e.scalar_tensor_tensor(out=t1, in0=ta, scalar=1.0, in1=banks[boa][:, :], op0=ALU.add, op1=ALU.mult),
                                     reads=[reg(f"ta{q}"), bank_r[boa]], writes=[reg("t1")], arena=True)
                                P.op("dve", lambda e, t2=t2, tb_=tb_, bob=bob: e.scalar_tensor_tensor(out=t2, in0=tb_, scalar=1.0, in1=banks[bob][:, :], op0=ALU.add, op1=ALU.mult),
                                     reads=[reg(f"tb{q}"), bank_r[bob]], writes=[reg("t2")], arena=True)
                                rel(boa)
                                rel(bob)
                                P.op("pool", lambda e, t1=t1, t2=t2, n=n: e.tensor_tensor(out=abf(A_MIT + n * 512, [[1, 512]]), in0=t1, in1=t2, op=ALU.add),
                                     reads=[reg("t1"), reg("t2")], writes=[reg(f"miT{n}")], arena=True)
                            wdone(sl_ga[1])
                            wdone(sl_gb[1])
                            wdone(sl_oa)
                            wdone(sl_ob)

                        ckpt('E')
                        halves = []
                        for hf in range(2):
                            sl = [wget(l, "wo", 2 * hf), wget(l, "wo", 2 * hf + 1)]
                            bs = [acq() for _ in range(NT)]
                            for t in range(NT):
                                for kc in range(KC):
                                    slot = sl[kc // 4]
                                    rhs = wap(slot, 4, 512, kc % 4, 0, 512)
                                    lhsT = abf(A_MIT + kc * 512 + t * 128, [[1, 128]])
                                    P.op("pe", lambda e, o=banks[bs[t]][:, :], lhsT=lhsT, rhs=rhs, kc=kc: e.matmul(o, lhsT=lhsT, rhs=rhs, start=(kc == 0), stop=(kc == KC - 1)),
                                         reads=[wslot_r[slot], reg(f"miT{kc}")], writes=[bank_r[bs[t]]] if kc in (0, KC - 1) else [], signal=(kc == KC - 1), arena=True)
                            for x in sl:
                                wdone(x)
                            halves.append(bs)
                        for t in range(NT):
                            post_ln(t, [halves[0][t], halves[1][t]], 0, 0, 1, t % 2)
                            rel(halves[0][t])
                            rel(halves[1][t])

                        ckpt('F')
                        ln_to_uT(xt_aps, xt_regs, l, s, 2, 3)
                        P.switch_mode()
                        ckpt('G')
                        for i in range(16):
                            sl = wget(l, "f1", i)
                            for cc in range(2):
                                f = 2 * i + cc
                                b = acq()
                                proj_feat(sl, cc, b, lambda kc: uT[:, kc, :], uT_regs)
                                rs_ = f % 2
                                rr = reg(f"r{rs_}")
                                P.op("act", lambda e, b=b, rs_=rs_: e.activation(out=abf(A_R + rs_ * 512, [[1, 512]]), in_=banks[b][:, :], func=AF.Relu),
                                     reads=[bank_r[b]], writes=[rr], arena=True)
                                rel(b)
                                P.op("pool", lambda e, rs_=rs_, f=f: e.tensor_tensor(out=abf(A_HT + f * 512, [[1, 512]]), in0=abf(A_R + rs_ * 512, [[1, 512]]),
                                                                                     in1=abf(A_R + rs_ * 512, [[1, 512]]), op=ALU.mult),
                                     reads=[rr], writes=[reg(f"hT{f}")], arena=True)
                            wdone(sl)
                        ckpt('H')
                        halves = []
                        for hf in range(2):
                            bs = [acq() for _ in range(NT)]
                            for g in range(8):
                                sl = wget(l, "f2", 8 * hf + g)
                                for t in range(NT):
                                    for k4 in range(4):
                                        f = 4 * g + k4
                                        rhs = wap(sl, 4, 512, k4, 0, 512)
                                        lhsT = abf(A_HT + f * 512 + t * 128, [[1, 128]])
                                        first, last = (f == 0), (f == 31)
                                        P.op("pe", lambda e, o=banks[bs[t]][:, :], lhsT=lhsT, rhs=rhs, first=first, last=last:
                                             e.matmul(o, lhsT=lhsT, rhs=rhs, start=first, stop=last),
                                             reads=[wslot_r[sl], reg(f"hT{f}")], writes=[bank_r[bs[t]]] if (first or last) else [],
                                             signal=(last or k4 == 3 and t == NT - 1), arena=True)
                                wdone(sl)
                            halves.append(bs)
                        for t in range(NT):
                            post_ln(t, [halves[0][t], halves[1][t]], 1, 2, 3, t % 2)
                            rel(halves[0][t])
                            rel(halves[1][t])
                            row0 = blk * TB + t * 128
                            P.dma("pool", mk(ys[s], row0 * D, [[D, 128], [1, D]]), xt[:, t, :], store_chan(),
                                  reads=[xt_regs[t]], writes=[reg(f"y{s}_{blk * NT + t}")])

        try:
            body()
        except _Stop:
            pass

        fin = [c.last for c in ch_st] + [c.last for c in ch_kst]
        P.op("pool", lambda e: e.memset(negh[:, 0:1], -0.5), extra=fin)
        for eng in ("pe", "act", "dve", "pool"):
            if P.pending[eng]:
                if stop_at is None:
                    raise RuntimeError(f"pending tokens on {eng}")
                for p in P.pending[eng]:
                    p.val = P.cnt[eng]
                P.pending[eng] = []
        P.check()

        with nc.Block() as block:
            @block.sync
            def _(e):
                P.emit("sp", e)

            @block.gpsimd
            def _(e):
                P.emit("pool", e)

            @block.tensor
            def _(e):
                P.emit("pe", e)

            @block.scalar
            def _(e):
                P.emit("act", e)

            @block.vector
            def _(e):
                P.emit("dve", e)
    stats = {e: len(P.ops[e]) for e in ENGS}
    return nc, stats


def _deint():
    return np.concatenate([np.arange(0, 64, 2), np.arange(1, 64, 2)])


HEAD_ORDER_A = [0, 4, 1, 5, 2, 6, 3, 7]


def w_in_perm():
    perm = np.arange(NIN)
    di = _deint()
    qa = np.concatenate([h * 64 + di for h in HEAD_ORDER_A])
    ka = np.concatenate([C_KA + h * 64 + di for h in range(2)])
    perm[0:512] = qa
    perm[C_KA:C_KA + 128] = ka
    return perm


def na_bias_index():
    idx = np.full((5, 128, 5, 128), 465, np.int64)
    a = np.arange(128) // 64
    kc = np.arange(128) % 64
    e = np.arange(128) // 64
    c = np.arange(128) % 64
    cs = np.clip(c - 8, 0, 48)
    for cfg in range(5):
        off = cfg
        rs_rel = {0: -e, 1: -2 - e, 2: -4 + 0 * e, 3: -4 - e, 4: -6 - e}[cfg]
        for j in range(5):
            dr = 2 * (j - off) + a[:, None] - e[None, :]
            vrow = (dr >= rs_rel[None, :]) & (dr <= rs_rel[None, :] + 7)
            vcol = (kc[:, None] >= cs[None, :]) & (kc[:, None] <= cs[None, :] + 15)
            dc = kc[:, None] - c[None, :]
            val = (dr + 7) * 31 + (dc + 15)
            ok = vrow & vcol
            idx[cfg, :, j, :] = np.where(ok, val, 465)
    return idx.reshape(5, 128, 640)


def rope_table(S):
    t = np.arange(S)
    inv = 1.0 / (10000.0 ** (np.arange(16) * 2.0 / 32))
    ang = np.concatenate([(t // GRID_W)[:, None] * inv[None, :], (t % GRID_W)[:, None] * inv[None, :]], -1)
    return np.concatenate([np.cos(ang), np.sin(ang)], -1).astype(np.float32)


def prep_shared(inp, depth):
    f = lambda a: np.ascontiguousarray(np.asarray(a, dtype=np.float32))
    L = depth
    perm = w_in_perm()
    di = _deint()
    out = {}
    out["w_ada"] = f(inp["w_ada"][:L])
    out["b_ada"] = f(inp["b_ada"][:L])
    out["b_ada_pm"] = f(np.asarray(inp["b_ada"][:L]).reshape(L, 48, 128).transpose(2, 0, 1).reshape(128, L * 48))
    out["w_in"] = f(np.asarray(inp["w_in"][:L])[:, :, perm])
    qg = np.asarray(inp["q_norm_a"][:L])[:, di]
    kg = np.asarray(inp["k_norm_a"][:L])[:, di]
    out["qkg"] = f(np.concatenate([np.tile(qg, (1, 8)), np.tile(kg, (1, 2))], 1))
    rows = np.concatenate([h * 64 + np.arange(64) for h in HEAD_ORDER_A])
    out["w_out_a"] = f(np.asarray(inp["w_out_a"][:L])[:, rows, :])
    out["w_out_b"] = f(inp["w_out_b"][:L])
    out["w_out"] = f(inp["w_out"][:L])
    out["w_ff1"] = f(inp["w_ff1"][:L])
    out["w_ff2"] = f(inp["w_ff2"][:L])
    for k in ("ln_mix_g", "ln_mix_b", "ln_ff_g", "ln_ff_b"):
        out[k] = f(inp[k][:L])
    rpb = np.asarray(inp["rpb_b"][:L], dtype=np.float32).reshape(L, 8, 465)
    rpb_pad = np.concatenate([rpb, np.full((L, 8, 1), NEG_FILL, np.float32)], -1)
    idx = na_bias_index()
    out["bt"] = f(rpb_pad[:, :, idx].transpose(0, 2, 1, 3, 4))
    return out


_PROG_CACHE = {}


def run_cores(seq_lens, depth, core_inputs, stop_at=None):
    key = (tuple(seq_lens), depth, stop_at)
    if key not in _PROG_CACHE:
        _PROG_CACHE[key] = build_program(list(seq_lens), depth, stop_at)
    nc, stats = _PROG_CACHE[key]
    res = run_bass_kernel_spmd(nc, core_inputs, core_ids=list(range(len(core_inputs))))
    return res


def kernel(x_prompt, x_sample, c_prompt, c_sample, w_ada, b_ada, w_in, q_norm_a, k_norm_a, rpb_b,
           w_out_a, w_out_b, w_out, ln_mix_g, ln_mix_b, w_ff1, w_ff2, ln_ff_g, ln_ff_b):
    inp = dict(w_ada=w_ada, b_ada=b_ada, w_in=w_in, q_norm_a=q_norm_a, k_norm_a=k_norm_a, rpb_b=rpb_b,
               w_out_a=w_out_a, w_out_b=w_out_b, w_out=w_out, ln_mix_g=ln_mix_g, ln_mix_b=ln_mix_b,
               w_ff1=w_ff1, w_ff2=w_ff2, ln_ff_g=ln_ff_g, ln_ff_b=ln_ff_b)
    depth = 4
    ncores = 8
    x_prompt = np.asarray(x_prompt, dtype=np.float32)
    x_sample = np.asarray(x_sample, dtype=np.float32)
    c_prompt = np.asarray(c_prompt, dtype=np.float32)
    c_sample = np.asarray(c_sample, dtype=np.float32)
    shared = prep_shared(inp, depth)
    SP, SS = x_prompt.shape[1], x_sample.shape[1]
    npp = x_prompt.shape[0] // ncores
    nsp = x_sample.shape[0] // ncores
    seq_lens = [SP] * npp + [SS] * nsp
    for S in set(seq_lens):
        shared[f"cs{S}"] = rope_table(S)
    core_inputs = []
    for i in range(ncores):
        m = dict(shared)
        cs = []
        for j in range(npp):
            m[f"x{j}"] = np.ascontiguousarray(x_prompt[i * npp + j])
            cs.append(c_prompt[i * npp + j])
        for j in range(nsp):
            m[f"x{npp + j}"] = np.ascontiguousarray(x_sample[i * nsp + j])
            cs.append(c_sample[i * nsp + j])
        c = np.stack(cs, 0)
        NS = len(seq_lens)
        m["cT"] = np.ascontiguousarray(c.T.reshape(KC, 128, NS).transpose(1, 0, 2).reshape(128, KC * NS))
        core_inputs.append(m)
    res = run_cores(seq_lens, depth, core_inputs)
    yp = np.empty_like(x_prompt)
    ysm = np.empty_like(x_sample)
    for i in range(ncores):
        r = res.results[i]
        for j in range(npp):
            yp[i * npp + j] = r[f"y{j}"]
        for j in range(nsp):
            ysm[i * nsp + j] = r[f"y{npp + j}"]
    return (yp, ysm)
```

```python
import numpy as np
from contextlib import ExitStack
from collections import deque

import concourse.bass as bass
import concourse.mybir as mybir
from concourse.bass_utils import run_bass_kernel_spmd

F32 = mybir.dt.float32
BF16 = mybir.dt.bfloat16
ALU = mybir.AluOpType
AF = mybir.ActivationFunctionType
AX = mybir.AxisListType

D = 1024
NIN = 4352
DFF = 4096
KC = 8
GRID_W = 64
ALPHA = 8.0 ** 0.25
EPS = 1e-6
TB = 512
NT = 4
C_QA, C_KA, C_VA, C_QB, C_KB, C_VB, C_GA, C_GB = 0, 512, 640, 768, 1280, 1792, 2304, 3328
UNIT = 2048
NSLOT = 7
NEG_FILL = -30000.0

def unit_table():
    units = []
    units.append(("kvA", 0, "w_in", 0, 8, C_KA, 256))
    for i in range(2):
        units.append(("vB", i, "w_in", 4 * i, 4, C_VB, 512))
    for i in range(2):
        units.append(("kB", i, "w_in", 0, 8, C_KB + 256 * i, 256))
    for i in range(2):
        units.append(("qA", i, "w_in", 4 * i, 4, C_QA, 512))
    for i in range(2):
        units.append(("qB", i, "w_in", 0, 8, C_QB + 256 * i, 256))
    for i in range(4):
        units.append(("ga", i, "w_in", 0, 8, C_GA + 256 * i, 256))
    for i in range(4):
        units.append(("gb", i, "w_in", 0, 8, C_GB + 256 * i, 256))
    for i in range(2):
        units.append(("oa", i, "w_out_a", 0, 4, 512 * i, 512))
    for i in range(2):
        units.append(("ob", i, "w_out_b", 0, 4, 512 * i, 512))
    for h in range(2):
        for i in range(2):
            units.append(("wo", 2 * h + i, "w_out", 4 * i, 4, 512 * h, 512))
    for i in range(16):
        units.append(("f1", i, "w_ff1", 0, 8, 256 * i, 256))
    for h in range(2):
        for g in range(8):
            units.append(("f2", 8 * h + g, "w_ff2", 4 * g, 4, 512 * h, 512))
    return units


UNITS = unit_table()
UIDX = {(u[0], u[1]): i for i, u in enumerate(UNITS)}
NU = len(UNITS)


class Tok:
    __slots__ = ("sem", "val", "eng")

    def __init__(self, sem, val, eng):
        self.sem, self.val, self.eng = sem, val, eng


class Region:
    __slots__ = ("name", "w", "r")

    def __init__(self, name):
        self.name, self.w, self.r = name, None, {}


class Chan:
    def __init__(self, sem):
        self.sem, self.n, self.last = sem, 0, None


ENGS = ("pe", "act", "dve", "pool", "sp")


class _Recorder:
    def __init__(self):
        self.call = None

    def __getattr__(self, name):
        def f(*args, **kwargs):
            assert self.call is None
            self.call = (name, args, kwargs)
            return None
        return f


def _eager(fn):
    rec = _Recorder()
    fn(rec)
    name, args, kwargs = rec.call

    def replay(e):
        return getattr(e, name)(*args, **kwargs)
    return replay


class Prog:
    def __init__(self, nc, es):
        self.nc = nc
        self.es = es
        self.ops = {e: [] for e in ENGS}
        self.sem = {e: es.enter_context(nc.semaphore("s_" + e)) for e in ENGS}
        self.cnt = {e: 0 for e in ENGS}
        self.pending = {e: [] for e in ENGS}
        self.last_arena = {}
        self.guard = []
        self.nsem = 5
        self.stage = 'pre'
        self.suffix = ''
        self.annotate = False

    def chan(self, name):
        self.nsem += 1
        return Chan(self.es.enter_context(self.nc.semaphore(name)))

    def _deps(self, eng, reads, writes, extra):
        waits = []
        for r in reads:
            if r.w is not None and not (r.w.eng == eng and eng == "pe"):
                waits.append(r.w)
        for w in writes:
            if w.w is not None and not (w.w.eng == eng and eng == "pe"):
                waits.append(w.w)
            for e, t in w.r.items():
                if not (e == eng and eng == "pe"):
                    waits.append(t)
        waits.extend(t for t in extra if t is not None)
        return waits

    def op(self, eng, fn, reads=(), writes=(), extra=(), signal=True, arena=False):
        if arena:
            extra = list(extra) + self.guard
        waits = self._deps(eng, reads, writes, extra)
        if signal:
            self.cnt[eng] += 1
            tok = Tok(self.sem[eng], self.cnt[eng], eng)
            for p in self.pending[eng]:
                p.val = self.cnt[eng]
            self.pending[eng] = []
        else:
            tok = Tok(self.sem[eng], None, eng)
            self.pending[eng].append(tok)
        self.ops[eng].append((_eager(fn), waits, tok if signal else None, None, self.stage))
        for r in reads:
            r.r[eng] = tok
        for w in writes:
            w.w = tok
            w.r = {}
        if arena:
            self.last_arena[eng] = tok
        return tok

    def dma(self, q, out_ap, in_ap, chan, reads=(), writes=(), extra=(), arena=False, serialize=True):
        if arena:
            extra = list(extra) + self.guard
        waits = self._deps("dma", reads, writes, list(extra) + ([chan.last] if serialize else []))
        chan.n += 1
        tok = Tok(chan.sem, 16 * chan.n, None)
        chan.last = tok

        def fn(e, out_ap=out_ap, in_ap=in_ap):
            return e.dma_start(out=out_ap, in_=in_ap)

        self.ops[q].append((fn, waits, None, tok, self.stage))
        for r in reads:
            r.r["dma" + str(id(chan))] = tok
        for w in writes:
            w.w = tok
            w.r = {}
        if arena:
            self.last_arena["dma" + str(id(chan))] = tok
        return tok

    def switch_mode(self):
        self.guard = [t for t in self.last_arena.values()]
        self.last_arena = {}

    def check(self):
        for e in ENGS:
            assert not self.pending[e], f"unresolved pending tokens on {e}"
        ptr = {e: 0 for e in ENGS}
        semv = {}
        total = sum(len(v) for v in self.ops.values())
        done = 0
        progress = True
        while progress:
            progress = False
            for e in ENGS:
                ops = self.ops[e]
                while ptr[e] < len(ops):
                    fn, waits, tok, dtok, _st = ops[ptr[e]]
                    ok = True
                    for w in waits:
                        assert w.val is not None
                        if semv.get(id(w.sem), 0) < w.val:
                            ok = False
                            break
                    if not ok:
                        break
                    if tok is not None:
                        semv[id(tok.sem)] = semv.get(id(tok.sem), 0) + 1
                        assert semv[id(tok.sem)] == tok.val
                    if dtok is not None:
                        semv[id(dtok.sem)] = semv.get(id(dtok.sem), 0) + 16
                        assert semv[id(dtok.sem)] == dtok.val
                    ptr[e] += 1
                    done += 1
                    progress = True
        if done != total:
            msg = {e: (ptr[e], len(self.ops[e])) for e in ENGS}
            raise RuntimeError(f"static deadlock: {msg}")

    def emit(self, eng_name, e):
        waited = {}
        for fn, waits, tok, dtok, _st in self.ops[eng_name]:
            for w in waits:
                k = id(w.sem)
                if waited.get(k, 0) < w.val:
                    e.wait_ge(w.sem, w.val)
                    waited[k] = w.val
            ins = fn(e)
            if self.annotate:
                ins.annotate(_st)
            if tok is not None:
                ins.then_inc(tok.sem, 1)
            if dtok is not None:
                ins.then_inc(dtok.sem, 16)


def mk(tensor, off, dims):
    return bass.AP(tensor, off, [list(d) for d in dims])


class _Stop(Exception):
    pass


def build_program(seq_lens, depth, stop_at=None, annotate=False):
    NS = len(seq_lens)
    L = depth
    SMAX = max(seq_lens)
    lens_set = sorted(set(seq_lens))
    nc = bass.Bass("TRN2", target_bir_lowering=False)

    def dram_in(name, shape, dt=F32):
        return nc.dram_tensor(name, list(shape), dt, kind="ExternalInput")

    xs = [dram_in(f"x{s}", [seq_lens[s], D]) for s in range(NS)]
    ys = [nc.dram_tensor(f"y{s}", [seq_lens[s], D], F32, kind="ExternalOutput") for s in range(NS)]
    cT = dram_in("cT", [128, KC * NS])
    w_ada = dram_in("w_ada", [L, D, 6 * D])
    b_ada = dram_in("b_ada", [L, 6 * D])
    b_ada_pm = dram_in("b_ada_pm", [128, L * 48])
    wsrc = {
        "w_in": dram_in("w_in", [L, D, NIN]),
        "w_out_a": dram_in("w_out_a", [L, 512, D]),
        "w_out_b": dram_in("w_out_b", [L, 512, D]),
        "w_out": dram_in("w_out", [L, D, D]),
        "w_ff1": dram_in("w_ff1", [L, D, DFF]),
        "w_ff2": dram_in("w_ff2", [L, DFF, D]),
    }
    qkg = dram_in("qkg", [L, 640])
    lnv = {k: dram_in(k, [L, D]) for k in ("ln_mix_g", "ln_mix_b", "ln_ff_g", "ln_ff_b")}
    bt = dram_in("bt", [L, 5, 8, 128, 640])
    cstab = {S: dram_in(f"cs{S}", [S, 64]) for S in lens_set}

    ws = [nc.dram_tensor(f"ws{l}", [NU, 128, UNIT], BF16) for l in range(L)]
    ebs = nc.dram_tensor("ebs", [L, 5, 8, 128, 640], BF16)
    kbs = nc.dram_tensor("kbs", [128, 4, SMAX], BF16)
    vbs = nc.dram_tensor("vbs", [SMAX // 128, 128, 520], BF16)
    grow = nc.dram_tensor("grow", [L, NS, 2 * D], F32)

    es = ExitStack()
    with es:
        P = Prog(nc, es)
        P.annotate = annotate

        def sb(name, shape, dt):
            return es.enter_context(nc.sbuf_tensor(name, list(shape), dt))

        ident = sb("ident", [128, 128], BF16)
        negh = sb("negh", [128, 16], F32)
        MODS = sb("MODS", [128, L * NS * 4 * KC], F32)
        G12 = sb("G12", [128, 2, D], F32)
        LNV = sb("LNV", [128, 4, D], F32)
        QKG = sb("QKG", [128, 640], F32)
        CSr = sb("CSr", [128, 2, 64], F32)
        kAT = sb("kAT", [128, SMAX], BF16)
        vAug = sb("vAug", [128, SMAX // 128, 256], BF16)
        kBr = sb("kBr", [128, 4, 6, 128], BF16)
        vBr = sb("vBr", [128, 6, 520], BF16)
        EBr = sb("EBr", [128, 3, 640], BF16)
        wring = sb("wring", [128, NSLOT, UNIT], BF16)
        xt = sb("xt", [128, NT, D], F32)
        xn = sb("xn", [128, 4, D], BF16)
        xin = sb("xin", [128, 2, D], F32)
        uT = sb("uT", [128, KC, TB], BF16)
        WK = sb("WK", [128, 2, D], F32)
        st6 = sb("st6", [128, 8, 12], F32)
        mv = sb("mv", [128, 8, 4], F32)
        ssq = sb("ssq", [128, 4, 16], F32)
        RT = sb("RT", [128, 4, 256], F32)
        QR = sb("QR", [128, 2, 512], BF16)
        vBst = sb("vBst", [128, 2, 520], BF16)
        silc = sb("silc", [128, KC * NS], BF16)
        A_QAT, A_QBT, A_PA, A_PB, A_REC, A_AOT, A_OB, A_RECB, A_OBT, A_TA, A_TB_, A_T12, A_MIT = (
            0, 2048, 4096, 6144, 7424, 9472, 11520, 12544, 12608, 14656, 16704, 18752, 20800)
        A_END = 20800 + 4096
        A_HT, A_R = 0, 16384
        ARENA = max(A_END, A_R + 1024)
        arena = sb("arena", [128, ARENA], BF16)

        def abf(off, dims):
            return mk(arena, off, [[ARENA, 128]] + dims)

        banks = [es.enter_context(nc.psum_tensor(f"bank{i}", [128, 512], F32)) for i in range(8)]
        banks_bf = [b.bitcast(BF16) for b in banks]
        bank_r = [Region(f"bank{i}") for i in range(8)]
        free_banks = deque(range(8))

        def acq():
            assert free_banks, "out of PSUM banks"
            return free_banks.popleft()

        def rel(b):
            free_banks.append(b)

        R = {}

        def reg(name):
            if name not in R:
                R[name] = Region(name)
            return R[name]

        ch_x = [P.chan(f"chx{i}") for i in range(NT)]
        ch_w = [P.chan(f"chw{i}") for i in range(NSLOT)]
        ch_st = [P.chan(f"chst{i}") for i in range(4)]
        ch_kb = [P.chan(f"chkb{i}") for i in range(6)]
        ch_vb = [P.chan(f"chvb{i}") for i in range(6)]
        ch_eb = [P.chan(f"cheb{i}") for i in range(3)]
        ch_cs = [P.chan(f"chcs{i}") for i in range(2)]
        ch_misc = P.chan("chmisc")
        ch_xin = [P.chan(f"chxin{i}") for i in range(2)]
        ch_xinp = [P.chan(f"chxinp{i}") for i in range(2)]
        ch_miscp = P.chan("chmiscp")
        ch_wp = [P.chan(f"chwp{i}") for i in range(NSLOT)]
        ch_pre = [P.chan(f"chpre{l}") for l in range(L)]
        ch_kst = [P.chan(f"chkst{i}") for i in range(2)]
        st_rr = [0]

        def store_chan():
            c = ch_st[st_rr[0] % len(ch_st)]
            st_rr[0] += 1
            return c

        wslot_r = [Region(f"wslot{i}") for i in range(NSLOT)]
        wfree = deque(range(NSLOT))
        wsched = []
        wstate = {"next": 0, "ptr": 0, "loaded": {}}

        def wpump():
            while wstate["next"] < len(wsched) and wfree:
                l, u = wsched[wstate["next"]]
                slot = wfree.popleft()
                src = mk(ws[l], u * 128 * UNIT, [[UNIT, 128], [1, UNIT]])
                P.dma("sp", wring[:, slot, :], src, ch_w[slot],
                      reads=[reg(f"ws{l}")], writes=[wslot_r[slot]])
                wstate["loaded"][wstate["next"]] = slot
                wstate["next"] += 1

        def wget(l, name, idx):
            i = wstate["ptr"]
            assert wsched[i] == (l, UIDX[(name, idx)]), (wsched[i], l, name, idx)
            wpump()
            assert i in wstate["loaded"], "weight ring exhausted (would deadlock)"
            wstate["ptr"] += 1
            return wstate["loaded"][i]

        def wdone(slot):
            wfree.append(slot)
            wpump()

        def wap(slot, nk, ncols, k, c0, n):
            return mk(wring, slot * UNIT + k * ncols + c0, [[NSLOT * UNIT, 128], [1, n]])

        for s in range(NS):
            nb = seq_lens[s] // TB
            for l in range(L):
                for b in range(nb):
                    wsched += [(l, UIDX[("kvA", 0)]), (l, UIDX[("vB", 0)]), (l, UIDX[("vB", 1)]),
                               (l, UIDX[("kB", 0)]), (l, UIDX[("kB", 1)])]
                for b in range(nb):
                    seq = [("qA", 0), ("qA", 1), ("qB", 0), ("qB", 1)]
                    for g in range(2):
                        seq += [("ga", 2 * g), ("gb", 2 * g), ("oa", g), ("ob", g), ("ga", 2 * g + 1), ("gb", 2 * g + 1)]
                    seq += [("wo", i) for i in range(4)]
                    seq += [("f1", i) for i in range(16)]
                    seq += [("f2", i) for i in range(16)]
                    wsched += [(l, UIDX[k]) for k in seq]

        def ckpt(name):
            P.stage = name + P.suffix
            if stop_at == name:
                raise _Stop()

        def body():
            r_ident = reg("ident")
            P.op("pool", lambda e: e.memset(ident[:], 0.0), writes=[r_ident])
            P.op("pool", lambda e: e.affine_select(out=ident[:], in_=ident[:], pattern=[[-1, 128]],
                                                   compare_op=ALU.not_equal, fill=1.0, base=0,
                                                   channel_multiplier=1), reads=[r_ident], writes=[r_ident])
            P.op("pool", lambda e: e.memset(negh[:], -0.5), writes=[reg("negh")])
            P.op("pool", lambda e: e.memset(vAug[:], 1.0), writes=[reg("vAugAll")])
            P.op("pool", lambda e: e.memset(vBst[:], 1.0), writes=[reg("vBst0"), reg("vBst1")])

            def precast(l):
                for ui, (name, idx, src, k0, nk, c0, ncols) in enumerate(UNITS):
                    w = wsrc[src]
                    rows = w.shape[1]
                    cols = w.shape[2]
                    base = l * rows * cols + (k0 * 128) * cols + c0
                    sap = mk(w, base, [[cols, 128], [128 * cols, nk], [1, ncols]])
                    dap = mk(ws[l], ui * 128 * UNIT, [[UNIT, 128], [ncols, nk], [1, ncols]])
                    P.dma("pool", dap, sap, ch_pre[l], writes=[], serialize=False)
                reg(f"ws{l}").w = ch_pre[l].last

            ckpt('consts')
            precast(0)
            ckpt('precast0')

            ebreg = reg("ebs")
            ebtoks = []
            for l in range(L):
                for cfg in range(5):
                    for h in range(8):
                        k = (l * 5 + cfg) * 8 + h
                        slot = k % 2
                        wr = reg(f"WK{slot}")
                        er = reg(f"EBr{slot}")
                        off = ((l * 5 + cfg) * 8 + h) * 128 * 640
                        P.dma("sp", WK[:, slot, 0:640], mk(bt, off, [[640, 128], [1, 640]]), ch_x[slot], writes=[wr])
                        P.op("act", lambda e, slot=slot: e.activation(out=EBr[:, slot, :], in_=WK[:, slot, 0:640], func=AF.Exp),
                             reads=[wr], writes=[er])
                        t = P.dma("pool", mk(ebs, off, [[640, 128], [1, 640]]), EBr[:, slot, :], ch_st[slot], reads=[er])
                        ebtoks.append(t)
            ebreg.w = None
            ckpt('eb')
            eb_guard = [ch_st[0].last, ch_st[1].last]

            r_silc = reg("silc")
            P.dma("sp", WK[:, 0, 0:KC * NS], cT[:, :], ch_misc, writes=[reg("WK0")])
            P.op("act", lambda e: e.activation(out=silc[:], in_=WK[:, 0, 0:KC * NS], func=AF.Silu),
                 reads=[reg("WK0")], writes=[r_silc])
            r_bpm = reg("bpm")
            bpm = sb("bpm", [128, L * 48], F32)
            P.dma("sp", bpm[:], b_ada_pm[:, :], ch_misc, writes=[r_bpm])
            brow = mk(G12, 0, [[2 * D, NS], [1, 2 * D]])
            grow_sb = mk(LNV, 0, [[4 * D, NS], [1, 2 * D]])
            KIND_OF = {1: 0, 0: 1, 4: 2, 3: 3}
            for l in range(L):
                rb = reg("G12")
                P.dma("sp", brow[:, 0:D], mk(b_ada, l * 6 * D + 2 * D, [[0, NS], [1, D]]), ch_misc, writes=[rb])
                P.dma("sp", brow[:, D:2 * D], mk(b_ada, l * 6 * D + 5 * D, [[0, NS], [1, D]]), ch_misc, writes=[rb])
                for blk in range(6):
                    for half in range(2):
                        slots = []
                        for kh in range(2):
                            slot = wfree.popleft()
                            base = l * D * 6 * D + (kh * 4 * 128) * 6 * D + blk * D + half * 512
                            sap = mk(w_ada, base, [[6 * D, 128], [128 * 6 * D, 4], [1, 512]])
                            dap = mk(wring, slot * UNIT, [[NSLOT * UNIT, 128], [512, 4], [1, 512]])
                            P.dma("pool", dap, sap, ch_wp[slot], writes=[wslot_r[slot]])
                            slots.append(slot)
                        if blk in KIND_OF:
                            kind = KIND_OF[blk]
                            b = acq()
                            for cc in range(4):
                                for kc in range(KC):
                                    slot = slots[kc // 4]
                                    lhsT = mk(wring, slot * UNIT + (kc % 4) * 512 + cc * 128, [[NSLOT * UNIT, 128], [1, 128]])
                                    rhs = silc[:, kc * NS:(kc + 1) * NS]
                                    P.op("pe", lambda e, o=banks[b][:, cc * NS:(cc + 1) * NS], lhsT=lhsT, rhs=rhs, kc=kc:
                                         e.matmul(o, lhsT=lhsT, rhs=rhs, start=(kc == 0), stop=(kc == KC - 1)),
                                         reads=[wslot_r[slot], r_silc], writes=[bank_r[b]] if kc in (0, KC - 1) else [],
                                         signal=(kc == KC - 1))
                            for cc in range(4):
                                kc = half * 4 + cc
                                bcol = l * 48 + blk * 8 + kc
                                oap = mk(MODS, (l * NS * 4 + kind) * KC + kc, [[L * NS * 4 * KC, 128], [4 * KC, NS]])
                                addc = 1.0 if kind in (0, 2) else 0.0
                                P.op("dve", lambda e, oap=oap, i=banks[b][:, cc * NS:(cc + 1) * NS], sc=bpm[:, bcol:bcol + 1], addc=addc:
                                     e.tensor_scalar(out=oap, in0=i, scalar1=sc, scalar2=addc, op0=ALU.add, op1=ALU.add),
                                     reads=[bank_r[b], r_bpm], writes=[reg("MODS")])
                            rel(b)
                        else:
                            gi = 0 if blk == 2 else 1
                            b = acq()
                            for kc in range(KC):
                                slot = slots[kc // 4]
                                rhs = mk(wring, slot * UNIT + (kc % 4) * 512, [[NSLOT * UNIT, 128], [1, 512]])
                                lhsT = silc[:, kc * NS:(kc + 1) * NS]
                                P.op("pe", lambda e, o=banks[b][0:NS, :], lhsT=lhsT, rhs=rhs, kc=kc:
                                     e.matmul(o, lhsT=lhsT, rhs=rhs, start=(kc == 0), stop=(kc == KC - 1)),
                                     reads=[wslot_r[slot], r_silc], writes=[bank_r[b]] if kc in (0, KC - 1) else [],
                                     signal=(kc == KC - 1))
                            c0 = gi * D + half * 512
                            P.op("dve", lambda e, o=grow_sb[:, c0:c0 + 512], i=banks[b][0:NS, :], bb=brow[:, c0:c0 + 512]:
                                 e.tensor_tensor(out=o, in0=i, in1=bb, op=ALU.add),
                                 reads=[bank_r[b], rb], writes=[reg("LNV")])
                            P.op("dve", lambda e, o=grow_sb[:, c0:c0 + 512], gi=gi:
                                 e.tensor_scalar(out=o, in0=o, scalar1=1.0, scalar2=((0.5 if gi == 0 else 1.0) / ALPHA), op0=ALU.add, op1=ALU.mult),
                                 reads=[reg("LNV")], writes=[reg("LNV")])
                            rel(b)
                        for slot in slots:
                            wfree.append(slot)
                P.dma("pool", mk(grow, l * NS * 2 * D, [[2 * D, NS], [1, 2 * D]]), grow_sb[:, :], ch_miscp,
                      reads=[reg("LNV")], writes=[reg("grow")])

            ckpt('mods')
            for l in range(1, L):
                precast(l)
            ckpt('precast')

            def mods_ap(l, s, kind, kc):
                col = ((l * NS + s) * 4 + kind) * KC + kc
                return MODS[:, col:col + 1]

            def ln_stats(src_ap, src_reg, j, eps=EPS):
                rs, rm = reg(f"st6_{j}"), reg(f"mv_{j}")
                P.op("dve", lambda e: e.bn_stats(out=st6[:, j, 0:6], in_=src_ap[:, 0:512]), reads=[src_reg], writes=[rs])
                P.op("dve", lambda e: e.bn_stats(out=st6[:, j, 6:12], in_=src_ap[:, 512:1024]), reads=[src_reg], writes=[rs])
                P.op("dve", lambda e: e.bn_aggr(out=mv[:, j, 0:2], in_=st6[:, j, :]), reads=[rs], writes=[rm])
                P.op("pool", lambda e: e.tensor_scalar(out=mv[:, j, 2:3], in0=mv[:, j, 1:2], scalar1=eps, scalar2=None, op0=ALU.add),
                     reads=[rm], writes=[rm])
                P.op("pool", lambda e: e.tensor_tensor(out=mv[:, j, 2:3], in0=mv[:, j, 2:3], in1=negh[:, 0:1], op=ALU.pow),
                     reads=[rm, reg("negh")], writes=[rm])
                P.op("dve", lambda e: e.scalar_tensor_tensor(out=mv[:, j, 3:4], in0=mv[:, j, 0:1], scalar=-1.0, in1=mv[:, j, 2:3],
                                                             op0=ALU.mult, op1=ALU.mult), reads=[rm], writes=[rm])
                return rm

            def ln_to_uT(tiles_ap, tiles_reg, l, s, kind_sc, kind_sh, loader=None, sbase=0, pair_mode=False):
                nsl = 4
                rxs = []
                for t in range(NT):
                    if loader is not None:
                        loader(t)
                    rm = ln_stats(tiles_ap[t], tiles_reg[t], t + sbase)
                    sl = t % nsl
                    rx = reg(f"xn{sl}")
                    P.op("act", lambda e, t=t, sl=sl: e.activation(out=xn[:, sl, :], in_=tiles_ap[t], func=AF.Identity,
                                                                   scale=mv[:, t + sbase, 2:3], bias=mv[:, t + sbase, 3:4]),
                         reads=[tiles_reg[t], rm], writes=[rx])
                    rxs.append(rx)
                groups = [[0, 1], [2, 3]] if pair_mode else [[0, 1, 2, 3]]
                for grp in groups:
                    ng = len(grp)
                    w = ng * 128
                    per_bank = 1024 // w
                    nb_ = KC // per_bank
                    tb = [acq() for _ in range(nb_)]
                    for gi_, t in enumerate(grp):
                        sl = t % nsl
                        for kc in range(KC):
                            oap = banks_bf[tb[kc // per_bank]][:, (kc % per_bank) * w + gi_ * 128:(kc % per_bank) * w + (gi_ + 1) * 128]
                            last = (kc == KC - 1)
                            P.op("pe", lambda e, oap=oap, i=xn[:, sl, kc * 128:(kc + 1) * 128]: e.transpose(oap, i, ident[:]),
                                 reads=[rxs[t], r_ident], writes=[bank_r[x] for x in tb] if (last or kc == 0) else [], signal=last)
                    tok0 = grp[0] * 128
                    for kc in range(KC):
                        iap = banks_bf[tb[kc // per_bank]][:, (kc % per_bank) * w:(kc % per_bank) * w + w]
                        sc, sh = mods_ap(l, s, kind_sc, kc), mods_ap(l, s, kind_sh, kc)
                        P.op("act", lambda e, kc=kc, iap=iap, sc=sc, sh=sh: e.activation(out=uT[:, kc, tok0:tok0 + w], in_=iap, func=AF.Identity, scale=sc, bias=sh),
                             reads=[bank_r[tb[kc // per_bank]], reg("MODS")], writes=[reg(f"uT{kc}")])
                    for b in tb:
                        rel(b)

            uT_regs = [reg(f"uT{kc}") for kc in range(KC)]

            def proj_tok(slots, nk_per, ncols, c0, n, tile, b, col0=0):
                for kc in range(KC):
                    slot = slots[kc // nk_per]
                    rhs = wap(slot, nk_per, ncols, kc % nk_per, c0, n)
                    lhsT = uT[:, kc, tile * 128:(tile + 1) * 128]
                    P.op("pe", lambda e, o=banks[b][:, col0:col0 + n], lhsT=lhsT, rhs=rhs, kc=kc:
                         e.matmul(o, lhsT=lhsT, rhs=rhs, start=(kc == 0), stop=(kc == KC - 1)),
                         reads=[wslot_r[slot], uT_regs[kc]], writes=[bank_r[b]] if kc in (0, KC - 1) else [],
                         signal=(kc == KC - 1))

            def proj_feat(slot, cc, b, rhs_fn, rhs_regs, nk=KC, ncols=256):
                for kc in range(nk):
                    lhsT = wap(slot, nk, ncols, kc, cc * 128, 128)
                    P.op("pe", lambda e, o=banks[b][:, :], lhsT=lhsT, rhs=rhs_fn(kc), kc=kc:
                         e.matmul(o, lhsT=lhsT, rhs=rhs, start=(kc == 0), stop=(kc == nk - 1)),
                         reads=[wslot_r[slot], rhs_regs[kc]], writes=[bank_r[b]] if kc in (0, nk - 1) else [],
                         signal=(kc == nk - 1))

            def qk_norm_rope(b, ncols, nh, gcol0, cs_slot, cs_reg, out_ap, out_reg, k, arena_out=False):
                wr = reg(f"WK{k}")
                ps = banks[b][:, 0:ncols]
                SQ0 = k * D
                QN0 = k * D + 512
                P.op("act", lambda e: e.activation(out=WK[:, k, 0:ncols], in_=ps, func=AF.Square), reads=[bank_r[b]], writes=[wr])
                rs = reg(f"ssq{k}")
                P.op("dve", lambda e: e.tensor_reduce(out=ssq[:, 2 * k, 0:nh], in_=mk(WK, SQ0, [[2 * D, 128], [64, nh], [1, 64]]), axis=AX.X, op=ALU.add),
                     reads=[wr], writes=[rs])
                P.op("pool", lambda e: e.tensor_scalar(out=ssq[:, 2 * k + 1, 0:nh], in0=ssq[:, 2 * k, 0:nh], scalar1=1.0 / 64.0, scalar2=EPS, op0=ALU.mult, op1=ALU.add),
                     reads=[rs], writes=[rs])
                P.op("pool", lambda e: e.tensor_tensor(out=ssq[:, 2 * k + 1, 0:nh], in0=ssq[:, 2 * k + 1, 0:nh], in1=negh[:, 0:nh], op=ALU.pow),
                     reads=[rs, reg("negh")], writes=[rs])
                wq = reg(f"WKq{k}")
                P.op("dve", lambda e: e.tensor_tensor(out=mk(WK, QN0, [[2 * D, 128], [64, nh], [1, 64]]),
                                                      in0=mk(banks[b], 0, [[512, 128], [64, nh], [1, 64]]),
                                                      in1=mk(ssq, (2 * k + 1) * 16, [[64, 128], [1, nh], [0, 64]]), op=ALU.mult),
                     reads=[bank_r[b], rs], writes=[wq], extra=[wr.w])
                P.op("dve", lambda e: e.tensor_tensor(out=WK[:, k, 512:512 + ncols], in0=WK[:, k, 512:512 + ncols], in1=QKG[:, gcol0:gcol0 + ncols], op=ALU.mult),
                     reads=[wq, reg("QKG")], writes=[wq])
                ev = mk(WK, QN0, [[2 * D, 128], [64, nh], [1, 32]])
                od = mk(WK, QN0 + 32, [[2 * D, 128], [64, nh], [1, 32]])
                Cb = mk(CSr, cs_slot * 64, [[128, 128], [0, nh], [1, 32]])
                Sb = mk(CSr, cs_slot * 64 + 32, [[128, 128], [0, nh], [1, 32]])
                T1 = mk(RT, (2 * k) * 256, [[1024, 128], [32, nh], [1, 32]])
                T2 = mk(RT, (2 * k + 1) * 256, [[1024, 128], [32, nh], [1, 32]])
                oe = mk(out_ap.tensor, out_ap.offset, [[out_ap.ap[0][0], 128], [64, nh], [1, 32]])
                oo = mk(out_ap.tensor, out_ap.offset + 32, [[out_ap.ap[0][0], 128], [64, nh], [1, 32]])
                r1, r2 = reg(f"RT{k}a"), reg(f"RT{k}b")
                P.op("dve", lambda e: e.tensor_tensor(out=T1, in0=ev, in1=Cb, op=ALU.mult), reads=[wq, cs_reg], writes=[r1])
                P.op("dve", lambda e: e.tensor_tensor(out=T2, in0=od, in1=Sb, op=ALU.mult), reads=[wq, cs_reg], writes=[r2])
                P.op("dve", lambda e: e.tensor_tensor(out=oe, in0=T1, in1=T2, op=ALU.subtract), reads=[r1, r2], writes=[out_reg], arena=arena_out)
                P.op("dve", lambda e: e.tensor_tensor(out=T1, in0=ev, in1=Sb, op=ALU.mult), reads=[wq, cs_reg], writes=[r1])
                P.op("dve", lambda e: e.tensor_tensor(out=T2, in0=od, in1=Cb, op=ALU.mult), reads=[wq, cs_reg], writes=[r2])
                P.op("dve", lambda e: e.tensor_tensor(out=oo, in0=T1, in1=T2, op=ALU.add), reads=[r1, r2], writes=[out_reg], arena=arena_out)

            def post_ln(t, half_banks, gi, lg, lb, wk):
                rw = reg(f"WK{wk}")
                rx = reg(f"xt{t}")
                for h in range(2):
                    P.op("dve", lambda e, h=h: e.tensor_tensor(out=WK[:, wk, h * 512:(h + 1) * 512], in0=banks[half_banks[h]][:, :],
                                                              in1=G12[:, gi, h * 512:(h + 1) * 512], op=ALU.mult),
                         reads=[bank_r[half_banks[h]], reg("G12")], writes=[rw])
                P.op("dve", lambda e: e.tensor_tensor(out=WK[:, wk, :], in0=WK[:, wk, :], in1=xt[:, t, :], op=ALU.add),
                     reads=[rx, rw], writes=[rw])
                rm = ln_stats(WK[:, wk, :], rw, t, eps=EPS / (ALPHA * ALPHA))
                P.op("act", lambda e: e.activation(out=WK[:, wk, :], in_=WK[:, wk, :], func=AF.Identity, scale=mv[:, t, 2:3], bias=mv[:, t, 3:4]),
                     reads=[rw, rm], writes=[rw])
                eng2 = "dve" if t % 2 == 0 else "pool"
                P.op(eng2, lambda e: e.tensor_tensor(out=WK[:, wk, :], in0=WK[:, wk, :], in1=LNV[:, lg, :], op=ALU.mult),
                     reads=[rw, reg("LNV")], writes=[rw])
                P.op(eng2, lambda e: e.tensor_tensor(out=xt[:, t, :], in0=WK[:, wk, :], in1=LNV[:, lb, :], op=ALU.add),
                     reads=[rw, reg("LNV")], writes=[rx])

            xt_regs = [reg(f"xt{t}") for t in range(NT)]
            xt_aps = [xt[:, t, :] for t in range(NT)]

            def load_x(src, s, blk):
                for t in range(NT):
                    row0 = blk * TB + t * 128
                    P.dma("sp", xt[:, t, :], mk(src, row0 * D, [[D, 128], [1, D]]), ch_x[t],
                          reads=[reg(f"y{s}_{blk * NT + t}")], writes=[xt_regs[t]])

            xin_aps = [xin[:, t % 2, :] for t in range(NT)]
            xin_regs = [reg(f"xin{t % 2}") for t in range(NT)]

            def stage_A(src, s, l, blk, pair_mode=False):
                def loader(t):
                    row0 = blk * TB + t * 128
                    P.dma("pool", xin[:, t % 2, :], mk(src, row0 * D, [[D, 128], [1, D]]), ch_xinp[t % 2],
                          reads=[reg(f"y{s}_{blk * NT + t}")], writes=[xin_regs[t]])
                ln_to_uT(xin_aps, xin_regs, l, s, 0, 1, loader=loader, sbase=4, pair_mode=pair_mode)

            def load_cs(S, tile, slot):
                P.dma("sp", CSr[:, slot, :], mk(cstab[S], tile * 128 * 64, [[64, 128], [1, 64]]), ch_cs[slot], writes=[reg(f"CS{slot}")])
                return reg(f"CS{slot}")

            cs_ctr = [0]

            first_layer_loaded = [False]
            for s in range(NS):
                S = seq_lens[s]
                NB = S // TB
                NTIL = S // 128
                for l in range(L):
                    src = xs[s] if l == 0 else ys[s]
                    misc_extra = eb_guard if not first_layer_loaded[0] else []
                    first_layer_loaded[0] = True
                    P.dma("sp", G12[:, 0, :], mk(grow, (l * NS + s) * 2 * D, [[0, 128], [1, D]]), ch_misc, reads=[reg("grow")], writes=[reg("G12")], extra=misc_extra)
                    P.dma("sp", G12[:, 1, :], mk(grow, (l * NS + s) * 2 * D + D, [[0, 128], [1, D]]), ch_misc, reads=[reg("grow")], writes=[reg("G12")])
                    for i, k in enumerate(("ln_mix_g", "ln_mix_b", "ln_ff_g", "ln_ff_b")):
                        P.dma("sp", LNV[:, i, :], mk(lnv[k], l * D, [[0, 128], [1, D]]), ch_misc, writes=[reg("LNV")])
                    P.dma("sp", QKG[:, :], mk(qkg, l * 640, [[0, 128], [1, 640]]), ch_misc, writes=[reg("QKG")])

                    P.switch_mode()
                    for blk in range(NB):
                        ckpt('p1x')
                        stage_A(src, s, l, blk)
                        ckpt('p1a')
                        s_kv = wget(l, "kvA", 0)
                        s_vb = [wget(l, "vB", 0), wget(l, "vB", 1)]
                        for t in range(NT):
                            tile = blk * NT + t
                            b = acq()
                            proj_tok([s_kv], 8, 256, 0, 256, t, b)
                            csl = cs_ctr[0] % 2
                            cs_ctr[0] += 1
                            csreg = load_cs(S, tile, csl)
                            rv = reg(f"vAug{tile}")
                            P.op("act", lambda e, b=b, tile=tile: e.activation(
                                out=mk(vAug, tile * 256, [[(SMAX // 128) * 256, 128], [192, 2], [1, 64]]),
                                in_=mk(banks[b], 128, [[512, 128], [64, 2], [1, 64]]), func=AF.Copy),
                                reads=[bank_r[b]], writes=[rv], extra=[reg("vAugAll").w])
                            ckpt('p1b')
                            kq = tile % 2
                            rqr = reg(f"QR{kq}")
                            qk_norm_rope(b, 128, 2, 512, csl, csreg, QR[:, kq, 0:128], rqr, kq)
                            rel(b)
                            b2 = acq()
                            P.op("pe", lambda e, b2=b2: e.transpose(banks_bf[b2][:, 0:128], QR[:, kq, 0:128], ident[:]),
                                 reads=[rqr, r_ident], writes=[bank_r[b2]])
                            P.op("dve", lambda e, b2=b2, tile=tile: e.tensor_copy(out=kAT[:, tile * 128:(tile + 1) * 128], in_=banks_bf[b2][:, 0:128]),
                                 reads=[bank_r[b2]], writes=[reg(f"kAT{tile}")])
                            rel(b2)
                            ckpt('p1c')
                            b3 = acq()
                            proj_tok(s_vb, 4, 512, 0, 512, t, b3)
                            vsl = tile % 2
                            rvs = reg(f"vBst{vsl}")
                            P.op("act", lambda e, b3=b3, vsl=vsl: e.activation(
                                out=mk(vBst, vsl * 520, [[2 * 520, 128], [65, 8], [1, 64]]),
                                in_=mk(banks[b3], 0, [[512, 128], [64, 8], [1, 64]]), func=AF.Copy),
                                reads=[bank_r[b3]], writes=[rvs])
                            rel(b3)
                            P.dma("pool", mk(vbs, tile * 128 * 520, [[520, 128], [1, 520]]), vBst[:, vsl, :], ch_kst[vsl],
                                  reads=[rvs], writes=[reg(f"vbs{tile}")])
                        wdone(s_kv)
                        for x in s_vb:
                            wdone(x)
                        ckpt('p1d')
                        rks = reg("kBst")
                        for i in range(2):
                            sl = wget(l, "kB", i)
                            for cc in range(2):
                                hp = 2 * i + cc
                                b = acq()
                                proj_feat(sl, cc, b, lambda kc: uT[:, kc, :], uT_regs)
                                P.op("dve", lambda e, b=b, hp=hp: e.tensor_copy(out=abf(hp * TB, [[1, TB]]), in_=banks[b][:, :]),
                                     reads=[bank_r[b]], writes=[rks], arena=True)
                                rel(b)
                            wdone(sl)
                        P.dma("pool", mk(kbs, blk * TB, [[4 * SMAX, 128], [SMAX, 4], [1, TB]]), abf(0, [[TB, 4], [1, TB]]), store_chan(),
                              reads=[rks], writes=[reg(f"kbs{blk}")], arena=True)

                    ckpt('p1')
                    kb_slot_of = {}
                    kb_state = {"next": 0}

                    def na_window(T):
                        return min(max(T - 2, 0), NTIL - 5)

                    def ensure_kv(upto):
                        while kb_state["next"] <= upto:
                            c = kb_state["next"]
                            slot = c % 6
                            P.dma("sp", kBr[:, :, slot, :], mk(kbs, c * 128, [[4 * SMAX, 128], [SMAX, 4], [1, 128]]), ch_kb[slot],
                                  reads=[reg(f"kbs{c // NT}")], writes=[reg(f"kBr{slot}")])
                            P.dma("sp", vBr[:, slot, :], mk(vbs, c * 128 * 520, [[520, 128], [1, 520]]), ch_vb[slot],
                                  reads=[reg(f"vbs{c}")], writes=[reg(f"vBr{slot}")])
                            kb_state["next"] += 1

                    eb_ctr = [0]
                    for blk in range(NB):
                        P.suffix = f'@{blk}'
                        ckpt('S')
                        if blk == 0:
                            stage_A(src, s, l, blk)
                        P.switch_mode()
                        load_x(src, s, blk)
                        ckpt('A')
                        s_qa = [wget(l, "qA", 0), wget(l, "qA", 1)]
                        qbanks = []
                        for t in range(NT):
                            b = acq()
                            proj_tok(s_qa, 4, 512, 0, 512, t, b)
                            qbanks.append(b)
                        for x in s_qa:
                            wdone(x)
                        for i in range(2):
                            sl = wget(l, "qB", i)
                            for cc in range(2):
                                hp = 2 * i + cc
                                b = acq()
                                proj_feat(sl, cc, b, lambda kc: uT[:, kc, :], uT_regs)
                                P.op("dve", lambda e, b=b, hp=hp: e.tensor_copy(out=abf(A_QBT + hp * 512, [[1, 512]]), in_=banks[b][:, :]),
                                     reads=[bank_r[b]], writes=[reg(f"qBT{hp}")], arena=True)
                                rel(b)
                            wdone(sl)
                        for t in range(NT):
                            tile = blk * NT + t
                            b = qbanks[t]
                            csl = cs_ctr[0] % 2
                            cs_ctr[0] += 1
                            csreg = load_cs(S, tile, csl)
                            kq = t % 2
                            rqr = reg(f"QR{kq}")
                            qk_norm_rope(b, 512, 8, 0, csl, csreg, QR[:, kq, 0:512], rqr, kq)
                            rel(b)
                            b2 = acq()
                            for c in range(4):
                                P.op("pe", lambda e, b2=b2, c=c: e.transpose(banks_bf[b2][:, c * 128:(c + 1) * 128], QR[:, kq, c * 128:(c + 1) * 128], ident[:]),
                                     reads=[rqr, r_ident], writes=[bank_r[b2]] if c in (0, 3) else [], signal=(c == 3))
                            P.op("act", lambda e, b2=b2, t=t: e.activation(
                                out=abf(A_QAT + t * 128, [[512, 4], [1, 128]]),
                                in_=mk(banks_bf[b2], 0, [[1024, 128], [128, 4], [1, 128]]), func=AF.Copy),
                                reads=[bank_r[b2]], writes=[reg(f"qAT{c}") for c in range(4)], arena=True)
                            rel(b2)
                        ckpt('NA')
                        na_st = {}
                        na_obk = {}
                        na_units = [(t, h) for t in range(NT) for h in range(8)]

                        def na_qk(t, h, u):
                            T = blk * NT + t
                            c0 = na_window(T)
                            cfg = 0 if T == 0 else 1 if T == 1 else 3 if T == NTIL - 2 else 4 if T == NTIL - 1 else 2
                            if h == 0:
                                ensure_kv(c0 + 4)
                            hp, side = h // 2, h % 2
                            esl = eb_ctr[0] % 3
                            eb_ctr[0] += 1
                            off = ((l * 5 + cfg) * 8 + h) * 128 * 640
                            P.dma("sp", EBr[:, esl, :], mk(ebs, off, [[640, 128], [1, 640]]), ch_eb[esl], writes=[reg(f"EBr{esl}")])
                            sa, sb2 = acq(), acq()
                            rhs = mk(arena, side * 64 * ARENA + A_QBT + hp * 512 + t * 128, [[ARENA, 64], [1, 128]])
                            for j in range(5):
                                slot = (c0 + j) % 6
                                lhsT = mk(kBr, side * 64 * (4 * 6 * 128) + (hp * 6 + slot) * 128, [[4 * 6 * 128, 64], [1, 128]])
                                o = banks[sa][:, j * 128:(j + 1) * 128] if j < 4 else banks[sb2][:, 0:128]
                                P.op("pe", lambda e, o=o, lhsT=lhsT, rhs=rhs: e.matmul(o, lhsT=lhsT, rhs=rhs, start=True, stop=True),
                                     reads=[reg(f"kBr{slot}"), reg(f"qBT{hp}")],
                                     writes=[bank_r[sa]] if j in (0, 3) else [bank_r[sb2]] if j == 4 else [],
                                     signal=(j >= 3), arena=True)
                            psl = u % 2
                            rp = reg(f"pB{psl}")
                            P.op("act", lambda e: e.activation(out=abf(A_PB + psl * 640, [[1, 512]]), in_=banks[sa][:, :], func=AF.Exp, scale=0.125),
                                 reads=[bank_r[sa]], writes=[rp], arena=True)
                            P.op("act", lambda e: e.activation(out=abf(A_PB + psl * 640 + 512, [[1, 128]]), in_=banks[sb2][:, 0:128], func=AF.Exp, scale=0.125),
                                 reads=[bank_r[sb2]], writes=[rp], arena=True)
                            rel(sa)
                            rel(sb2)
                            P.op("dve", lambda e: e.tensor_tensor(out=abf(A_PB + psl * 640, [[1, 640]]), in0=abf(A_PB + psl * 640, [[1, 640]]),
                                                                  in1=EBr[:, esl, :], op=ALU.mult),
                                 reads=[rp, reg(f"EBr{esl}")], writes=[rp], arena=True)
                            na_st[(t, h)] = (psl, c0)

                        def na_pv(t, h):
                            psl, c0 = na_st[(t, h)]
                            rp = reg(f"pB{psl}")
                            if h == 0:
                                na_obk[t] = [acq(), acq()]
                            obk = na_obk[t]
                            bk = obk[h // 4]
                            for j in range(5):
                                slot = (c0 + j) % 6
                                lhsT = abf(A_PB + psl * 640 + j * 128, [[1, 128]])
                                rhs2 = mk(vBr, slot * 520 + h * 65, [[6 * 520, 128], [1, 65]])
                                o = banks[bk][:, (h % 4) * 65:(h % 4) * 65 + 65]
                                P.op("pe", lambda e, o=o, lhsT=lhsT, rhs2=rhs2, j=j: e.matmul(o, lhsT=lhsT, rhs=rhs2, start=(j == 0), stop=(j == 4)),
                                     reads=[rp, reg(f"vBr{slot}")], writes=[bank_r[bk]] if j in (0, 4) else [], signal=(j == 4), arena=True)
                            if h != 7:
                                return
                            osl = t % 2
                            rob = reg(f"ob{osl}")
                            for g in range(2):
                                bk = obk[g]
                                rb_ap = mk(arena, A_RECB + osl * 32 + g * 8, [[ARENA, 128], [1, 8]]).bitcast(F32)
                                P.op("dve", lambda e, bk=bk, rb_ap=rb_ap: e.reciprocal(out=rb_ap, in_=mk(banks[bk], 64, [[512, 128], [65, 4]])),
                                     reads=[bank_r[bk]], writes=[reg(f"recB{osl}")], arena=True)
                                P.op("dve", lambda e, bk=bk, rb_ap=rb_ap, g=g, osl=osl: e.tensor_tensor(
                                    out=abf(A_OB + osl * 512 + g * 256, [[64, 4], [1, 64]]),
                                    in0=mk(banks[bk], 0, [[512, 128], [65, 4], [1, 64]]),
                                    in1=mk(rb_ap.tensor, rb_ap.offset, [[rb_ap.ap[0][0], 128], [1, 4], [0, 64]]), op=ALU.mult),
                                    reads=[bank_r[bk], reg(f"recB{osl}")], writes=[rob], arena=True)
                                rel(bk)
                            b2 = acq()
                            for hp in range(4):
                                P.op("pe", lambda e, b2=b2, hp=hp, osl=osl: e.transpose(banks_bf[b2][:, hp * 128:(hp + 1) * 128],
                                                                                        abf(A_OB + osl * 512 + hp * 128, [[1, 128]]), ident[:]),
                                     reads=[rob, r_ident], writes=[bank_r[b2]] if hp in (0, 3) else [], signal=(hp == 3), arena=True)
                            P.op("dve", lambda e, b2=b2, t=t: e.tensor_copy(out=abf(A_OBT + t * 128, [[512, 4], [1, 128]]),
                                                                            in_=mk(banks_bf[b2], 0, [[1024, 128], [128, 4], [1, 128]])),
                                 reads=[bank_r[b2]], writes=[reg(f"obT{hp}") for hp in range(4)], arena=True)
                            rel(b2)

                        na_qk(0, 0, 0)
                        for u in range(1, len(na_units)):
                            na_qk(na_units[u][0], na_units[u][1], u)
                            na_pv(*na_units[u - 1])
                        na_pv(*na_units[-1])

                        ckpt('GA')
                        NK = NTIL
                        for c in range(4):
                            ob_ = [acq(), acq()]
                            prev = None
                            pa_ctr = 0

                            def pv(prev):
                                kt, slots_ = prev
                                for side in range(2):
                                    lhsT = mk(vAug, kt * 256 + side * 128, [[(SMAX // 128) * 256, 128], [1, 128]])
                                    rhs = abf(A_PA + slots_[side] * 512, [[1, 512]])
                                    last = (kt == NK - 1)
                                    P.op("pe", lambda e, o=banks[ob_[side]][:, :], lhsT=lhsT, rhs=rhs, kt=kt:
                                         e.matmul(o, lhsT=lhsT, rhs=rhs, start=(kt == 0), stop=(kt == NK - 1)),
                                         reads=[reg(f"vAug{kt}"), reg(f"pA{slots_[side]}")],
                                         writes=[bank_r[ob_[side]]] if (kt == 0 or last) else [], signal=True, arena=True)

                            for kt in range(NK):
                                sb_ = [acq(), acq()]
                                slots_ = [(pa_ctr * 2) % 4, (pa_ctr * 2 + 1) % 4]
                                pa_ctr += 1
                                for side in range(2):
                                    lhsT = kAT[side * 64:(side + 1) * 64, kt * 128:(kt + 1) * 128]
                                    rhs = mk(arena, side * 64 * ARENA + A_QAT + c * 512, [[ARENA, 64], [1, 512]])
                                    P.op("pe", lambda e, o=banks[sb_[side]][:, :], lhsT=lhsT, rhs=rhs:
                                         e.matmul(o, lhsT=lhsT, rhs=rhs, start=True, stop=True),
                                         reads=[reg(f"kAT{kt}"), reg(f"qAT{c}")], writes=[bank_r[sb_[side]]], arena=True)
                                if prev is not None:
                                    pv(prev)
                                for side in range(2):
                                    P.op("act", lambda e, i=banks[sb_[side]][:, :], o=abf(A_PA + slots_[side] * 512, [[1, 512]]):
                                         e.activation(out=o, in_=i, func=AF.Exp, scale=0.125),
                                         reads=[bank_r[sb_[side]]], writes=[reg(f"pA{slots_[side]}")], arena=True)
                                    rel(sb_[side])
                                prev = (kt, slots_)
                            pv(prev)
                            for side in range(2):
                                b = ob_[side]
                                lo, hi = (0, 64) if side == 0 else (64, 128)
                                dlo, dhi = (64, 128) if side == 0 else (0, 64)
                                rr = reg(f"rec{side}")
                                rec_f = mk(arena, lo * ARENA + A_REC + side * 1024, [[ARENA, 64], [1, 1024]]).bitcast(F32)
                                P.op("dve", lambda e, b=b, rec_f=rec_f, dlo=dlo, dhi=dhi: e.reciprocal(out=rec_f, in_=banks[b][dlo:dhi, :]),
                                     reads=[bank_r[b]], writes=[rr], arena=True)
                                P.op("dve", lambda e, b=b, rec_f=rec_f, lo=lo, hi=hi, c=c:
                                     e.tensor_tensor(out=mk(arena, lo * ARENA + A_AOT + c * 512, [[ARENA, 64], [1, 512]]),
                                                     in0=banks[b][lo:hi, :], in1=rec_f, op=ALU.mult),
                                     reads=[bank_r[b], rr], writes=[reg(f"aoT{c}")], arena=True)
                                rel(b)

                        ckpt('D')
                        for g in range(2):
                            sl_ga = [None, None]
                            sl_gb = [None, None]
                            sl_ga[0] = wget(l, "ga", 2 * g)
                            sl_gb[0] = wget(l, "gb", 2 * g)
                            sl_oa = wget(l, "oa", g)
                            sl_ob = wget(l, "ob", g)
                            for q in range(4):
                                n = 4 * g + q
                                if q == 2:
                                    sl_ga[1] = wget(l, "ga", 2 * g + 1)
                                    sl_gb[1] = wget(l, "gb", 2 * g + 1)
                                bga, bgb, boa, bob = acq(), acq(), acq(), acq()
                                proj_feat(sl_ga[q // 2], q % 2, bga, lambda kc: uT[:, kc, :], uT_regs)
                                proj_feat(sl_gb[q // 2], q % 2, bgb, lambda kc: uT[:, kc, :], uT_regs)
                                proj_feat(sl_oa, q, boa, lambda kc: abf(A_AOT + kc * 512, [[1, 512]]), [reg(f"aoT{c}") for c in range(4)], nk=4, ncols=512)
                                proj_feat(sl_ob, q, bob, lambda kc: abf(A_OBT + kc * 512, [[1, 512]]), [reg(f"obT{c}") for c in range(4)], nk=4, ncols=512)
                                if q == 1:
                                    wdone(sl_ga[0])
                                    wdone(sl_gb[0])
                                ta = abf(A_TA + q * 512, [[1, 512]])
                                tb_ = abf(A_TB_ + q * 512, [[1, 512]])
                                P.op("act", lambda e, ta=ta, bga=bga: e.activation(out=ta, in_=banks[bga][:, :], func=AF.Tanh, scale=0.5),
                                     reads=[bank_r[bga]], writes=[reg(f"ta{q}")], arena=True)
                                P.op("act", lambda e, tb_=tb_, bgb=bgb: e.activation(out=tb_, in_=banks[bgb][:, :], func=AF.Tanh, scale=0.5),
                                     reads=[bank_r[bgb]], writes=[reg(f"tb{q}")], arena=True)
                                rel(bga)
                                rel(bgb)
                                t1 = abf(A_T12, [[1, 1024]]).bitcast(F32)
                                t2 = abf(A_T12 + 1024, [[1, 1024]]).bitcast(F32)
                                P.op("dve", lambda e, t1=t1, ta=ta, boa=boa: e.scalar_tensor_tensor(out=t1, in0=ta, scalar=1.0, in1=banks[boa][:, :], op0=ALU.add, op1=ALU.mult),
                                     reads=[reg(f"ta{q}"), bank_r[boa]], writes=[reg("t1")], arena=True)
                                P.op("dve", lambda e, t2=t2, tb_=tb_, bob=bob: e.scalar_tensor_tensor(out=t2, in0=tb_, scalar=1.0, in1=banks[bob][:, :], op0=ALU.add, op1=ALU.mult),
                                     reads=[reg(f"tb{q}"), bank_r[bob]], writes=[reg("t2")], arena=True)
                                rel(boa)
                                rel(bob)
                                P.op("pool", lambda e, t1=t1, t2=t2, n=n: e.tensor_tensor(out=abf(A_MIT + n * 512, [[1, 512]]), in0=t1, in1=t2, op=ALU.add),
                                     reads=[reg("t1"), reg("t2")], writes=[reg(f"miT{n}")], arena=True)
                            wdone(sl_ga[1])
                            wdone(sl_gb[1])
                            wdone(sl_oa)
                            wdone(sl_ob)

                        ckpt('E')
                        halves = []
                        for hf in range(2):
                            sl = [wget(l, "wo", 2 * hf), wget(l, "wo", 2 * hf + 1)]
                            bs = [acq() for _ in range(NT)]
                            for t in range(NT):
                                for kc in range(KC):
                                    slot = sl[kc // 4]
                                    rhs = wap(slot, 4, 512, kc % 4, 0, 512)
                                    lhsT = abf(A_MIT + kc * 512 + t * 128, [[1, 128]])
                                    P.op("pe", lambda e, o=banks[bs[t]][:, :], lhsT=lhsT, rhs=rhs, kc=kc: e.matmul(o, lhsT=lhsT, rhs=rhs, start=(kc == 0), stop=(kc == KC - 1)),
                                         reads=[wslot_r[slot], reg(f"miT{kc}")], writes=[bank_r[bs[t]]] if kc in (0, KC - 1) else [], signal=(kc == KC - 1), arena=True)
                            for x in sl:
                                wdone(x)
                            halves.append(bs)
                        for t in range(NT):
                            post_ln(t, [halves[0][t], halves[1][t]], 0, 0, 1, t % 2)
                            rel(halves[0][t])
                            rel(halves[1][t])

                        ckpt('F')
                        ln_to_uT(xt_aps, xt_regs, l, s, 2, 3)
                        P.switch_mode()
                        ckpt('G')
                        for i in range(16):
                            sl = wget(l, "f1", i)
                            for cc in range(2):
                                f = 2 * i + cc
                                b = acq()
                                proj_feat(sl, cc, b, lambda kc: uT[:, kc, :], uT_regs)
                                rs_ = f % 2
                                rr = reg(f"r{rs_}")
                                P.op("act", lambda e, b=b, rs_=rs_: e.activation(out=abf(A_R + rs_ * 512, [[1, 512]]), in_=banks[b][:, :], func=AF.Relu),
                                     reads=[bank_r[b]], writes=[rr], arena=True)
                                rel(b)
                                P.op("pool", lambda e, rs_=rs_, f=f: e.tensor_tensor(out=abf(A_HT + f * 512, [[1, 512]]), in0=abf(A_R + rs_ * 512, [[1, 512]]),
                                                                                     in1=abf(A_R + rs_ * 512, [[1, 512]]), op=ALU.mult),
                                     reads=[rr], writes=[reg(f"hT{f}")], arena=True)
                            wdone(sl)
                        ckpt('H')
                        halves = []
                        for hf in range(2):
                            bs = [acq() for _ in range(NT)]
                            for g in range(8):
                                sl = wget(l, "f2", 8 * hf + g)
                                for t in range(NT):
                                    for k4 in range(4):
                                        f = 4 * g + k4
                                        rhs = wap(sl, 4, 512, k4, 0, 512)
                                        lhsT = abf(A_HT + f * 512 + t * 128, [[1, 128]])
                                        first, last = (f == 0), (f == 31)
                                        P.op("pe", lambda e, o=banks[bs[t]][:, :], lhsT=lhsT, rhs=rhs, first=first, last=last:
                                             e.matmul(o, lhsT=lhsT, rhs=rhs, start=first, stop=last),
                                             reads=[wslot_r[sl], reg(f"hT{f}")], writes=[bank_r[bs[t]]] if (first or last) else [],
                                             signal=(last or k4 == 3 and t == NT - 1), arena=True)
                                wdone(sl)
                            halves.append(bs)
                            if hf == 0 and blk + 1 < NB:
                                P.suffix = f'@{blk + 1}'
                                ckpt('S')
                                stage_A(src, s, l, blk + 1, pair_mode=True)
                                P.suffix = f'@{blk}'
                                ckpt('H')
                        for t in range(NT):
                            post_ln(t, [halves[0][t], halves[1][t]], 1, 2, 3, t % 2)
                            rel(halves[0][t])
                            rel(halves[1][t])
                            row0 = blk * TB + t * 128
                            P.dma("pool", mk(ys[s], row0 * D, [[D, 128], [1, D]]), xt[:, t, :], store_chan(),
                                  reads=[xt_regs[t]], writes=[reg(f"y{s}_{blk * NT + t}")])

        try:
            body()
        except _Stop:
            pass

        fin = [c.last for c in ch_st] + [c.last for c in ch_kst]
        P.op("pool", lambda e: e.memset(negh[:, 0:1], -0.5), extra=fin)
        for eng in ("pe", "act", "dve", "pool"):
            if P.pending[eng]:
                if stop_at is None:
                    raise RuntimeError(f"pending tokens on {eng}")
                for p in P.pending[eng]:
                    p.val = P.cnt[eng]
                P.pending[eng] = []
        P.check()

        with nc.Block() as block:
            @block.sync
            def _(e):
                P.emit("sp", e)

            @block.gpsimd
            def _(e):
                P.emit("pool", e)

            @block.tensor
            def _(e):
                P.emit("pe", e)

            @block.scalar
            def _(e):
                P.emit("act", e)

            @block.vector
            def _(e):
                P.emit("dve", e)
    stats = {e: len(P.ops[e]) for e in ENGS}
    return nc, stats


def _deint():
    return np.concatenate([np.arange(0, 64, 2), np.arange(1, 64, 2)])


HEAD_ORDER_A = [0, 4, 1, 5, 2, 6, 3, 7]


def w_in_perm():
    perm = np.arange(NIN)
    di = _deint()
    qa = np.concatenate([h * 64 + di for h in HEAD_ORDER_A])
    ka = np.concatenate([C_KA + h * 64 + di for h in range(2)])
    perm[0:512] = qa
    perm[C_KA:C_KA + 128] = ka
    return perm


def na_bias_index():
    idx = np.full((5, 128, 5, 128), 465, np.int64)
    a = np.arange(128) // 64
    kc = np.arange(128) % 64
    e = np.arange(128) // 64
    c = np.arange(128) % 64
    cs = np.clip(c - 8, 0, 48)
    for cfg in range(5):
        off = cfg
        rs_rel = {0: -e, 1: -2 - e, 2: -4 + 0 * e, 3: -4 - e, 4: -6 - e}[cfg]
        for j in range(5):
            dr = 2 * (j - off) + a[:, None] - e[None, :]
            vrow = (dr >= rs_rel[None, :]) & (dr <= rs_rel[None, :] + 7)
            vcol = (kc[:, None] >= cs[None, :]) & (kc[:, None] <= cs[None, :] + 15)
            dc = kc[:, None] - c[None, :]
            val = (dr + 7) * 31 + (dc + 15)
            ok = vrow & vcol
            idx[cfg, :, j, :] = np.where(ok, val, 465)
    return idx.reshape(5, 128, 640)


def rope_table(S):
    t = np.arange(S)
    inv = 1.0 / (10000.0 ** (np.arange(16) * 2.0 / 32))
    ang = np.concatenate([(t // GRID_W)[:, None] * inv[None, :], (t % GRID_W)[:, None] * inv[None, :]], -1)
    return np.concatenate([np.cos(ang), np.sin(ang)], -1).astype(np.float32)


def prep_shared(inp, depth):
    f = lambda a: np.ascontiguousarray(np.asarray(a, dtype=np.float32))
    L = depth
    perm = w_in_perm()
    di = _deint()
    out = {}
    out["w_ada"] = f(inp["w_ada"][:L])
    out["b_ada"] = f(inp["b_ada"][:L])
    out["b_ada_pm"] = f(np.asarray(inp["b_ada"][:L]).reshape(L, 48, 128).transpose(2, 0, 1).reshape(128, L * 48))
    out["w_in"] = f(np.asarray(inp["w_in"][:L])[:, :, perm])
    qg = np.asarray(inp["q_norm_a"][:L])[:, di]
    kg = np.asarray(inp["k_norm_a"][:L])[:, di]
    out["qkg"] = f(np.concatenate([np.tile(qg, (1, 8)), np.tile(kg, (1, 2))], 1))
    rows = np.concatenate([h * 64 + np.arange(64) for h in HEAD_ORDER_A])
    out["w_out_a"] = f(np.asarray(inp["w_out_a"][:L])[:, rows, :])
    out["w_out_b"] = f(inp["w_out_b"][:L])
    out["w_out"] = f(inp["w_out"][:L])
    out["w_ff1"] = f(inp["w_ff1"][:L])
    out["w_ff2"] = f(inp["w_ff2"][:L])
    for k in ("ln_mix_g", "ln_mix_b", "ln_ff_g", "ln_ff_b"):
        out[k] = f(inp[k][:L])
    rpb = np.asarray(inp["rpb_b"][:L], dtype=np.float32).reshape(L, 8, 465)
    rpb_pad = np.concatenate([rpb, np.full((L, 8, 1), NEG_FILL, np.float32)], -1)
    idx = na_bias_index()
    out["bt"] = f(rpb_pad[:, :, idx].transpose(0, 2, 1, 3, 4))
    return out


_PROG_CACHE = {}


def run_cores(seq_lens, depth, core_inputs, stop_at=None):
    key = (tuple(seq_lens), depth, stop_at)
    if key not in _PROG_CACHE:
        _PROG_CACHE[key] = build_program(list(seq_lens), depth, stop_at)
    nc, stats = _PROG_CACHE[key]
    res = run_bass_kernel_spmd(nc, core_inputs, core_ids=list(range(len(core_inputs))))
    return res


def kernel(x_prompt, x_sample, c_prompt, c_sample, w_ada, b_ada, w_in, q_norm_a, k_norm_a, rpb_b,
           w_out_a, w_out_b, w_out, ln_mix_g, ln_mix_b, w_ff1, w_ff2, ln_ff_g, ln_ff_b):
    inp = dict(w_ada=w_ada, b_ada=b_ada, w_in=w_in, q_norm_a=q_norm_a, k_norm_a=k_norm_a, rpb_b=rpb_b,
               w_out_a=w_out_a, w_out_b=w_out_b, w_out=w_out, ln_mix_g=ln_mix_g, ln_mix_b=ln_mix_b,
               w_ff1=w_ff1, w_ff2=w_ff2, ln_ff_g=ln_ff_g, ln_ff_b=ln_ff_b)
    depth = 4
    ncores = 8
    x_prompt = np.asarray(x_prompt, dtype=np.float32)
    x_sample = np.asarray(x_sample, dtype=np.float32)
    c_prompt = np.asarray(c_prompt, dtype=np.float32)
    c_sample = np.asarray(c_sample, dtype=np.float32)
    shared = prep_shared(inp, depth)
    SP, SS = x_prompt.shape[1], x_sample.shape[1]
    npp = x_prompt.shape[0] // ncores
    nsp = x_sample.shape[0] // ncores
    seq_lens = [SP] * npp + [SS] * nsp
    for S in set(seq_lens):
        shared[f"cs{S}"] = rope_table(S)
    core_inputs = []
    for i in range(ncores):
        m = dict(shared)
        cs = []
        for j in range(npp):
            m[f"x{j}"] = np.ascontiguousarray(x_prompt[i * npp + j])
            cs.append(c_prompt[i * npp + j])
        for j in range(nsp):
            m[f"x{npp + j}"] = np.ascontiguousarray(x_sample[i * nsp + j])
            cs.append(c_sample[i * nsp + j])
        c = np.stack(cs, 0)
        NS = len(seq_lens)
        m["cT"] = np.ascontiguousarray(c.T.reshape(KC, 128, NS).transpose(1, 0, 2).reshape(128, KC * NS))
        core_inputs.append(m)
    res = run_cores(seq_lens, depth, core_inputs)
    yp = np.empty_like(x_prompt)
    ysm = np.empty_like(x_sample)
    for i in range(ncores):
        r = res.results[i]
        for j in range(npp):
            yp[i * npp + j] = r[f"y{j}"]
        for j in range(nsp):
            ysm[i * nsp + j] = r[f"y{npp + j}"]
    return (yp, ysm)
```

```python
import numpy as np
from contextlib import ExitStack
from collections import deque

import concourse.bass as bass
import concourse.mybir as mybir
from concourse.bass_utils import run_bass_kernel_spmd

F32 = mybir.dt.float32
BF16 = mybir.dt.bfloat16
ALU = mybir.AluOpType
AF = mybir.ActivationFunctionType
AX = mybir.AxisListType

D = 1024
NIN = 4352
DFF = 4096
KC = 8
GRID_W = 64
ALPHA = 8.0 ** 0.25
EPS = 1e-6
TB = 512
NT = 4
C_QA, C_KA, C_VA, C_QB, C_KB, C_VB, C_GA, C_GB = 0, 512, 640, 768, 1280, 1792, 2304, 3328
UNIT = 2048
NSLOT = 7
NEG_FILL = -30000.0

def unit_table():
    units = []
    units.append(("kvA", 0, "w_in", 0, 8, C_KA, 256))
    for i in range(2):
        units.append(("vB", i, "w_in", 4 * i, 4, C_VB, 512))
    for i in range(2):
        units.append(("kB", i, "w_in", 0, 8, C_KB + 256 * i, 256))
    for i in range(2):
        units.append(("qA", i, "w_in", 4 * i, 4, C_QA, 512))
    for i in range(2):
        units.append(("qB", i, "w_in", 0, 8, C_QB + 256 * i, 256))
    for i in range(4):
        units.append(("ga", i, "w_in", 0, 8, C_GA + 256 * i, 256))
    for i in range(4):
        units.append(("gb", i, "w_in", 0, 8, C_GB + 256 * i, 256))
    for i in range(2):
        units.append(("oa", i, "w_out_a", 0, 4, 512 * i, 512))
    for i in range(2):
        units.append(("ob", i, "w_out_b", 0, 4, 512 * i, 512))
    for h in range(2):
        for i in range(2):
            units.append(("wo", 2 * h + i, "w_out", 4 * i, 4, 512 * h, 512))
    for i in range(16):
        units.append(("f1", i, "w_ff1", 0, 8, 256 * i, 256))
    for h in range(2):
        for g in range(8):
            units.append(("f2", 8 * h + g, "w_ff2", 4 * g, 4, 512 * h, 512))
    return units


UNITS = unit_table()
UIDX = {(u[0], u[1]): i for i, u in enumerate(UNITS)}
NU = len(UNITS)


class Tok:
    __slots__ = ("sem", "val", "eng")

    def __init__(self, sem, val, eng):
        self.sem, self.val, self.eng = sem, val, eng


class Region:
    __slots__ = ("name", "w", "r")

    def __init__(self, name):
        self.name, self.w, self.r = name, None, {}


class Chan:
    def __init__(self, sem):
        self.sem, self.n, self.last = sem, 0, None


ENGS = ("pe", "act", "dve", "pool", "sp")


class _Recorder:
    def __init__(self):
        self.call = None

    def __getattr__(self, name):
        def f(*args, **kwargs):
            assert self.call is None
            self.call = (name, args, kwargs)
            return None
        return f


def _eager(fn):
    rec = _Recorder()
    fn(rec)
    name, args, kwargs = rec.call

    def replay(e):
        return getattr(e, name)(*args, **kwargs)
    return replay


class Prog:
    def __init__(self, nc, es):
        self.nc = nc
        self.es = es
        self.ops = {e: [] for e in ENGS}
        self.sem = {e: es.enter_context(nc.semaphore("s_" + e)) for e in ENGS}
        self.cnt = {e: 0 for e in ENGS}
        self.pending = {e: [] for e in ENGS}
        self.last_arena = {}
        self.guard = []
        self.nsem = 5
        self.stage = 'pre'
        self.suffix = ''
        self.annotate = False

    def chan(self, name):
        self.nsem += 1
        return Chan(self.es.enter_context(self.nc.semaphore(name)))

    def _deps(self, eng, reads, writes, extra):
        waits = []
        for r in reads:
            if r.w is not None and not (r.w.eng == eng and eng == "pe"):
                waits.append(r.w)
        for w in writes:
            if w.w is not None and not (w.w.eng == eng and eng == "pe"):
                waits.append(w.w)
            for e, t in w.r.items():
                if not (e == eng and eng == "pe"):
                    waits.append(t)
        waits.extend(t for t in extra if t is not None)
        return waits

    def op(self, eng, fn, reads=(), writes=(), extra=(), signal=True, arena=False):
        if arena:
            extra = list(extra) + self.guard
        waits = self._deps(eng, reads, writes, extra)
        if signal:
            self.cnt[eng] += 1
            tok = Tok(self.sem[eng], self.cnt[eng], eng)
            for p in self.pending[eng]:
                p.val = self.cnt[eng]
            self.pending[eng] = []
        else:
            tok = Tok(self.sem[eng], None, eng)
            self.pending[eng].append(tok)
        self.ops[eng].append((_eager(fn), waits, tok if signal else None, None, self.stage))
        for r in reads:
            r.r[eng] = tok
        for w in writes:
            w.w = tok
            w.r = {}
        if arena:
            self.last_arena[eng] = tok
        return tok

    def dma(self, q, out_ap, in_ap, chan, reads=(), writes=(), extra=(), arena=False, serialize=True):
        if arena:
            extra = list(extra) + self.guard
        waits = self._deps("dma", reads, writes, list(extra) + ([chan.last] if serialize else []))
        chan.n += 1
        tok = Tok(chan.sem, 16 * chan.n, None)
        chan.last = tok

        def fn(e, out_ap=out_ap, in_ap=in_ap):
            return e.dma_start(out=out_ap, in_=in_ap)

        self.ops[q].append((fn, waits, None, tok, self.stage))
        for r in reads:
            r.r["dma" + str(id(chan))] = tok
        for w in writes:
            w.w = tok
            w.r = {}
        if arena:
            self.last_arena["dma" + str(id(chan))] = tok
        return tok

    def switch_mode(self):
        self.guard = [t for t in self.last_arena.values()]
        self.last_arena = {}

    def check(self):
        for e in ENGS:
            assert not self.pending[e], f"unresolved pending tokens on {e}"
        ptr = {e: 0 for e in ENGS}
        semv = {}
        total = sum(len(v) for v in self.ops.values())
        done = 0
        progress = True
        while progress:
            progress = False
            for e in ENGS:
                ops = self.ops[e]
                while ptr[e] < len(ops):
                    fn, waits, tok, dtok, _st = ops[ptr[e]]
                    ok = True
                    for w in waits:
                        assert w.val is not None
                        if semv.get(id(w.sem), 0) < w.val:
                            ok = False
                            break
                    if not ok:
                        break
                    if tok is not None:
                        semv[id(tok.sem)] = semv.get(id(tok.sem), 0) + 1
                        assert semv[id(tok.sem)] == tok.val
                    if dtok is not None:
                        semv[id(dtok.sem)] = semv.get(id(dtok.sem), 0) + 16
                        assert semv[id(dtok.sem)] == dtok.val
                    ptr[e] += 1
                    done += 1
                    progress = True
        if done != total:
            msg = {e: (ptr[e], len(self.ops[e])) for e in ENGS}
            raise RuntimeError(f"static deadlock: {msg}")

    def emit(self, eng_name, e):
        waited = {}
        for fn, waits, tok, dtok, _st in self.ops[eng_name]:
            for w in waits:
                k = id(w.sem)
                if waited.get(k, 0) < w.val:
                    e.wait_ge(w.sem, w.val)
                    waited[k] = w.val
            ins = fn(e)
            if self.annotate:
                ins.annotate(_st)
            if tok is not None:
                ins.then_inc(tok.sem, 1)
            if dtok is not None:
                ins.then_inc(dtok.sem, 16)


def mk(tensor, off, dims):
    return bass.AP(tensor, off, [list(d) for d in dims])


class _Stop(Exception):
    pass


def build_program(seq_lens, depth, stop_at=None, annotate=False):
    NS = len(seq_lens)
    L = depth
    SMAX = max(seq_lens)
    lens_set = sorted(set(seq_lens))
    nc = bass.Bass("TRN2", target_bir_lowering=False)

    def dram_in(name, shape, dt=F32):
        return nc.dram_tensor(name, list(shape), dt, kind="ExternalInput")

    xs = [dram_in(f"x{s}", [seq_lens[s], D]) for s in range(NS)]
    ys = [nc.dram_tensor(f"y{s}", [seq_lens[s], D], F32, kind="ExternalOutput") for s in range(NS)]
    cT = dram_in("cT", [128, KC * NS])
    w_ada = dram_in("w_ada", [L, D, 6 * D])
    b_ada = dram_in("b_ada", [L, 6 * D])
    b_ada_pm = dram_in("b_ada_pm", [128, L * 48])
    wsrc = {
        "w_in": dram_in("w_in", [L, D, NIN]),
        "w_out_a": dram_in("w_out_a", [L, 512, D]),
        "w_out_b": dram_in("w_out_b", [L, 512, D]),
        "w_out": dram_in("w_out", [L, D, D]),
        "w_ff1": dram_in("w_ff1", [L, D, DFF]),
        "w_ff2": dram_in("w_ff2", [L, DFF, D]),
    }
    qkg = dram_in("qkg", [L, 640])
    lnv = {k: dram_in(k, [L, D]) for k in ("ln_mix_g", "ln_mix_b", "ln_ff_g", "ln_ff_b")}
    bt = dram_in("bt", [L, 5, 8, 128, 640])
    cstab = {S: dram_in(f"cs{S}", [S, 64]) for S in lens_set}

    ws = [nc.dram_tensor(f"ws{l}", [NU, 128, UNIT], BF16) for l in range(L)]
    ebs = nc.dram_tensor("ebs", [L, 5, 8, 128, 640], BF16)
    kbs = nc.dram_tensor("kbs", [128, 4, SMAX], BF16)
    vbs = nc.dram_tensor("vbs", [SMAX // 128, 128, 520], BF16)
    grow = nc.dram_tensor("grow", [L, NS, 2 * D], F32)

    es = ExitStack()
    with es:
        P = Prog(nc, es)
        P.annotate = annotate

        def sb(name, shape, dt):
            return es.enter_context(nc.sbuf_tensor(name, list(shape), dt))

        ident = sb("ident", [128, 128], BF16)
        negh = sb("negh", [128, 16], F32)
        MODS = sb("MODS", [128, L * NS * 4 * KC], F32)
        G12 = sb("G12", [128, 2, D], F32)
        LNV = sb("LNV", [128, 4, D], F32)
        QKG = sb("QKG", [128, 640], F32)
        CSr = sb("CSr", [128, 2, 64], F32)
        kAT = sb("kAT", [128, SMAX], BF16)
        vAug = sb("vAug", [128, SMAX // 128, 256], BF16)
        kBr = sb("kBr", [128, 4, 6, 128], BF16)
        vBr = sb("vBr", [128, 6, 520], BF16)
        EBr = sb("EBr", [128, 3, 640], BF16)
        wring = sb("wring", [128, NSLOT, UNIT], BF16)
        xt = sb("xt", [128, NT, D], F32)
        xn = sb("xn", [128, 4, D], BF16)
        xin = sb("xin", [128, 2, D], F32)
        uT = sb("uT", [128, KC, TB], BF16)
        WK = sb("WK", [128, 2, D], F32)
        st6 = sb("st6", [128, 8, 12], F32)
        mv = sb("mv", [128, 8, 4], F32)
        ssq = sb("ssq", [128, 4, 16], F32)
        RT = sb("RT", [128, 4, 256], F32)
        QR = sb("QR", [128, 2, 512], BF16)
        vBst = sb("vBst", [128, 2, 520], BF16)
        silc = sb("silc", [128, KC * NS], BF16)
        A_QAT, A_QBT, A_PA, A_PB, A_REC, A_AOT, A_OB, A_RECB, A_OBT, A_TA, A_TB_, A_T12, A_MIT = (
            0, 2048, 4096, 7168, 9088, 11136, 13184, 14208, 14272, 16320, 18368, 20416, 22464)
        A_END = 22464 + 4096
        A_HT, A_R = 0, 16384
        ARENA = max(A_END, A_R + 1024)
        arena = sb("arena", [128, ARENA], BF16)

        def abf(off, dims):
            return mk(arena, off, [[ARENA, 128]] + dims)

        banks = [es.enter_context(nc.psum_tensor(f"bank{i}", [128, 512], F32)) for i in range(8)]
        banks_bf = [b.bitcast(BF16) for b in banks]
        bank_r = [Region(f"bank{i}") for i in range(8)]
        free_banks = deque(range(8))

        def acq():
            assert free_banks, "out of PSUM banks"
            return free_banks.popleft()

        def rel(b):
            free_banks.append(b)

        R = {}

        def reg(name):
            if name not in R:
                R[name] = Region(name)
            return R[name]

        ch_x = [P.chan(f"chx{i}") for i in range(NT)]
        ch_w = [P.chan(f"chw{i}") for i in range(NSLOT)]
        ch_st = [P.chan(f"chst{i}") for i in range(4)]
        ch_kb = [P.chan(f"chkb{i}") for i in range(6)]
        ch_vb = [P.chan(f"chvb{i}") for i in range(6)]
        ch_eb = [P.chan(f"cheb{i}") for i in range(3)]
        ch_cs = [P.chan(f"chcs{i}") for i in range(2)]
        ch_misc = P.chan("chmisc")
        ch_xin = [P.chan(f"chxin{i}") for i in range(2)]
        ch_xinp = [P.chan(f"chxinp{i}") for i in range(2)]
        ch_miscp = P.chan("chmiscp")
        ch_wp = [P.chan(f"chwp{i}") for i in range(NSLOT)]
        ch_pre = [P.chan(f"chpre{l}") for l in range(L)]
        ch_kst = [P.chan(f"chkst{i}") for i in range(2)]
        st_rr = [0]

        def store_chan():
            c = ch_st[st_rr[0] % len(ch_st)]
            st_rr[0] += 1
            return c

        wslot_r = [Region(f"wslot{i}") for i in range(NSLOT)]
        wfree = deque(range(NSLOT))
        wsched = []
        wstate = {"next": 0, "ptr": 0, "loaded": {}}

        def wpump():
            while wstate["next"] < len(wsched) and wfree:
                l, u = wsched[wstate["next"]]
                slot = wfree.popleft()
                src = mk(ws[l], u * 128 * UNIT, [[UNIT, 128], [1, UNIT]])
                P.dma("sp", wring[:, slot, :], src, ch_w[slot],
                      reads=[reg(f"ws{l}")], writes=[wslot_r[slot]])
                wstate["loaded"][wstate["next"]] = slot
                wstate["next"] += 1

        def wget(l, name, idx):
            i = wstate["ptr"]
            assert wsched[i] == (l, UIDX[(name, idx)]), (wsched[i], l, name, idx)
            wpump()
            assert i in wstate["loaded"], "weight ring exhausted (would deadlock)"
            wstate["ptr"] += 1
            return wstate["loaded"][i]

        def wdone(slot):
            wfree.append(slot)
            wpump()

        def wap(slot, nk, ncols, k, c0, n):
            return mk(wring, slot * UNIT + k * ncols + c0, [[NSLOT * UNIT, 128], [1, n]])

        for s in range(NS):
            nb = seq_lens[s] // TB
            for l in range(L):
                for b in range(nb):
                    wsched += [(l, UIDX[("kvA", 0)]), (l, UIDX[("vB", 0)]), (l, UIDX[("vB", 1)]),
                               (l, UIDX[("kB", 0)]), (l, UIDX[("kB", 1)])]
                for b in range(nb):
                    seq = [("qA", 0), ("qA", 1), ("qB", 0), ("qB", 1)]
                    for g in range(2):
                        seq += [("ga", 2 * g), ("gb", 2 * g), ("oa", g), ("ob", g), ("ga", 2 * g + 1), ("gb", 2 * g + 1)]
                    seq += [("wo", i) for i in range(4)]
                    seq += [("f1", i) for i in range(16)]
                    seq += [("f2", i) for i in range(16)]
                    wsched += [(l, UIDX[k]) for k in seq]

        def ckpt(name):
            P.stage = name + P.suffix
            if stop_at == name:
                raise _Stop()

        def body():
            r_ident = reg("ident")
            P.op("pool", lambda e: e.memset(ident[:], 0.0), writes=[r_ident])
            P.op("pool", lambda e: e.affine_select(out=ident[:], in_=ident[:], pattern=[[-1, 128]],
                                                   compare_op=ALU.not_equal, fill=1.0, base=0,
                                                   channel_multiplier=1), reads=[r_ident], writes=[r_ident])
            P.op("pool", lambda e: e.memset(negh[:], -0.5), writes=[reg("negh")])
            P.op("pool", lambda e: e.memset(vAug[:], 1.0), writes=[reg("vAugAll")])
            P.op("pool", lambda e: e.memset(vBst[:], 1.0), writes=[reg("vBst0"), reg("vBst1")])

            def precast(l):
                for ui, (name, idx, src, k0, nk, c0, ncols) in enumerate(UNITS):
                    w = wsrc[src]
                    rows = w.shape[1]
                    cols = w.shape[2]
                    base = l * rows * cols + (k0 * 128) * cols + c0
                    sap = mk(w, base, [[cols, 128], [128 * cols, nk], [1, ncols]])
                    dap = mk(ws[l], ui * 128 * UNIT, [[UNIT, 128], [ncols, nk], [1, ncols]])
                    P.dma("pool", dap, sap, ch_pre[l], writes=[], serialize=False)
                reg(f"ws{l}").w = ch_pre[l].last

            ckpt('consts')
            precast(0)
            ckpt('precast0')

            ebreg = reg("ebs")
            ebtoks = []
            for l in range(L):
                for cfg in range(5):
                    for h in range(8):
                        k = (l * 5 + cfg) * 8 + h
                        slot = k % 2
                        wr = reg(f"WK{slot}")
                        er = reg(f"EBr{slot}")
                        off = ((l * 5 + cfg) * 8 + h) * 128 * 640
                        P.dma("sp", WK[:, slot, 0:640], mk(bt, off, [[640, 128], [1, 640]]), ch_x[slot], writes=[wr])
                        P.op("act", lambda e, slot=slot: e.activation(out=EBr[:, slot, :], in_=WK[:, slot, 0:640], func=AF.Exp),
                             reads=[wr], writes=[er])
                        t = P.dma("pool", mk(ebs, off, [[640, 128], [1, 640]]), EBr[:, slot, :], ch_st[slot], reads=[er])
                        ebtoks.append(t)
            ebreg.w = None
            ckpt('eb')
            eb_guard = [ch_st[0].last, ch_st[1].last]

            r_silc = reg("silc")
            P.dma("sp", WK[:, 0, 0:KC * NS], cT[:, :], ch_misc, writes=[reg("WK0")])
            P.op("act", lambda e: e.activation(out=silc[:], in_=WK[:, 0, 0:KC * NS], func=AF.Silu),
                 reads=[reg("WK0")], writes=[r_silc])
            r_bpm = reg("bpm")
            bpm = sb("bpm", [128, L * 48], F32)
            P.dma("sp", bpm[:], b_ada_pm[:, :], ch_misc, writes=[r_bpm])
            brow = mk(G12, 0, [[2 * D, NS], [1, 2 * D]])
            grow_sb = mk(LNV, 0, [[4 * D, NS], [1, 2 * D]])
            KIND_OF = {1: 0, 0: 1, 4: 2, 3: 3}
            for l in range(L):
                rb = reg("G12")
                P.dma("sp", brow[:, 0:D], mk(b_ada, l * 6 * D + 2 * D, [[0, NS], [1, D]]), ch_misc, writes=[rb])
                P.dma("sp", brow[:, D:2 * D], mk(b_ada, l * 6 * D + 5 * D, [[0, NS], [1, D]]), ch_misc, writes=[rb])
                for blk in range(6):
                    for half in range(2):
                        slots = []
                        for kh in range(2):
                            slot = wfree.popleft()
                            base = l * D * 6 * D + (kh * 4 * 128) * 6 * D + blk * D + half * 512
                            sap = mk(w_ada, base, [[6 * D, 128], [128 * 6 * D, 4], [1, 512]])
                            dap = mk(wring, slot * UNIT, [[NSLOT * UNIT, 128], [512, 4], [1, 512]])
                            P.dma("pool", dap, sap, ch_wp[slot], writes=[wslot_r[slot]])
                            slots.append(slot)
                        if blk in KIND_OF:
                            kind = KIND_OF[blk]
                            b = acq()
                            for cc in range(4):
                                for kc in range(KC):
                                    slot = slots[kc // 4]
                                    lhsT = mk(wring, slot * UNIT + (kc % 4) * 512 + cc * 128, [[NSLOT * UNIT, 128], [1, 128]])
                                    rhs = silc[:, kc * NS:(kc + 1) * NS]
                                    P.op("pe", lambda e, o=banks[b][:, cc * NS:(cc + 1) * NS], lhsT=lhsT, rhs=rhs, kc=kc:
                                         e.matmul(o, lhsT=lhsT, rhs=rhs, start=(kc == 0), stop=(kc == KC - 1)),
                                         reads=[wslot_r[slot], r_silc], writes=[bank_r[b]] if kc in (0, KC - 1) else [],
                                         signal=(kc == KC - 1))
                            for cc in range(4):
                                kc = half * 4 + cc
                                bcol = l * 48 + blk * 8 + kc
                                oap = mk(MODS, (l * NS * 4 + kind) * KC + kc, [[L * NS * 4 * KC, 128], [4 * KC, NS]])
                                addc = 1.0 if kind in (0, 2) else 0.0
                                P.op("dve", lambda e, oap=oap, i=banks[b][:, cc * NS:(cc + 1) * NS], sc=bpm[:, bcol:bcol + 1], addc=addc:
                                     e.tensor_scalar(out=oap, in0=i, scalar1=sc, scalar2=addc, op0=ALU.add, op1=ALU.add),
                                     reads=[bank_r[b], r_bpm], writes=[reg("MODS")])
                            rel(b)
                        else:
                            gi = 0 if blk == 2 else 1
                            b = acq()
                            for kc in range(KC):
                                slot = slots[kc // 4]
                                rhs = mk(wring, slot * UNIT + (kc % 4) * 512, [[NSLOT * UNIT, 128], [1, 512]])
                                lhsT = silc[:, kc * NS:(kc + 1) * NS]
                                P.op("pe", lambda e, o=banks[b][0:NS, :], lhsT=lhsT, rhs=rhs, kc=kc:
                                     e.matmul(o, lhsT=lhsT, rhs=rhs, start=(kc == 0), stop=(kc == KC - 1)),
                                     reads=[wslot_r[slot], r_silc], writes=[bank_r[b]] if kc in (0, KC - 1) else [],
                                     signal=(kc == KC - 1))
                            c0 = gi * D + half * 512
                            P.op("dve", lambda e, o=grow_sb[:, c0:c0 + 512], i=banks[b][0:NS, :], bb=brow[:, c0:c0 + 512]:
                                 e.tensor_tensor(out=o, in0=i, in1=bb, op=ALU.add),
                                 reads=[bank_r[b], rb], writes=[reg("LNV")])
                            P.op("dve", lambda e, o=grow_sb[:, c0:c0 + 512], gi=gi:
                                 e.tensor_scalar(out=o, in0=o, scalar1=1.0, scalar2=((0.5 if gi == 0 else 1.0) / ALPHA), op0=ALU.add, op1=ALU.mult),
                                 reads=[reg("LNV")], writes=[reg("LNV")])
                            rel(b)
                        for slot in slots:
                            wfree.append(slot)
                P.dma("pool", mk(grow, l * NS * 2 * D, [[2 * D, NS], [1, 2 * D]]), grow_sb[:, :], ch_miscp,
                      reads=[reg("LNV")], writes=[reg("grow")])

            ckpt('mods')
            for l in range(1, L):
                precast(l)
            ckpt('precast')

            def mods_ap(l, s, kind, kc):
                col = ((l * NS + s) * 4 + kind) * KC + kc
                return MODS[:, col:col + 1]

            def interleave(gens):
                gens = list(gens)
                while gens:
                    for g in list(gens):
                        try:
                            next(g)
                        except StopIteration:
                            gens.remove(g)

            def drain(g):
                for _ in g:
                    pass

            def ln_stats(src_ap, src_reg, j, eps=EPS):
                rs, rm = reg(f"st6_{j}"), reg(f"mv_{j}")
                P.op("dve", lambda e: e.bn_stats(out=st6[:, j, 0:6], in_=src_ap[:, 0:512]), reads=[src_reg], writes=[rs])
                yield
                P.op("dve", lambda e: e.bn_stats(out=st6[:, j, 6:12], in_=src_ap[:, 512:1024]), reads=[src_reg], writes=[rs])
                yield
                P.op("dve", lambda e: e.bn_aggr(out=mv[:, j, 0:2], in_=st6[:, j, :]), reads=[rs], writes=[rm])
                yield
                P.op("pool", lambda e: e.tensor_scalar(out=mv[:, j, 2:3], in0=mv[:, j, 1:2], scalar1=eps, scalar2=None, op0=ALU.add),
                     reads=[rm], writes=[rm])
                yield
                P.op("pool", lambda e: e.tensor_tensor(out=mv[:, j, 2:3], in0=mv[:, j, 2:3], in1=negh[:, 0:1], op=ALU.pow),
                     reads=[rm, reg("negh")], writes=[rm])
                yield
                P.op("dve", lambda e: e.scalar_tensor_tensor(out=mv[:, j, 3:4], in0=mv[:, j, 0:1], scalar=-1.0, in1=mv[:, j, 2:3],
                                                             op0=ALU.mult, op1=ALU.mult), reads=[rm], writes=[rm])
                yield

            def ln_to_uT(tiles_ap, tiles_reg, l, s, kind_sc, kind_sh, loader=None, sbase=0, pair_mode=False, part=0):
                nsl = 4
                rxs = [reg(f"xn{t % nsl}") for t in range(NT)]

                def ln_tile(t):
                    rm = reg(f"mv_{t + sbase}")
                    yield from ln_stats(tiles_ap[t], tiles_reg[t], t + sbase)
                    sl = t % nsl
                    P.op("act", lambda e, t=t, sl=sl: e.activation(out=xn[:, sl, :], in_=tiles_ap[t], func=AF.Identity,
                                                                   scale=mv[:, t + sbase, 2:3], bias=mv[:, t + sbase, 3:4]),
                         reads=[tiles_reg[t], rm], writes=[rxs[t]])
                    yield

                if part in (0, 1):
                    for pair in ((0, 1), (2, 3)):
                        if loader is not None:
                            for t in pair:
                                loader(t)
                        interleave([ln_tile(t) for t in pair])
                if part == 1:
                    return
                groups = [[0, 1], [2, 3]] if pair_mode else [[0, 1, 2, 3]]
                for grp in groups:
                    ng = len(grp)
                    w = ng * 128
                    per_bank = 1024 // w
                    nb_ = KC // per_bank
                    tb = [acq() for _ in range(nb_)]
                    for gi_, t in enumerate(grp):
                        sl = t % nsl
                        for kc in range(KC):
                            oap = banks_bf[tb[kc // per_bank]][:, (kc % per_bank) * w + gi_ * 128:(kc % per_bank) * w + (gi_ + 1) * 128]
                            last = (kc == KC - 1)
                            P.op("pe", lambda e, oap=oap, i=xn[:, sl, kc * 128:(kc + 1) * 128]: e.transpose(oap, i, ident[:]),
                                 reads=[rxs[t], r_ident], writes=[bank_r[x] for x in tb] if (last or kc == 0) else [], signal=last)
                    tok0 = grp[0] * 128
                    for kc in range(KC):
                        iap = banks_bf[tb[kc // per_bank]][:, (kc % per_bank) * w:(kc % per_bank) * w + w]
                        sc, sh = mods_ap(l, s, kind_sc, kc), mods_ap(l, s, kind_sh, kc)
                        P.op("act", lambda e, kc=kc, iap=iap, sc=sc, sh=sh: e.activation(out=uT[:, kc, tok0:tok0 + w], in_=iap, func=AF.Identity, scale=sc, bias=sh),
                             reads=[bank_r[tb[kc // per_bank]], reg("MODS")], writes=[reg(f"uT{kc}")])
                    for b in tb:
                        rel(b)

            uT_regs = [reg(f"uT{kc}") for kc in range(KC)]

            def proj_tok(slots, nk_per, ncols, c0, n, tile, b, col0=0):
                for kc in range(KC):
                    slot = slots[kc // nk_per]
                    rhs = wap(slot, nk_per, ncols, kc % nk_per, c0, n)
                    lhsT = uT[:, kc, tile * 128:(tile + 1) * 128]
                    P.op("pe", lambda e, o=banks[b][:, col0:col0 + n], lhsT=lhsT, rhs=rhs, kc=kc:
                         e.matmul(o, lhsT=lhsT, rhs=rhs, start=(kc == 0), stop=(kc == KC - 1)),
                         reads=[wslot_r[slot], uT_regs[kc]], writes=[bank_r[b]] if kc in (0, KC - 1) else [],
                         signal=(kc == KC - 1))

            def proj_feat(slot, cc, b, rhs_fn, rhs_regs, nk=KC, ncols=256):
                for kc in range(nk):
                    lhsT = wap(slot, nk, ncols, kc, cc * 128, 128)
                    P.op("pe", lambda e, o=banks[b][:, :], lhsT=lhsT, rhs=rhs_fn(kc), kc=kc:
                         e.matmul(o, lhsT=lhsT, rhs=rhs, start=(kc == 0), stop=(kc == nk - 1)),
                         reads=[wslot_r[slot], rhs_regs[kc]], writes=[bank_r[b]] if kc in (0, nk - 1) else [],
                         signal=(kc == nk - 1))

            def qk_norm_rope(b, ncols, nh, gcol0, cs_slot, cs_reg, out_ap, out_reg, k, arena_out=False):
                wr = reg(f"WK{k}")
                ps = banks[b][:, 0:ncols]
                SQ0 = k * D
                QN0 = k * D + 512
                P.op("act", lambda e: e.activation(out=WK[:, k, 0:ncols], in_=ps, func=AF.Square), reads=[bank_r[b]], writes=[wr])
                yield
                rs = reg(f"ssq{k}")
                P.op("dve", lambda e: e.tensor_reduce(out=ssq[:, 2 * k, 0:nh], in_=mk(WK, SQ0, [[2 * D, 128], [64, nh], [1, 64]]), axis=AX.X, op=ALU.add),
                     reads=[wr], writes=[rs])
                yield
                P.op("pool", lambda e: e.tensor_scalar(out=ssq[:, 2 * k + 1, 0:nh], in0=ssq[:, 2 * k, 0:nh], scalar1=1.0 / 64.0, scalar2=EPS, op0=ALU.mult, op1=ALU.add),
                     reads=[rs], writes=[rs])
                yield
                P.op("pool", lambda e: e.tensor_tensor(out=ssq[:, 2 * k + 1, 0:nh], in0=ssq[:, 2 * k + 1, 0:nh], in1=negh[:, 0:nh], op=ALU.pow),
                     reads=[rs, reg("negh")], writes=[rs])
                yield
                wq = reg(f"WKq{k}")
                P.op("dve", lambda e: e.tensor_tensor(out=mk(WK, QN0, [[2 * D, 128], [64, nh], [1, 64]]),
                                                      in0=mk(banks[b], 0, [[512, 128], [64, nh], [1, 64]]),
                                                      in1=mk(ssq, (2 * k + 1) * 16, [[64, 128], [1, nh], [0, 64]]), op=ALU.mult),
                     reads=[bank_r[b], rs], writes=[wq], extra=[wr.w])
                yield
                P.op("dve", lambda e: e.tensor_tensor(out=WK[:, k, 512:512 + ncols], in0=WK[:, k, 512:512 + ncols], in1=QKG[:, gcol0:gcol0 + ncols], op=ALU.mult),
                     reads=[wq, reg("QKG")], writes=[wq])
                yield
                ev = mk(WK, QN0, [[2 * D, 128], [64, nh], [1, 32]])
                od = mk(WK, QN0 + 32, [[2 * D, 128], [64, nh], [1, 32]])
                Cb = mk(CSr, cs_slot * 64, [[128, 128], [0, nh], [1, 32]])
                Sb = mk(CSr, cs_slot * 64 + 32, [[128, 128], [0, nh], [1, 32]])
                T1 = mk(RT, (2 * k) * 256, [[1024, 128], [32, nh], [1, 32]])
                T2 = mk(RT, (2 * k + 1) * 256, [[1024, 128], [32, nh], [1, 32]])
                oe = mk(out_ap.tensor, out_ap.offset, [[out_ap.ap[0][0], 128], [64, nh], [1, 32]])
                oo = mk(out_ap.tensor, out_ap.offset + 32, [[out_ap.ap[0][0], 128], [64, nh], [1, 32]])
                r1, r2 = reg(f"RT{k}a"), reg(f"RT{k}b")
                P.op("dve", lambda e: e.tensor_tensor(out=T1, in0=ev, in1=Cb, op=ALU.mult), reads=[wq, cs_reg], writes=[r1])
                yield
                P.op("dve", lambda e: e.tensor_tensor(out=T2, in0=od, in1=Sb, op=ALU.mult), reads=[wq, cs_reg], writes=[r2])
                yield
                P.op("dve", lambda e: e.tensor_tensor(out=oe, in0=T1, in1=T2, op=ALU.subtract), reads=[r1, r2], writes=[out_reg], arena=arena_out)
                yield
                P.op("dve", lambda e: e.tensor_tensor(out=T1, in0=ev, in1=Sb, op=ALU.mult), reads=[wq, cs_reg], writes=[r1])
                yield
                P.op("dve", lambda e: e.tensor_tensor(out=T2, in0=od, in1=Cb, op=ALU.mult), reads=[wq, cs_reg], writes=[r2])
                yield
                P.op("dve", lambda e: e.tensor_tensor(out=oo, in0=T1, in1=T2, op=ALU.add), reads=[r1, r2], writes=[out_reg], arena=arena_out)
                yield

            def post_ln(t, half_banks, gi, lg, lb, wk):
                rw = reg(f"WK{wk}")
                rx = reg(f"xt{t}")
                for h in range(2):
                    P.op("dve", lambda e, h=h: e.tensor_tensor(out=WK[:, wk, h * 512:(h + 1) * 512], in0=banks[half_banks[h]][:, :],
                                                              in1=G12[:, gi, h * 512:(h + 1) * 512], op=ALU.mult),
                         reads=[bank_r[half_banks[h]], reg("G12")], writes=[rw])
                    yield
                P.op("dve", lambda e: e.tensor_tensor(out=WK[:, wk, :], in0=WK[:, wk, :], in1=xt[:, t, :], op=ALU.add),
                     reads=[rx, rw], writes=[rw])
                yield
                rm = reg(f"mv_{t}")
                yield from ln_stats(WK[:, wk, :], rw, t, eps=EPS / (ALPHA * ALPHA))
                P.op("act", lambda e: e.activation(out=WK[:, wk, :], in_=WK[:, wk, :], func=AF.Identity, scale=mv[:, t, 2:3], bias=mv[:, t, 3:4]),
                     reads=[rw, rm], writes=[rw])
                yield
                eng2 = "dve" if t % 2 == 0 else "pool"
                P.op(eng2, lambda e: e.tensor_tensor(out=WK[:, wk, :], in0=WK[:, wk, :], in1=LNV[:, lg, :], op=ALU.mult),
                     reads=[rw, reg("LNV")], writes=[rw])
                yield
                P.op(eng2, lambda e: e.tensor_tensor(out=xt[:, t, :], in0=WK[:, wk, :], in1=LNV[:, lb, :], op=ALU.add),
                     reads=[rw, reg("LNV")], writes=[rx])
                yield

            xt_regs = [reg(f"xt{t}") for t in range(NT)]
            xt_aps = [xt[:, t, :] for t in range(NT)]

            def load_x(src, s, blk):
                for t in range(NT):
                    row0 = blk * TB + t * 128
                    P.dma("sp", xt[:, t, :], mk(src, row0 * D, [[D, 128], [1, D]]), ch_x[t],
                          reads=[reg(f"y{s}_{blk * NT + t}")], writes=[xt_regs[t]])

            xin_aps = [xin[:, t % 2, :] for t in range(NT)]
            xin_regs = [reg(f"xin{t % 2}") for t in range(NT)]

            def stage_A(src, s, l, blk, pair_mode=False, part=0):
                def loader(t):
                    row0 = blk * TB + t * 128
                    P.dma("pool", xin[:, t % 2, :], mk(src, row0 * D, [[D, 128], [1, D]]), ch_xinp[t % 2],
                          reads=[reg(f"y{s}_{blk * NT + t}")], writes=[xin_regs[t]])
                ln_to_uT(xin_aps, xin_regs, l, s, 0, 1, loader=loader, sbase=4, pair_mode=pair_mode, part=part)

            def load_cs(S, tile, slot):
                P.dma("sp", CSr[:, slot, :], mk(cstab[S], tile * 128 * 64, [[64, 128], [1, 64]]), ch_cs[slot], writes=[reg(f"CS{slot}")])
                return reg(f"CS{slot}")

            cs_ctr = [0]

            first_layer_loaded = [False]
            for s in range(NS):
                S = seq_lens[s]
                NB = S // TB
                NTIL = S // 128
                for l in range(L):
                    src = xs[s] if l == 0 else ys[s]
                    misc_extra = eb_guard if not first_layer_loaded[0] else []
                    first_layer_loaded[0] = True
                    P.dma("sp", G12[:, 0, :], mk(grow, (l * NS + s) * 2 * D, [[0, 128], [1, D]]), ch_misc, reads=[reg("grow")], writes=[reg("G12")], extra=misc_extra)
                    P.dma("sp", G12[:, 1, :], mk(grow, (l * NS + s) * 2 * D + D, [[0, 128], [1, D]]), ch_misc, reads=[reg("grow")], writes=[reg("G12")])
                    for i, k in enumerate(("ln_mix_g", "ln_mix_b", "ln_ff_g", "ln_ff_b")):
                        P.dma("sp", LNV[:, i, :], mk(lnv[k], l * D, [[0, 128], [1, D]]), ch_misc, writes=[reg("LNV")])
                    P.dma("sp", QKG[:, :], mk(qkg, l * 640, [[0, 128], [1, 640]]), ch_misc, writes=[reg("QKG")])

                    P.switch_mode()
                    for blk in range(NB):
                        ckpt('p1x')
                        if blk == 0:
                            stage_A(src, s, l, blk, part=1)
                        stage_A(src, s, l, blk, part=2)
                        if blk + 1 < NB:
                            stage_A(src, s, l, blk + 1, part=1)
                        ckpt('p1a')
                        s_kv = wget(l, "kvA", 0)
                        s_vb = [wget(l, "vB", 0), wget(l, "vB", 1)]
                        for t in range(NT):
                            tile = blk * NT + t
                            b = acq()
                            proj_tok([s_kv], 8, 256, 0, 256, t, b)
                            csl = cs_ctr[0] % 2
                            cs_ctr[0] += 1
                            csreg = load_cs(S, tile, csl)
                            rv = reg(f"vAug{tile}")
                            P.op("act", lambda e, b=b, tile=tile: e.activation(
                                out=mk(vAug, tile * 256, [[(SMAX // 128) * 256, 128], [192, 2], [1, 64]]),
                                in_=mk(banks[b], 128, [[512, 128], [64, 2], [1, 64]]), func=AF.Copy),
                                reads=[bank_r[b]], writes=[rv], extra=[reg("vAugAll").w])
                            ckpt('p1b')
                            kq = tile % 2
                            rqr = reg(f"QR{kq}")
                            drain(qk_norm_rope(b, 128, 2, 512, csl, csreg, QR[:, kq, 0:128], rqr, kq))
                            rel(b)
                            b2 = acq()
                            P.op("pe", lambda e, b2=b2: e.transpose(banks_bf[b2][:, 0:128], QR[:, kq, 0:128], ident[:]),
                                 reads=[rqr, r_ident], writes=[bank_r[b2]])
                            P.op("dve", lambda e, b2=b2, tile=tile: e.tensor_copy(out=kAT[:, tile * 128:(tile + 1) * 128], in_=banks_bf[b2][:, 0:128]),
                                 reads=[bank_r[b2]], writes=[reg(f"kAT{tile}")])
                            rel(b2)
                            ckpt('p1c')
                            b3 = acq()
                            proj_tok(s_vb, 4, 512, 0, 512, t, b3)
                            vsl = tile % 2
                            rvs = reg(f"vBst{vsl}")
                            P.op("act", lambda e, b3=b3, vsl=vsl: e.activation(
                                out=mk(vBst, vsl * 520, [[2 * 520, 128], [65, 8], [1, 64]]),
                                in_=mk(banks[b3], 0, [[512, 128], [64, 8], [1, 64]]), func=AF.Copy),
                                reads=[bank_r[b3]], writes=[rvs])
                            rel(b3)
                            P.dma("pool", mk(vbs, tile * 128 * 520, [[520, 128], [1, 520]]), vBst[:, vsl, :], ch_kst[vsl],
                                  reads=[rvs], writes=[reg(f"vbs{tile}")])
                        wdone(s_kv)
                        for x in s_vb:
                            wdone(x)
                        ckpt('p1d')
                        rks = reg("kBst")
                        for i in range(2):
                            sl = wget(l, "kB", i)
                            for cc in range(2):
                                hp = 2 * i + cc
                                b = acq()
                                proj_feat(sl, cc, b, lambda kc: uT[:, kc, :], uT_regs)
                                P.op("dve", lambda e, b=b, hp=hp: e.tensor_copy(out=abf(hp * TB, [[1, TB]]), in_=banks[b][:, :]),
                                     reads=[bank_r[b]], writes=[rks], arena=True)
                                rel(b)
                            wdone(sl)
                        P.dma("pool", mk(kbs, blk * TB, [[4 * SMAX, 128], [SMAX, 4], [1, TB]]), abf(0, [[TB, 4], [1, TB]]), store_chan(),
                              reads=[rks], writes=[reg(f"kbs{blk}")], arena=True)

                    ckpt('p1')
                    kb_slot_of = {}
                    kb_state = {"next": 0}

                    def na_window(T):
                        return min(max(T - 2, 0), NTIL - 5)

                    def ensure_kv(upto):
                        while kb_state["next"] <= upto:
                            c = kb_state["next"]
                            slot = c % 6
                            P.dma("sp", kBr[:, :, slot, :], mk(kbs, c * 128, [[4 * SMAX, 128], [SMAX, 4], [1, 128]]), ch_kb[slot],
                                  reads=[reg(f"kbs{c // NT}")], writes=[reg(f"kBr{slot}")])
                            P.dma("sp", vBr[:, slot, :], mk(vbs, c * 128 * 520, [[520, 128], [1, 520]]), ch_vb[slot],
                                  reads=[reg(f"vbs{c}")], writes=[reg(f"vBr{slot}")])
                            kb_state["next"] += 1

                    eb_ctr = [0]
                    for blk in range(NB):
                        P.suffix = f'@{blk}'
                        ckpt('S')
                        if blk == 0:
                            stage_A(src, s, l, blk)
                        P.switch_mode()
                        load_x(src, s, blk)
                        ckpt('A')
                        s_qa = [wget(l, "qA", 0), wget(l, "qA", 1)]
                        qbanks = []
                        for t in range(NT):
                            b = acq()
                            proj_tok(s_qa, 4, 512, 0, 512, t, b)
                            qbanks.append(b)
                        for x in s_qa:
                            wdone(x)
                        for i in range(2):
                            sl = wget(l, "qB", i)
                            for cc in range(2):
                                hp = 2 * i + cc
                                b = acq()
                                proj_feat(sl, cc, b, lambda kc: uT[:, kc, :], uT_regs)
                                P.op("dve", lambda e, b=b, hp=hp: e.tensor_copy(out=abf(A_QBT + hp * 512, [[1, 512]]), in_=banks[b][:, :]),
                                     reads=[bank_r[b]], writes=[reg(f"qBT{hp}")], arena=True)
                                rel(b)
                            wdone(sl)
                        for pair in ((0, 1), (2, 3)):
                            gens = []
                            for t in pair:
                                tile = blk * NT + t
                                csl = cs_ctr[0] % 2
                                cs_ctr[0] += 1
                                csreg = load_cs(S, tile, csl)
                                kq = t % 2
                                gens.append(qk_norm_rope(qbanks[t], 512, 8, 0, csl, csreg, QR[:, kq, 0:512], reg(f"QR{kq}"), kq))
                            interleave(gens)
                            for t in pair:
                                kq = t % 2
                                rqr = reg(f"QR{kq}")
                                rel(qbanks[t])
                                b2 = acq()
                                for c in range(4):
                                    P.op("pe", lambda e, b2=b2, c=c: e.transpose(banks_bf[b2][:, c * 128:(c + 1) * 128], QR[:, kq, c * 128:(c + 1) * 128], ident[:]),
                                         reads=[rqr, r_ident], writes=[bank_r[b2]] if c in (0, 3) else [], signal=(c == 3))
                                P.op("act", lambda e, b2=b2, t=t: e.activation(
                                    out=abf(A_QAT + t * 128, [[512, 4], [1, 128]]),
                                    in_=mk(banks_bf[b2], 0, [[1024, 128], [128, 4], [1, 128]]), func=AF.Copy),
                                    reads=[bank_r[b2]], writes=[reg(f"qAT{c}") for c in range(4)], arena=True)
                                rel(b2)
                        ckpt('NA')
                        na_st = {}
                        na_obk = {}
                        na_units = [(t, h) for t in range(NT) for h in range(8)]

                        def na_qk(t, h, u):
                            T = blk * NT + t
                            c0 = na_window(T)
                            cfg = 0 if T == 0 else 1 if T == 1 else 3 if T == NTIL - 2 else 4 if T == NTIL - 1 else 2
                            if h == 0:
                                ensure_kv(c0 + 4)
                            hp, side = h // 2, h % 2
                            esl = eb_ctr[0] % 3
                            eb_ctr[0] += 1
                            off = ((l * 5 + cfg) * 8 + h) * 128 * 640
                            P.dma("sp", EBr[:, esl, :], mk(ebs, off, [[640, 128], [1, 640]]), ch_eb[esl], writes=[reg(f"EBr{esl}")])
                            sa, sb2 = acq(), acq()
                            rhs = mk(arena, side * 64 * ARENA + A_QBT + hp * 512 + t * 128, [[ARENA, 64], [1, 128]])
                            for j in range(5):
                                slot = (c0 + j) % 6
                                lhsT = mk(kBr, side * 64 * (4 * 6 * 128) + (hp * 6 + slot) * 128, [[4 * 6 * 128, 64], [1, 128]])
                                o = banks[sa][:, j * 128:(j + 1) * 128] if j < 4 else banks[sb2][:, 0:128]
                                P.op("pe", lambda e, o=o, lhsT=lhsT, rhs=rhs: e.matmul(o, lhsT=lhsT, rhs=rhs, start=True, stop=True),
                                     reads=[reg(f"kBr{slot}"), reg(f"qBT{hp}")],
                                     writes=[bank_r[sa]] if j in (0, 3) else [bank_r[sb2]] if j == 4 else [],
                                     signal=(j >= 3), arena=True)
                            psl = u % 3
                            rp = reg(f"pB{psl}")
                            P.op("act", lambda e: e.activation(out=abf(A_PB + psl * 640, [[1, 512]]), in_=banks[sa][:, :], func=AF.Exp, scale=0.125),
                                 reads=[bank_r[sa]], writes=[rp], arena=True)
                            P.op("act", lambda e: e.activation(out=abf(A_PB + psl * 640 + 512, [[1, 128]]), in_=banks[sb2][:, 0:128], func=AF.Exp, scale=0.125),
                                 reads=[bank_r[sb2]], writes=[rp], arena=True)
                            rel(sa)
                            rel(sb2)
                            P.op("dve", lambda e: e.tensor_tensor(out=abf(A_PB + psl * 640, [[1, 640]]), in0=abf(A_PB + psl * 640, [[1, 640]]),
                                                                  in1=EBr[:, esl, :], op=ALU.mult),
                                 reads=[rp, reg(f"EBr{esl}")], writes=[rp], arena=True)
                            na_st[(t, h)] = (psl, c0)

                        def na_pv(t, h):
                            psl, c0 = na_st[(t, h)]
                            rp = reg(f"pB{psl}")
                            if h == 0:
                                na_obk[t] = [acq(), acq()]
                            obk = na_obk[t]
                            bk = obk[h // 4]
                            for j in range(5):
                                slot = (c0 + j) % 6
                                lhsT = abf(A_PB + psl * 640 + j * 128, [[1, 128]])
                                rhs2 = mk(vBr, slot * 520 + h * 65, [[6 * 520, 128], [1, 65]])
                                o = banks[bk][:, (h % 4) * 65:(h % 4) * 65 + 65]
                                P.op("pe", lambda e, o=o, lhsT=lhsT, rhs2=rhs2, j=j: e.matmul(o, lhsT=lhsT, rhs=rhs2, start=(j == 0), stop=(j == 4)),
                                     reads=[rp, reg(f"vBr{slot}")], writes=[bank_r[bk]] if j in (0, 4) else [], signal=(j == 4), arena=True)
                            if h != 7:
                                return
                            osl = t % 2
                            rob = reg(f"ob{osl}")
                            for g in range(2):
                                bk = obk[g]
                                rb_ap = mk(arena, A_RECB + osl * 32 + g * 8, [[ARENA, 128], [1, 8]]).bitcast(F32)
                                P.op("dve", lambda e, bk=bk, rb_ap=rb_ap: e.reciprocal(out=rb_ap, in_=mk(banks[bk], 64, [[512, 128], [65, 4]])),
                                     reads=[bank_r[bk]], writes=[reg(f"recB{osl}")], arena=True)
                                P.op("dve", lambda e, bk=bk, rb_ap=rb_ap, g=g, osl=osl: e.tensor_tensor(
                                    out=abf(A_OB + osl * 512 + g * 256, [[64, 4], [1, 64]]),
                                    in0=mk(banks[bk], 0, [[512, 128], [65, 4], [1, 64]]),
                                    in1=mk(rb_ap.tensor, rb_ap.offset, [[rb_ap.ap[0][0], 128], [1, 4], [0, 64]]), op=ALU.mult),
                                    reads=[bank_r[bk], reg(f"recB{osl}")], writes=[rob], arena=True)
                                rel(bk)
                            b2 = acq()
                            for hp in range(4):
                                P.op("pe", lambda e, b2=b2, hp=hp, osl=osl: e.transpose(banks_bf[b2][:, hp * 128:(hp + 1) * 128],
                                                                                        abf(A_OB + osl * 512 + hp * 128, [[1, 128]]), ident[:]),
                                     reads=[rob, r_ident], writes=[bank_r[b2]] if hp in (0, 3) else [], signal=(hp == 3), arena=True)
                            P.op("dve", lambda e, b2=b2, t=t: e.tensor_copy(out=abf(A_OBT + t * 128, [[512, 4], [1, 128]]),
                                                                            in_=mk(banks_bf[b2], 0, [[1024, 128], [128, 4], [1, 128]])),
                                 reads=[bank_r[b2]], writes=[reg(f"obT{hp}") for hp in range(4)], arena=True)
                            rel(b2)

                        na_pend = []
                        for u in range(len(na_units)):
                            na_qk(na_units[u][0], na_units[u][1], u)
                            if len(na_pend) == 2:
                                na_pv(*na_pend.pop(0))
                            na_pend.append(na_units[u])
                        while na_pend:
                            na_pv(*na_pend.pop(0))

                        ckpt('GA')
                        NK = NTIL
                        for c in range(4):
                            ob_ = [acq(), acq()]
                            prev = None
                            pa_ctr = 0

                            def pv(prev):
                                kt, slots_ = prev
                                for side in range(2):
                                    lhsT = mk(vAug, kt * 256 + side * 128, [[(SMAX // 128) * 256, 128], [1, 128]])
                                    rhs = abf(A_PA + slots_[side] * 512, [[1, 512]])
                                    last = (kt == NK - 1)
                                    P.op("pe", lambda e, o=banks[ob_[side]][:, :], lhsT=lhsT, rhs=rhs, kt=kt:
                                         e.matmul(o, lhsT=lhsT, rhs=rhs, start=(kt == 0), stop=(kt == NK - 1)),
                                         reads=[reg(f"vAug{kt}"), reg(f"pA{slots_[side]}")],
                                         writes=[bank_r[ob_[side]]] if (kt == 0 or last) else [], signal=True, arena=True)

                            pend = []
                            for kt in range(NK):
                                sb_ = [acq(), acq()]
                                slots_ = [(kt % 3) * 2, (kt % 3) * 2 + 1]
                                for side in range(2):
                                    lhsT = kAT[side * 64:(side + 1) * 64, kt * 128:(kt + 1) * 128]
                                    rhs = mk(arena, side * 64 * ARENA + A_QAT + c * 512, [[ARENA, 64], [1, 512]])
                                    P.op("pe", lambda e, o=banks[sb_[side]][:, :], lhsT=lhsT, rhs=rhs:
                                         e.matmul(o, lhsT=lhsT, rhs=rhs, start=True, stop=True),
                                         reads=[reg(f"kAT{kt}"), reg(f"qAT{c}")], writes=[bank_r[sb_[side]]], arena=True)
                                if len(pend) == 2:
                                    pv(pend.pop(0))
                                for side in range(2):
                                    P.op("act", lambda e, i=banks[sb_[side]][:, :], o=abf(A_PA + slots_[side] * 512, [[1, 512]]):
                                         e.activation(out=o, in_=i, func=AF.Exp, scale=0.125),
                                         reads=[bank_r[sb_[side]]], writes=[reg(f"pA{slots_[side]}")], arena=True)
                                    rel(sb_[side])
                                pend.append((kt, slots_))
                            while pend:
                                pv(pend.pop(0))
                            for side in range(2):
                                b = ob_[side]
                                lo, hi = (0, 64) if side == 0 else (64, 128)
                                dlo, dhi = (64, 128) if side == 0 else (0, 64)
                                rr = reg(f"rec{side}")
                                rec_f = mk(arena, lo * ARENA + A_REC + side * 1024, [[ARENA, 64], [1, 1024]]).bitcast(F32)
                                P.op("dve", lambda e, b=b, rec_f=rec_f, dlo=dlo, dhi=dhi: e.reciprocal(out=rec_f, in_=banks[b][dlo:dhi, :]),
                                     reads=[bank_r[b]], writes=[rr], arena=True)
                                P.op("dve", lambda e, b=b, rec_f=rec_f, lo=lo, hi=hi, c=c:
                                     e.tensor_tensor(out=mk(arena, lo * ARENA + A_AOT + c * 512, [[ARENA, 64], [1, 512]]),
                                                     in0=banks[b][lo:hi, :], in1=rec_f, op=ALU.mult),
                                     reads=[bank_r[b], rr], writes=[reg(f"aoT{c}")], arena=True)
                                rel(b)

                        ckpt('D')
                        for g in range(2):
                            sl_ga = [None, None]
                            sl_gb = [None, None]
                            sl_ga[0] = wget(l, "ga", 2 * g)
                            sl_gb[0] = wget(l, "gb", 2 * g)
                            sl_oa = wget(l, "oa", g)
                            sl_ob = wget(l, "ob", g)
                            for q in range(4):
                                n = 4 * g + q
                                if q == 2:
                                    sl_ga[1] = wget(l, "ga", 2 * g + 1)
                                    sl_gb[1] = wget(l, "gb", 2 * g + 1)
                                bga, bgb, boa, bob = acq(), acq(), acq(), acq()
                                proj_feat(sl_ga[q // 2], q % 2, bga, lambda kc: uT[:, kc, :], uT_regs)
                                proj_feat(sl_gb[q // 2], q % 2, bgb, lambda kc: uT[:, kc, :], uT_regs)
                                proj_feat(sl_oa, q, boa, lambda kc: abf(A_AOT + kc * 512, [[1, 512]]), [reg(f"aoT{c}") for c in range(4)], nk=4, ncols=512)
                                proj_feat(sl_ob, q, bob, lambda kc: abf(A_OBT + kc * 512, [[1, 512]]), [reg(f"obT{c}") for c in range(4)], nk=4, ncols=512)
                                if q == 1:
                                    wdone(sl_ga[0])
                                    wdone(sl_gb[0])
                                ta = abf(A_TA + q * 512, [[1, 512]])
                                tb_ = abf(A_TB_ + q * 512, [[1, 512]])
                                P.op("act", lambda e, ta=ta, bga=bga: e.activation(out=ta, in_=banks[bga][:, :], func=AF.Tanh, scale=0.5),
                                     reads=[bank_r[bga]], writes=[reg(f"ta{q}")], arena=True)
                                P.op("act", lambda e, tb_=tb_, bgb=bgb: e.activation(out=tb_, in_=banks[bgb][:, :], func=AF.Tanh, scale=0.5),
                                     reads=[bank_r[bgb]], writes=[reg(f"tb{q}")], arena=True)
                                rel(bga)
                                rel(bgb)
                                t1 = abf(A_T12, [[1, 1024]]).bitcast(F32)
                                t2 = abf(A_T12 + 1024, [[1, 1024]]).bitcast(F32)
                                P.op("dve", lambda e, t1=t1, ta=ta, boa=boa: e.scalar_tensor_tensor(out=t1, in0=ta, scalar=1.0, in1=banks[boa][:, :], op0=ALU.add, op1=ALU.mult),
                                     reads=[reg(f"ta{q}"), bank_r[boa]], writes=[reg("t1")], arena=True)
                                P.op("dve", lambda e, t2=t2, tb_=tb_, bob=bob: e.scalar_tensor_tensor(out=t2, in0=tb_, scalar=1.0, in1=banks[bob][:, :], op0=ALU.add, op1=ALU.mult),
                                     reads=[reg(f"tb{q}"), bank_r[bob]], writes=[reg("t2")], arena=True)
                                rel(boa)
                                rel(bob)
                                P.op("pool", lambda e, t1=t1, t2=t2, n=n: e.tensor_tensor(out=abf(A_MIT + n * 512, [[1, 512]]), in0=t1, in1=t2, op=ALU.add),
                                     reads=[reg("t1"), reg("t2")], writes=[reg(f"miT{n}")], arena=True)
                            wdone(sl_ga[1])
                            wdone(sl_gb[1])
                            wdone(sl_oa)
                            wdone(sl_ob)

                        ckpt('E')
                        halves = []
                        for hf in range(2):
                            sl = [wget(l, "wo", 2 * hf), wget(l, "wo", 2 * hf + 1)]
                            bs = [acq() for _ in range(NT)]
                            for t in range(NT):
                                for kc in range(KC):
                                    slot = sl[kc // 4]
                                    rhs = wap(slot, 4, 512, kc % 4, 0, 512)
                                    lhsT = abf(A_MIT + kc * 512 + t * 128, [[1, 128]])
                                    P.op("pe", lambda e, o=banks[bs[t]][:, :], lhsT=lhsT, rhs=rhs, kc=kc: e.matmul(o, lhsT=lhsT, rhs=rhs, start=(kc == 0), stop=(kc == KC - 1)),
                                         reads=[wslot_r[slot], reg(f"miT{kc}")], writes=[bank_r[bs[t]]] if kc in (0, KC - 1) else [], signal=(kc == KC - 1), arena=True)
                            for x in sl:
                                wdone(x)
                            halves.append(bs)
                        for pair in ((0, 1), (2, 3)):
                            interleave([post_ln(t, [halves[0][t], halves[1][t]], 0, 0, 1, t % 2) for t in pair])
                            for t in pair:
                                rel(halves[0][t])
                                rel(halves[1][t])

                        ckpt('F')
                        ln_to_uT(xt_aps, xt_regs, l, s, 2, 3)
                        P.switch_mode()
                        if blk + 1 < NB:
                            P.suffix = f'@{blk + 1}'
                            ckpt('S')
                            stage_A(src, s, l, blk + 1, part=1)
                            P.suffix = f'@{blk}'
                        ckpt('G')
                        for i in range(16):
                            sl = wget(l, "f1", i)
                            for cc in range(2):
                                f = 2 * i + cc
                                b = acq()
                                proj_feat(sl, cc, b, lambda kc: uT[:, kc, :], uT_regs)
                                rs_ = f % 2
                                rr = reg(f"r{rs_}")
                                P.op("act", lambda e, b=b, rs_=rs_: e.activation(out=abf(A_R + rs_ * 512, [[1, 512]]), in_=banks[b][:, :], func=AF.Relu),
                                     reads=[bank_r[b]], writes=[rr], arena=True)
                                rel(b)
                                P.op("pool", lambda e, rs_=rs_, f=f: e.tensor_tensor(out=abf(A_HT + f * 512, [[1, 512]]), in0=abf(A_R + rs_ * 512, [[1, 512]]),
                                                                                     in1=abf(A_R + rs_ * 512, [[1, 512]]), op=ALU.mult),
                                     reads=[rr], writes=[reg(f"hT{f}")], arena=True)
                            wdone(sl)
                        ckpt('H')
                        halves = []
                        for hf in range(2):
                            bs = [acq() for _ in range(NT)]
                            for g in range(8):
                                sl = wget(l, "f2", 8 * hf + g)
                                for t in range(NT):
                                    for k4 in range(4):
                                        f = 4 * g + k4
                                        rhs = wap(sl, 4, 512, k4, 0, 512)
                                        lhsT = abf(A_HT + f * 512 + t * 128, [[1, 128]])
                                        first, last = (f == 0), (f == 31)
                                        P.op("pe", lambda e, o=banks[bs[t]][:, :], lhsT=lhsT, rhs=rhs, first=first, last=last:
                                             e.matmul(o, lhsT=lhsT, rhs=rhs, start=first, stop=last),
                                             reads=[wslot_r[sl], reg(f"hT{f}")], writes=[bank_r[bs[t]]] if (first or last) else [],
                                             signal=(last or k4 == 3 and t == NT - 1), arena=True)
                                wdone(sl)
                            halves.append(bs)
                            if hf == 0 and blk + 1 < NB:
                                P.suffix = f'@{blk + 1}'
                                ckpt('S')
                                stage_A(src, s, l, blk + 1, pair_mode=True, part=2)
                                P.suffix = f'@{blk}'
                                ckpt('H')
                        for pair in ((0, 1), (2, 3)):
                            interleave([post_ln(t, [halves[0][t], halves[1][t]], 1, 2, 3, t % 2) for t in pair])
                            for t in pair:
                                rel(halves[0][t])
                                rel(halves[1][t])
                                row0 = blk * TB + t * 128
                                P.dma("pool", mk(ys[s], row0 * D, [[D, 128], [1, D]]), xt[:, t, :], store_chan(),
                                      reads=[xt_regs[t]], writes=[reg(f"y{s}_{blk * NT + t}")])

        try:
            body()
        except _Stop:
            pass

        fin = [c.last for c in ch_st] + [c.last for c in ch_kst]
        P.op("pool", lambda e: e.memset(negh[:, 0:1], -0.5), extra=fin)
        for eng in ("pe", "act", "dve", "pool"):
            if P.pending[eng]:
                if stop_at is None:
                    raise RuntimeError(f"pending tokens on {eng}")
                for p in P.pending[eng]:
                    p.val = P.cnt[eng]
                P.pending[eng] = []
        P.check()

        with nc.Block() as block:
            @block.sync
            def _(e):
                P.emit("sp", e)

            @block.gpsimd
            def _(e):
                P.emit("pool", e)

            @block.tensor
            def _(e):
                P.emit("pe", e)

            @block.scalar
            def _(e):
                P.emit("act", e)

            @block.vector
            def _(e):
                P.emit("dve", e)
    stats = {e: len(P.ops[e]) for e in ENGS}
    return nc, stats


def _deint():
    return np.concatenate([np.arange(0, 64, 2), np.arange(1, 64, 2)])


HEAD_ORDER_A = [0, 4, 1, 5, 2, 6, 3, 7]


def w_in_perm():
    perm = np.arange(NIN)
    di = _deint()
    qa = np.concatenate([h * 64 + di for h in HEAD_ORDER_A])
    ka = np.concatenate([C_KA + h * 64 + di for h in range(2)])
    perm[0:512] = qa
    perm[C_KA:C_KA + 128] = ka
    return perm


def na_bias_index():
    idx = np.full((5, 128, 5, 128), 465, np.int64)
    a = np.arange(128) // 64
    kc = np.arange(128) % 64
    e = np.arange(128) // 64
    c = np.arange(128) % 64
    cs = np.clip(c - 8, 0, 48)
    for cfg in range(5):
        off = cfg
        rs_rel = {0: -e, 1: -2 - e, 2: -4 + 0 * e, 3: -4 - e, 4: -6 - e}[cfg]
        for j in range(5):
            dr = 2 * (j - off) + a[:, None] - e[None, :]
            vrow = (dr >= rs_rel[None, :]) & (dr <= rs_rel[None, :] + 7)
            vcol = (kc[:, None] >= cs[None, :]) & (kc[:, None] <= cs[None, :] + 15)
            dc = kc[:, None] - c[None, :]
            val = (dr + 7) * 31 + (dc + 15)
            ok = vrow & vcol
            idx[cfg, :, j, :] = np.where(ok, val, 465)
    return idx.reshape(5, 128, 640)


def rope_table(S):
    t = np.arange(S)
    inv = 1.0 / (10000.0 ** (np.arange(16) * 2.0 / 32))
    ang = np.concatenate([(t // GRID_W)[:, None] * inv[None, :], (t % GRID_W)[:, None] * inv[None, :]], -1)
    return np.concatenate([np.cos(ang), np.sin(ang)], -1).astype(np.float32)


def prep_shared(inp, depth):
    f = lambda a: np.ascontiguousarray(np.asarray(a, dtype=np.float32))
    L = depth
    perm = w_in_perm()
    di = _deint()
    out = {}
    out["w_ada"] = f(inp["w_ada"][:L])
    out["b_ada"] = f(inp["b_ada"][:L])
    out["b_ada_pm"] = f(np.asarray(inp["b_ada"][:L]).reshape(L, 48, 128).transpose(2, 0, 1).reshape(128, L * 48))
    out["w_in"] = f(np.asarray(inp["w_in"][:L])[:, :, perm])
    qg = np.asarray(inp["q_norm_a"][:L])[:, di]
    kg = np.asarray(inp["k_norm_a"][:L])[:, di]
    out["qkg"] = f(np.concatenate([np.tile(qg, (1, 8)), np.tile(kg, (1, 2))], 1))
    rows = np.concatenate([h * 64 + np.arange(64) for h in HEAD_ORDER_A])
    out["w_out_a"] = f(np.asarray(inp["w_out_a"][:L])[:, rows, :])
    out["w_out_b"] = f(inp["w_out_b"][:L])
    out["w_out"] = f(inp["w_out"][:L])
    out["w_ff1"] = f(inp["w_ff1"][:L])
    out["w_ff2"] = f(inp["w_ff2"][:L])
    for k in ("ln_mix_g", "ln_mix_b", "ln_ff_g", "ln_ff_b"):
        out[k] = f(inp[k][:L])
    rpb = np.asarray(inp["rpb_b"][:L], dtype=np.float32).reshape(L, 8, 465)
    rpb_pad = np.concatenate([rpb, np.full((L, 8, 1), NEG_FILL, np.float32)], -1)
    idx = na_bias_index()
    out["bt"] = f(rpb_pad[:, :, idx].transpose(0, 2, 1, 3, 4))
    return out


_PROG_CACHE = {}


def run_cores(seq_lens, depth, core_inputs, stop_at=None):
    key = (tuple(seq_lens), depth, stop_at)
    if key not in _PROG_CACHE:
        _PROG_CACHE[key] = build_program(list(seq_lens), depth, stop_at)
    nc, stats = _PROG_CACHE[key]
    res = run_bass_kernel_spmd(nc, core_inputs, core_ids=list(range(len(core_inputs))))
    return res


def kernel(x_prompt, x_sample, c_prompt, c_sample, w_ada, b_ada, w_in, q_norm_a, k_norm_a, rpb_b,
           w_out_a, w_out_b, w_out, ln_mix_g, ln_mix_b, w_ff1, w_ff2, ln_ff_g, ln_ff_b):
    inp = dict(w_ada=w_ada, b_ada=b_ada, w_in=w_in, q_norm_a=q_norm_a, k_norm_a=k_norm_a, rpb_b=rpb_b,
               w_out_a=w_out_a, w_out_b=w_out_b, w_out=w_out, ln_mix_g=ln_mix_g, ln_mix_b=ln_mix_b,
               w_ff1=w_ff1, w_ff2=w_ff2, ln_ff_g=ln_ff_g, ln_ff_b=ln_ff_b)
    depth = 4
    ncores = 8
    x_prompt = np.asarray(x_prompt, dtype=np.float32)
    x_sample = np.asarray(x_sample, dtype=np.float32)
    c_prompt = np.asarray(c_prompt, dtype=np.float32)
    c_sample = np.asarray(c_sample, dtype=np.float32)
    shared = prep_shared(inp, depth)
    SP, SS = x_prompt.shape[1], x_sample.shape[1]
    npp = x_prompt.shape[0] // ncores
    nsp = x_sample.shape[0] // ncores
    seq_lens = [SP] * npp + [SS] * nsp
    for S in set(seq_lens):
        shared[f"cs{S}"] = rope_table(S)
    core_inputs = []
    for i in range(ncores):
        m = dict(shared)
        cs = []
        for j in range(npp):
            m[f"x{j}"] = np.ascontiguousarray(x_prompt[i * npp + j])
            cs.append(c_prompt[i * npp + j])
        for j in range(nsp):
            m[f"x{npp + j}"] = np.ascontiguousarray(x_sample[i * nsp + j])
            cs.append(c_sample[i * nsp + j])
        c = np.stack(cs, 0)
        NS = len(seq_lens)
        m["cT"] = np.ascontiguousarray(c.T.reshape(KC, 128, NS).transpose(1, 0, 2).reshape(128, KC * NS))
        core_inputs.append(m)
    res = run_cores(seq_lens, depth, core_inputs)
    yp = np.empty_like(x_prompt)
    ysm = np.empty_like(x_sample)
    for i in range(ncores):
        r = res.results[i]
        for j in range(npp):
            yp[i * npp + j] = r[f"y{j}"]
        for j in range(nsp):
            ysm[i * nsp + j] = r[f"y{npp + j}"]
    return (yp, ysm)
```

```python
import numpy as np
from contextlib import ExitStack
from collections import deque

import concourse.bass as bass
import concourse.mybir as mybir
from concourse.bass_utils import run_bass_kernel_spmd

F32 = mybir.dt.float32
BF16 = mybir.dt.bfloat16
ALU = mybir.AluOpType
AF = mybir.ActivationFunctionType
AX = mybir.AxisListType

D = 1024
NIN = 4352
DFF = 4096
KC = 8
GRID_W = 64
ALPHA = 8.0 ** 0.25
EPS = 1e-6
TB = 512
NT = 4
C_QA, C_KA, C_VA, C_QB, C_KB, C_VB, C_GA, C_GB = 0, 512, 640, 768, 1280, 1792, 2304, 3328
UNIT = 2048
NSLOT = 7
NEG_FILL = -30000.0

def unit_table():
    units = []
    units.append(("kvA", 0, "w_in", 0, 8, C_KA, 256))
    for i in range(2):
        units.append(("vB", i, "w_in", 4 * i, 4, C_VB, 512))
    for i in range(2):
        units.append(("kB", i, "w_in", 0, 8, C_KB + 256 * i, 256))
    for i in range(2):
        units.append(("qA", i, "w_in", 4 * i, 4, C_QA, 512))
    for i in range(2):
        units.append(("qB", i, "w_in", 0, 8, C_QB + 256 * i, 256))
    for i in range(4):
        units.append(("ga", i, "w_in", 0, 8, C_GA + 256 * i, 256))
    for i in range(4):
        units.append(("gb", i, "w_in", 0, 8, C_GB + 256 * i, 256))
    for i in range(2):
        units.append(("oa", i, "w_out_a", 0, 4, 512 * i, 512))
    for i in range(2):
        units.append(("ob", i, "w_out_b", 0, 4, 512 * i, 512))
    for h in range(2):
        for i in range(2):
            units.append(("wo", 2 * h + i, "w_out", 4 * i, 4, 512 * h, 512))
    for i in range(16):
        units.append(("f1", i, "w_ff1", 0, 8, 256 * i, 256))
    for h in range(2):
        for g in range(8):
            units.append(("f2", 8 * h + g, "w_ff2", 4 * g, 4, 512 * h, 512))
    return units


UNITS = unit_table()
UIDX = {(u[0], u[1]): i for i, u in enumerate(UNITS)}
NU = len(UNITS)


class Tok:
    __slots__ = ("sem", "val", "eng")

    def __init__(self, sem, val, eng):
        self.sem, self.val, self.eng = sem, val, eng


class Region:
    __slots__ = ("name", "w", "r")

    def __init__(self, name):
        self.name, self.w, self.r = name, None, {}


class Chan:
    def __init__(self, sem):
        self.sem, self.n, self.last = sem, 0, None


ENGS = ("pe", "act", "dve", "pool", "sp")


class _Recorder:
    def __init__(self):
        self.call = None

    def __getattr__(self, name):
        def f(*args, **kwargs):
            assert self.call is None
            self.call = (name, args, kwargs)
            return None
        return f


def _eager(fn):
    rec = _Recorder()
    fn(rec)
    name, args, kwargs = rec.call

    def replay(e):
        return getattr(e, name)(*args, **kwargs)
    return replay


class Prog:
    def __init__(self, nc, es):
        self.nc = nc
        self.es = es
        self.ops = {e: [] for e in ENGS}
        self.sem = {e: es.enter_context(nc.semaphore("s_" + e)) for e in ENGS}
        self.cnt = {e: 0 for e in ENGS}
        self.pending = {e: [] for e in ENGS}
        self.last_arena = {}
        self.guard = []
        self.nsem = 5
        self.stage = 'pre'
        self.suffix = ''
        self.annotate = False

    def chan(self, name):
        self.nsem += 1
        return Chan(self.es.enter_context(self.nc.semaphore(name)))

    def _deps(self, eng, reads, writes, extra):
        waits = []
        for r in reads:
            if r.w is not None and not (r.w.eng == eng and eng == "pe"):
                waits.append(r.w)
        for w in writes:
            if w.w is not None and not (w.w.eng == eng and eng == "pe"):
                waits.append(w.w)
            for e, t in w.r.items():
                if not (e == eng and eng == "pe"):
                    waits.append(t)
        waits.extend(t for t in extra if t is not None)
        return waits

    def op(self, eng, fn, reads=(), writes=(), extra=(), signal=True, arena=False):
        if arena:
            extra = list(extra) + self.guard
        waits = self._deps(eng, reads, writes, extra)
        if signal:
            self.cnt[eng] += 1
            tok = Tok(self.sem[eng], self.cnt[eng], eng)
            for p in self.pending[eng]:
                p.val = self.cnt[eng]
            self.pending[eng] = []
        else:
            tok = Tok(self.sem[eng], None, eng)
            self.pending[eng].append(tok)
        self.ops[eng].append((_eager(fn), waits, tok if signal else None, None, self.stage))
        for r in reads:
            r.r[eng] = tok
        for w in writes:
            w.w = tok
            w.r = {}
        if arena:
            self.last_arena[eng] = tok
        return tok

    def dma(self, q, out_ap, in_ap, chan, reads=(), writes=(), extra=(), arena=False, serialize=True):
        if arena:
            extra = list(extra) + self.guard
        waits = self._deps("dma", reads, writes, list(extra) + ([chan.last] if serialize else []))
        chan.n += 1
        tok = Tok(chan.sem, 16 * chan.n, None)
        chan.last = tok

        def fn(e, out_ap=out_ap, in_ap=in_ap):
            return e.dma_start(out=out_ap, in_=in_ap)

        self.ops[q].append((fn, waits, None, tok, self.stage))
        for r in reads:
            r.r["dma" + str(id(chan))] = tok
        for w in writes:
            w.w = tok
            w.r = {}
        if arena:
            self.last_arena["dma" + str(id(chan))] = tok
        return tok

    def switch_mode(self):
        self.guard = [t for t in self.last_arena.values()]
        self.last_arena = {}

    def check(self):
        for e in ENGS:
            assert not self.pending[e], f"unresolved pending tokens on {e}"
        ptr = {e: 0 for e in ENGS}
        semv = {}
        total = sum(len(v) for v in self.ops.values())
        done = 0
        progress = True
        while progress:
            progress = False
            for e in ENGS:
                ops = self.ops[e]
                while ptr[e] < len(ops):
                    fn, waits, tok, dtok, _st = ops[ptr[e]]
                    ok = True
                    for w in waits:
                        assert w.val is not None
                        if semv.get(id(w.sem), 0) < w.val:
                            ok = False
                            break
                    if not ok:
                        break
                    if tok is not None:
                        semv[id(tok.sem)] = semv.get(id(tok.sem), 0) + 1
                        assert semv[id(tok.sem)] == tok.val
                    if dtok is not None:
                        semv[id(dtok.sem)] = semv.get(id(dtok.sem), 0) + 16
                        assert semv[id(dtok.sem)] == dtok.val
                    ptr[e] += 1
                    done += 1
                    progress = True
        if done != total:
            msg = {e: (ptr[e], len(self.ops[e])) for e in ENGS}
            raise RuntimeError(f"static deadlock: {msg}")

    def emit(self, eng_name, e):
        waited = {}
        for fn, waits, tok, dtok, _st in self.ops[eng_name]:
            for w in waits:
                k = id(w.sem)
                if waited.get(k, 0) < w.val:
                    e.wait_ge(w.sem, w.val)
                    waited[k] = w.val
            ins = fn(e)
            if self.annotate:
                ins.annotate(_st)
            if tok is not None:
                ins.then_inc(tok.sem, 1)
            if dtok is not None:
                ins.then_inc(dtok.sem, 16)


def mk(tensor, off, dims):
    return bass.AP(tensor, off, [list(d) for d in dims])


class _Stop(Exception):
    pass


def build_program(seq_lens, depth, stop_at=None, annotate=False):
    NS = len(seq_lens)
    L = depth
    SMAX = max(seq_lens)
    lens_set = sorted(set(seq_lens))
    nc = bass.Bass("TRN2", target_bir_lowering=False)

    def dram_in(name, shape, dt=F32):
        return nc.dram_tensor(name, list(shape), dt, kind="ExternalInput")

    xs = [dram_in(f"x{s}", [seq_lens[s], D]) for s in range(NS)]
    ys = [nc.dram_tensor(f"y{s}", [seq_lens[s], D], F32, kind="ExternalOutput") for s in range(NS)]
    cT = dram_in("cT", [128, KC * NS])
    w_ada = dram_in("w_ada", [L, D, 6 * D])
    b_ada = dram_in("b_ada", [L, 6 * D])
    b_ada_pm = dram_in("b_ada_pm", [128, L * 48])
    wsrc = {
        "w_in": dram_in("w_in", [L, D, NIN]),
        "w_out_a": dram_in("w_out_a", [L, 512, D]),
        "w_out_b": dram_in("w_out_b", [L, 512, D]),
        "w_out": dram_in("w_out", [L, D, D]),
        "w_ff1": dram_in("w_ff1", [L, D, DFF]),
        "w_ff2": dram_in("w_ff2", [L, DFF, D]),
    }
    qkg = dram_in("qkg", [L, 640])
    lnv = {k: dram_in(k, [L, D]) for k in ("ln_mix_g", "ln_mix_b", "ln_ff_g", "ln_ff_b")}
    bt = dram_in("bt", [L, 5, 8, 128, 640])
    cstab = {S: dram_in(f"cs{S}", [S, 64]) for S in lens_set}

    ws = [nc.dram_tensor(f"ws{l}", [NU, 128, UNIT], BF16) for l in range(L)]
    ebs = nc.dram_tensor("ebs", [L, 5, 8, 128, 640], BF16)
    kbs = nc.dram_tensor("kbs", [128, 4, SMAX], BF16)
    vbs = nc.dram_tensor("vbs", [SMAX // 128, 128, 520], BF16)
    grow = nc.dram_tensor("grow", [L, NS, 2 * D], F32)

    es = ExitStack()
    with es:
        P = Prog(nc, es)
        P.annotate = annotate

        def sb(name, shape, dt):
            return es.enter_context(nc.sbuf_tensor(name, list(shape), dt))

        ident = sb("ident", [128, 128], BF16)
        negh = sb("negh", [128, 16], F32)
        MODS = sb("MODS", [128, L * NS * 4 * KC], F32)
        G12 = sb("G12", [128, 2, D], F32)
        LNV = sb("LNV", [128, 4, D], F32)
        QKG = sb("QKG", [128, 640], F32)
        CSr = sb("CSr", [128, 2, 64], F32)
        kAT = sb("kAT", [128, SMAX], BF16)
        vAug = sb("vAug", [128, SMAX // 128, 256], BF16)
        kBr = sb("kBr", [128, 4, 6, 128], BF16)
        vBr = sb("vBr", [128, 6, 520], BF16)
        EBr = sb("EBr", [128, 3, 640], BF16)
        wring = sb("wring", [128, NSLOT, UNIT], BF16)
        xt = sb("xt", [128, NT, D], F32)
        xn = sb("xn", [128, 4, D], BF16)
        xin = sb("xin", [128, 2, D], F32)
        uT = sb("uT", [128, KC, TB], BF16)
        WK = sb("WK", [128, 2, D], F32)
        st6 = sb("st6", [128, 8, 12], F32)
        mv = sb("mv", [128, 8, 4], F32)
        ssq = sb("ssq", [128, 4, 16], F32)
        RT = sb("RT", [128, 4, 256], F32)
        QR = sb("QR", [128, 2, 512], BF16)
        vBst = sb("vBst", [128, 2, 520], BF16)
        silc = sb("silc", [128, KC * NS], BF16)
        A_QAT, A_QBT, A_PA, A_PB, A_REC, A_AOT, A_OB, A_RECB, A_OBT, A_TA, A_TB_, A_T12, A_MIT = (
            0, 2048, 4096, 7168, 9088, 11136, 13184, 14208, 14272, 16320, 18368, 20416, 22464)
        A_END = 22464 + 4096
        A_HT, A_R = 0, 16384
        ARENA = max(A_END, A_R + 1024)
        arena = sb("arena", [128, ARENA], BF16)

        def abf(off, dims):
            return mk(arena, off, [[ARENA, 128]] + dims)

        banks = [es.enter_context(nc.psum_tensor(f"bank{i}", [128, 512], F32)) for i in range(8)]
        banks_bf = [b.bitcast(BF16) for b in banks]
        bank_r = [Region(f"bank{i}") for i in range(8)]
        free_banks = deque(range(8))

        def acq():
            assert free_banks, "out of PSUM banks"
            return free_banks.popleft()

        def rel(b):
            free_banks.append(b)

        R = {}

        def reg(name):
            if name not in R:
                R[name] = Region(name)
            return R[name]

        ch_x = [P.chan(f"chx{i}") for i in range(NT)]
        ch_w = [P.chan(f"chw{i}") for i in range(NSLOT)]
        ch_st = [P.chan(f"chst{i}") for i in range(4)]
        ch_kb = [P.chan(f"chkb{i}") for i in range(6)]
        ch_vb = [P.chan(f"chvb{i}") for i in range(6)]
        ch_eb = [P.chan(f"cheb{i}") for i in range(3)]
        ch_cs = [P.chan(f"chcs{i}") for i in range(2)]
        ch_misc = P.chan("chmisc")
        ch_xin = [P.chan(f"chxin{i}") for i in range(2)]
        ch_xinp = [P.chan(f"chxinp{i}") for i in range(2)]
        ch_miscp = P.chan("chmiscp")
        ch_wp = [P.chan(f"chwp{i}") for i in range(NSLOT)]
        ch_pre = [P.chan(f"chpre{l}") for l in range(L)]
        ch_kst = [P.chan(f"chkst{i}") for i in range(2)]
        st_rr = [0]

        def store_chan():
            c = ch_st[st_rr[0] % len(ch_st)]
            st_rr[0] += 1
            return c

        wslot_r = [Region(f"wslot{i}") for i in range(NSLOT)]
        wfree = deque(range(NSLOT))
        wsched = []
        wstate = {"next": 0, "ptr": 0, "loaded": {}}

        def wpump():
            while wstate["next"] < len(wsched) and wfree:
                l, u = wsched[wstate["next"]]
                slot = wfree.popleft()
                src = mk(ws[l], u * 128 * UNIT, [[UNIT, 128], [1, UNIT]])
                P.dma("sp", wring[:, slot, :], src, ch_w[slot],
                      reads=[reg(f"ws{l}")], writes=[wslot_r[slot]])
                wstate["loaded"][wstate["next"]] = slot
                wstate["next"] += 1

        def wget(l, name, idx):
            i = wstate["ptr"]
            assert wsched[i] == (l, UIDX[(name, idx)]), (wsched[i], l, name, idx)
            wpump()
            assert i in wstate["loaded"], "weight ring exhausted (would deadlock)"
            wstate["ptr"] += 1
            return wstate["loaded"][i]

        def wdone(slot):
            wfree.append(slot)
            wpump()

        def wap(slot, nk, ncols, k, c0, n):
            return mk(wring, slot * UNIT + k * ncols + c0, [[NSLOT * UNIT, 128], [1, n]])

        for s in range(NS):
            nb = seq_lens[s] // TB
            for l in range(L):
                for b in range(nb):
                    wsched += [(l, UIDX[("kvA", 0)]), (l, UIDX[("vB", 0)]), (l, UIDX[("vB", 1)]),
                               (l, UIDX[("kB", 0)]), (l, UIDX[("kB", 1)])]
                for b in range(nb):
                    seq = [("qA", 0), ("qA", 1), ("qB", 0), ("qB", 1)]
                    for g in range(2):
                        seq += [("ga", 2 * g), ("gb", 2 * g), ("oa", g), ("ob", g), ("ga", 2 * g + 1), ("gb", 2 * g + 1)]
                    seq += [("wo", i) for i in range(4)]
                    seq += [("f1", i) for i in range(16)]
                    seq += [("f2", i) for i in range(16)]
                    wsched += [(l, UIDX[k]) for k in seq]

        def ckpt(name):
            P.stage = name + P.suffix
            if stop_at == name:
                raise _Stop()

        def body():
            r_ident = reg("ident")
            P.op("pool", lambda e: e.memset(ident[:], 0.0), writes=[r_ident])
            P.op("pool", lambda e: e.affine_select(out=ident[:], in_=ident[:], pattern=[[-1, 128]],
                                                   compare_op=ALU.not_equal, fill=1.0, base=0,
                                                   channel_multiplier=1), reads=[r_ident], writes=[r_ident])
            P.op("pool", lambda e: e.memset(negh[:], -0.5), writes=[reg("negh")])
            P.op("pool", lambda e: e.memset(vAug[:], 1.0), writes=[reg("vAugAll")])
            P.op("pool", lambda e: e.memset(vBst[:], 1.0), writes=[reg("vBst0"), reg("vBst1")])

            def precast(l):
                for ui, (name, idx, src, k0, nk, c0, ncols) in enumerate(UNITS):
                    w = wsrc[src]
                    rows = w.shape[1]
                    cols = w.shape[2]
                    base = l * rows * cols + (k0 * 128) * cols + c0
                    sap = mk(w, base, [[cols, 128], [128 * cols, nk], [1, ncols]])
                    dap = mk(ws[l], ui * 128 * UNIT, [[UNIT, 128], [ncols, nk], [1, ncols]])
                    P.dma("pool", dap, sap, ch_pre[l], writes=[], serialize=False)
                reg(f"ws{l}").w = ch_pre[l].last

            ckpt('consts')
            precast(0)
            ckpt('precast0')

            ebreg = reg("ebs")
            ebtoks = []
            for l in range(L):
                for cfg in range(5):
                    for h in range(8):
                        k = (l * 5 + cfg) * 8 + h
                        slot = k % 2
                        wr = reg(f"WK{slot}")
                        er = reg(f"EBr{slot}")
                        off = ((l * 5 + cfg) * 8 + h) * 128 * 640
                        P.dma("sp", WK[:, slot, 0:640], mk(bt, off, [[640, 128], [1, 640]]), ch_x[slot], writes=[wr])
                        P.op("act", lambda e, slot=slot: e.activation(out=EBr[:, slot, :], in_=WK[:, slot, 0:640], func=AF.Exp),
                             reads=[wr], writes=[er])
                        t = P.dma("pool", mk(ebs, off, [[640, 128], [1, 640]]), EBr[:, slot, :], ch_st[slot], reads=[er])
                        ebtoks.append(t)
            ebreg.w = None
            ckpt('eb')
            eb_guard = [ch_st[0].last, ch_st[1].last]

            r_silc = reg("silc")
            P.dma("sp", WK[:, 0, 0:KC * NS], cT[:, :], ch_misc, writes=[reg("WK0")])
            P.op("act", lambda e: e.activation(out=silc[:], in_=WK[:, 0, 0:KC * NS], func=AF.Silu),
                 reads=[reg("WK0")], writes=[r_silc])
            r_bpm = reg("bpm")
            bpm = sb("bpm", [128, L * 48], F32)
            P.dma("sp", bpm[:], b_ada_pm[:, :], ch_misc, writes=[r_bpm])
            brow = mk(G12, 0, [[2 * D, NS], [1, 2 * D]])
            grow_sb = mk(LNV, 0, [[4 * D, NS], [1, 2 * D]])
            KIND_OF = {1: 0, 0: 1, 4: 2, 3: 3}
            for l in range(L):
                rb = reg("G12")
                P.dma("sp", brow[:, 0:D], mk(b_ada, l * 6 * D + 2 * D, [[0, NS], [1, D]]), ch_misc, writes=[rb])
                P.dma("sp", brow[:, D:2 * D], mk(b_ada, l * 6 * D + 5 * D, [[0, NS], [1, D]]), ch_misc, writes=[rb])
                for blk in range(6):
                    for half in range(2):
                        slots = []
                        for kh in range(2):
                            slot = wfree.popleft()
                            base = l * D * 6 * D + (kh * 4 * 128) * 6 * D + blk * D + half * 512
                            sap = mk(w_ada, base, [[6 * D, 128], [128 * 6 * D, 4], [1, 512]])
                            dap = mk(wring, slot * UNIT, [[NSLOT * UNIT, 128], [512, 4], [1, 512]])
                            P.dma("pool", dap, sap, ch_wp[slot], writes=[wslot_r[slot]])
                            slots.append(slot)
                        if blk in KIND_OF:
                            kind = KIND_OF[blk]
                            b = acq()
                            for cc in range(4):
                                for kc in range(KC):
                                    slot = slots[kc // 4]
                                    lhsT = mk(wring, slot * UNIT + (kc % 4) * 512 + cc * 128, [[NSLOT * UNIT, 128], [1, 128]])
                                    rhs = silc[:, kc * NS:(kc + 1) * NS]
                                    P.op("pe", lambda e, o=banks[b][:, cc * NS:(cc + 1) * NS], lhsT=lhsT, rhs=rhs, kc=kc:
                                         e.matmul(o, lhsT=lhsT, rhs=rhs, start=(kc == 0), stop=(kc == KC - 1)),
                                         reads=[wslot_r[slot], r_silc], writes=[bank_r[b]] if kc in (0, KC - 1) else [],
                                         signal=(kc == KC - 1))
                            for cc in range(4):
                                kc = half * 4 + cc
                                bcol = l * 48 + blk * 8 + kc
                                oap = mk(MODS, (l * NS * 4 + kind) * KC + kc, [[L * NS * 4 * KC, 128], [4 * KC, NS]])
                                addc = 1.0 if kind in (0, 2) else 0.0
                                P.op("dve", lambda e, oap=oap, i=banks[b][:, cc * NS:(cc + 1) * NS], sc=bpm[:, bcol:bcol + 1], addc=addc:
                                     e.tensor_scalar(out=oap, in0=i, scalar1=sc, scalar2=addc, op0=ALU.add, op1=ALU.add),
                                     reads=[bank_r[b], r_bpm], writes=[reg("MODS")])
                            rel(b)
                        else:
                            gi = 0 if blk == 2 else 1
                            b = acq()
                            for kc in range(KC):
                                slot = slots[kc // 4]
                                rhs = mk(wring, slot * UNIT + (kc % 4) * 512, [[NSLOT * UNIT, 128], [1, 512]])
                                lhsT = silc[:, kc * NS:(kc + 1) * NS]
                                P.op("pe", lambda e, o=banks[b][0:NS, :], lhsT=lhsT, rhs=rhs, kc=kc:
                                     e.matmul(o, lhsT=lhsT, rhs=rhs, start=(kc == 0), stop=(kc == KC - 1)),
                                     reads=[wslot_r[slot], r_silc], writes=[bank_r[b]] if kc in (0, KC - 1) else [],
                                     signal=(kc == KC - 1))
                            c0 = gi * D + half * 512
                            P.op("dve", lambda e, o=grow_sb[:, c0:c0 + 512], i=banks[b][0:NS, :], bb=brow[:, c0:c0 + 512]:
                                 e.tensor_tensor(out=o, in0=i, in1=bb, op=ALU.add),
                                 reads=[bank_r[b], rb], writes=[reg("LNV")])
                            P.op("dve", lambda e, o=grow_sb[:, c0:c0 + 512], gi=gi:
                                 e.tensor_scalar(out=o, in0=o, scalar1=1.0, scalar2=((0.5 if gi == 0 else 1.0) / ALPHA), op0=ALU.add, op1=ALU.mult),
                                 reads=[reg("LNV")], writes=[reg("LNV")])
                            rel(b)
                        for slot in slots:
                            wfree.append(slot)
                P.dma("pool", mk(grow, l * NS * 2 * D, [[2 * D, NS], [1, 2 * D]]), grow_sb[:, :], ch_miscp,
                      reads=[reg("LNV")], writes=[reg("grow")])

            ckpt('mods')
            for l in range(1, L):
                precast(l)
            ckpt('precast')

            def mods_ap(l, s, kind, kc):
                col = ((l * NS + s) * 4 + kind) * KC + kc
                return MODS[:, col:col + 1]

            def interleave(gens):
                gens = list(gens)
                while gens:
                    for g in list(gens):
                        try:
                            next(g)
                        except StopIteration:
                            gens.remove(g)

            def drain(g):
                for _ in g:
                    pass

            def ln_stats(src_ap, src_reg, j, eps=EPS):
                rs, rm = reg(f"st6_{j}"), reg(f"mv_{j}")
                P.op("dve", lambda e: e.bn_stats(out=st6[:, j, 0:6], in_=src_ap[:, 0:512]), reads=[src_reg], writes=[rs])
                yield
                P.op("dve", lambda e: e.bn_stats(out=st6[:, j, 6:12], in_=src_ap[:, 512:1024]), reads=[src_reg], writes=[rs])
                yield
                P.op("dve", lambda e: e.bn_aggr(out=mv[:, j, 0:2], in_=st6[:, j, :]), reads=[rs], writes=[rm])
                yield
                P.op("pool", lambda e: e.tensor_scalar(out=mv[:, j, 2:3], in0=mv[:, j, 1:2], scalar1=eps, scalar2=None, op0=ALU.add),
                     reads=[rm], writes=[rm])
                yield
                P.op("pool", lambda e: e.tensor_tensor(out=mv[:, j, 2:3], in0=mv[:, j, 2:3], in1=negh[:, 0:1], op=ALU.pow),
                     reads=[rm, reg("negh")], writes=[rm])
                yield
                P.op("dve", lambda e: e.scalar_tensor_tensor(out=mv[:, j, 3:4], in0=mv[:, j, 0:1], scalar=-1.0, in1=mv[:, j, 2:3],
                                                             op0=ALU.mult, op1=ALU.mult), reads=[rm], writes=[rm])
                yield

            def ln_to_uT(tiles_ap, tiles_reg, l, s, kind_sc, kind_sh, loader=None, sbase=0, pair_mode=False, part=0):
                nsl = 4
                rxs = [reg(f"xn{t % nsl}") for t in range(NT)]

                def ln_tile(t):
                    rm = reg(f"mv_{t + sbase}")
                    yield from ln_stats(tiles_ap[t], tiles_reg[t], t + sbase)
                    sl = t % nsl
                    P.op("act", lambda e, t=t, sl=sl: e.activation(out=xn[:, sl, :], in_=tiles_ap[t], func=AF.Identity,
                                                                   scale=mv[:, t + sbase, 2:3], bias=mv[:, t + sbase, 3:4]),
                         reads=[tiles_reg[t], rm], writes=[rxs[t]])
                    yield

                if part in (0, 1):
                    for pair in ((0, 1), (2, 3)):
                        if loader is not None:
                            for t in pair:
                                loader(t)
                        interleave([ln_tile(t) for t in pair])
                if part == 1:
                    return
                groups = [[0, 1], [2, 3]] if pair_mode else [[0, 1, 2, 3]]
                for grp in groups:
                    ng = len(grp)
                    w = ng * 128
                    per_bank = 1024 // w
                    nb_ = KC // per_bank
                    tb = [acq() for _ in range(nb_)]
                    for gi_, t in enumerate(grp):
                        sl = t % nsl
                        for kc in range(KC):
                            oap = banks_bf[tb[kc // per_bank]][:, (kc % per_bank) * w + gi_ * 128:(kc % per_bank) * w + (gi_ + 1) * 128]
                            last = (kc == KC - 1)
                            P.op("pe", lambda e, oap=oap, i=xn[:, sl, kc * 128:(kc + 1) * 128]: e.transpose(oap, i, ident[:]),
                                 reads=[rxs[t], r_ident], writes=[bank_r[x] for x in tb] if (last or kc == 0) else [], signal=last)
                    tok0 = grp[0] * 128
                    for kc in range(KC):
                        iap = banks_bf[tb[kc // per_bank]][:, (kc % per_bank) * w:(kc % per_bank) * w + w]
                        sc, sh = mods_ap(l, s, kind_sc, kc), mods_ap(l, s, kind_sh, kc)
                        P.op("act", lambda e, kc=kc, iap=iap, sc=sc, sh=sh: e.activation(out=uT[:, kc, tok0:tok0 + w], in_=iap, func=AF.Identity, scale=sc, bias=sh),
                             reads=[bank_r[tb[kc // per_bank]], reg("MODS")], writes=[reg(f"uT{kc}")])
                    for b in tb:
                        rel(b)

            uT_regs = [reg(f"uT{kc}") for kc in range(KC)]

            def proj_tok(slots, nk_per, ncols, c0, n, tile, b, col0=0):
                for kc in range(KC):
                    slot = slots[kc // nk_per]
                    rhs = wap(slot, nk_per, ncols, kc % nk_per, c0, n)
                    lhsT = uT[:, kc, tile * 128:(tile + 1) * 128]
                    P.op("pe", lambda e, o=banks[b][:, col0:col0 + n], lhsT=lhsT, rhs=rhs, kc=kc:
                         e.matmul(o, lhsT=lhsT, rhs=rhs, start=(kc == 0), stop=(kc == KC - 1)),
                         reads=[wslot_r[slot], uT_regs[kc]], writes=[bank_r[b]] if kc in (0, KC - 1) else [],
                         signal=(kc == KC - 1))

            def proj_feat(slot, cc, b, rhs_fn, rhs_regs, nk=KC, ncols=256):
                for kc in range(nk):
                    lhsT = wap(slot, nk, ncols, kc, cc * 128, 128)
                    P.op("pe", lambda e, o=banks[b][:, :], lhsT=lhsT, rhs=rhs_fn(kc), kc=kc:
                         e.matmul(o, lhsT=lhsT, rhs=rhs, start=(kc == 0), stop=(kc == nk - 1)),
                         reads=[wslot_r[slot], rhs_regs[kc]], writes=[bank_r[b]] if kc in (0, nk - 1) else [],
                         signal=(kc == nk - 1))

            def qk_norm_rope(b, ncols, nh, gcol0, cs_slot, cs_reg, out_ap, out_reg, k, arena_out=False):
                wr = reg(f"WK{k}")
                ps = banks[b][:, 0:ncols]
                SQ0 = k * D
                QN0 = k * D + 512
                P.op("act", lambda e: e.activation(out=WK[:, k, 0:ncols], in_=ps, func=AF.Square), reads=[bank_r[b]], writes=[wr])
                yield
                rs = reg(f"ssq{k}")
                P.op("dve", lambda e: e.tensor_reduce(out=ssq[:, 2 * k, 0:nh], in_=mk(WK, SQ0, [[2 * D, 128], [64, nh], [1, 64]]), axis=AX.X, op=ALU.add),
                     reads=[wr], writes=[rs])
                yield
                P.op("pool", lambda e: e.tensor_scalar(out=ssq[:, 2 * k + 1, 0:nh], in0=ssq[:, 2 * k, 0:nh], scalar1=1.0 / 64.0, scalar2=EPS, op0=ALU.mult, op1=ALU.add),
                     reads=[rs], writes=[rs])
                yield
                P.op("pool", lambda e: e.tensor_tensor(out=ssq[:, 2 * k + 1, 0:nh], in0=ssq[:, 2 * k + 1, 0:nh], in1=negh[:, 0:nh], op=ALU.pow),
                     reads=[rs, reg("negh")], writes=[rs])
                yield
                wq = reg(f"WKq{k}")
                P.op("dve", lambda e: e.tensor_tensor(out=mk(WK, QN0, [[2 * D, 128], [64, nh], [1, 64]]),
                                                      in0=mk(banks[b], 0, [[512, 128], [64, nh], [1, 64]]),
                                                      in1=mk(ssq, (2 * k + 1) * 16, [[64, 128], [1, nh], [0, 64]]), op=ALU.mult),
                     reads=[bank_r[b], rs], writes=[wq], extra=[wr.w])
                yield
                P.op("dve", lambda e: e.tensor_tensor(out=WK[:, k, 512:512 + ncols], in0=WK[:, k, 512:512 + ncols], in1=QKG[:, gcol0:gcol0 + ncols], op=ALU.mult),
                     reads=[wq, reg("QKG")], writes=[wq])
                yield
                ev = mk(WK, QN0, [[2 * D, 128], [64, nh], [1, 32]])
                od = mk(WK, QN0 + 32, [[2 * D, 128], [64, nh], [1, 32]])
                Cb = mk(CSr, cs_slot * 64, [[128, 128], [0, nh], [1, 32]])
                Sb = mk(CSr, cs_slot * 64 + 32, [[128, 128], [0, nh], [1, 32]])
                T1 = mk(RT, (2 * k) * 256, [[1024, 128], [32, nh], [1, 32]])
                T2 = mk(RT, (2 * k + 1) * 256, [[1024, 128], [32, nh], [1, 32]])
                oe = mk(out_ap.tensor, out_ap.offset, [[out_ap.ap[0][0], 128], [64, nh], [1, 32]])
                oo = mk(out_ap.tensor, out_ap.offset + 32, [[out_ap.ap[0][0], 128], [64, nh], [1, 32]])
                r1, r2 = reg(f"RT{k}a"), reg(f"RT{k}b")
                P.op("dve", lambda e: e.tensor_tensor(out=T1, in0=ev, in1=Cb, op=ALU.mult), reads=[wq, cs_reg], writes=[r1])
                yield
                P.op("dve", lambda e: e.tensor_tensor(out=T2, in0=od, in1=Sb, op=ALU.mult), reads=[wq, cs_reg], writes=[r2])
                yield
                P.op("dve", lambda e: e.tensor_tensor(out=oe, in0=T1, in1=T2, op=ALU.subtract), reads=[r1, r2], writes=[out_reg], arena=arena_out)
                yield
                P.op("dve", lambda e: e.tensor_tensor(out=T1, in0=ev, in1=Sb, op=ALU.mult), reads=[wq, cs_reg], writes=[r1])
                yield
                P.op("dve", lambda e: e.tensor_tensor(out=T2, in0=od, in1=Cb, op=ALU.mult), reads=[wq, cs_reg], writes=[r2])
                yield
                P.op("dve", lambda e: e.tensor_tensor(out=oo, in0=T1, in1=T2, op=ALU.add), reads=[r1, r2], writes=[out_reg], arena=arena_out)
                yield

            def post_ln(t, half_banks, gi, lg, lb, wk):
                rw = reg(f"WK{wk}")
                rx = reg(f"xt{t}")
                for h in range(2):
                    P.op("dve", lambda e, h=h: e.tensor_tensor(out=WK[:, wk, h * 512:(h + 1) * 512], in0=banks[half_banks[h]][:, :],
                                                              in1=G12[:, gi, h * 512:(h + 1) * 512], op=ALU.mult),
                         reads=[bank_r[half_banks[h]], reg("G12")], writes=[rw])
                    yield
                P.op("dve", lambda e: e.tensor_tensor(out=WK[:, wk, :], in0=WK[:, wk, :], in1=xt[:, t, :], op=ALU.add),
                     reads=[rx, rw], writes=[rw])
                yield
                rm = reg(f"mv_{t}")
                yield from ln_stats(WK[:, wk, :], rw, t, eps=EPS / (ALPHA * ALPHA))
                P.op("act", lambda e: e.activation(out=WK[:, wk, :], in_=WK[:, wk, :], func=AF.Identity, scale=mv[:, t, 2:3], bias=mv[:, t, 3:4]),
                     reads=[rw, rm], writes=[rw])
                yield
                eng2 = "dve" if t % 2 == 0 else "pool"
                P.op(eng2, lambda e: e.tensor_tensor(out=WK[:, wk, :], in0=WK[:, wk, :], in1=LNV[:, lg, :], op=ALU.mult),
                     reads=[rw, reg("LNV")], writes=[rw])
                yield
                P.op(eng2, lambda e: e.tensor_tensor(out=xt[:, t, :], in0=WK[:, wk, :], in1=LNV[:, lb, :], op=ALU.add),
                     reads=[rw, reg("LNV")], writes=[rx])
                yield

            xt_regs = [reg(f"xt{t}") for t in range(NT)]
            xt_aps = [xt[:, t, :] for t in range(NT)]

            def load_x(src, s, blk):
                for t in range(NT):
                    row0 = blk * TB + t * 128
                    P.dma("sp", xt[:, t, :], mk(src, row0 * D, [[D, 128], [1, D]]), ch_x[t],
                          reads=[reg(f"y{s}_{blk * NT + t}")], writes=[xt_regs[t]])

            xin_aps = [xin[:, t % 2, :] for t in range(NT)]
            xin_regs = [reg(f"xin{t % 2}") for t in range(NT)]

            def stage_A(src, s, l, blk, pair_mode=False, part=0):
                def loader(t):
                    row0 = blk * TB + t * 128
                    P.dma("pool", xin[:, t % 2, :], mk(src, row0 * D, [[D, 128], [1, D]]), ch_xinp[t % 2],
                          reads=[reg(f"y{s}_{blk * NT + t}")], writes=[xin_regs[t]])
                ln_to_uT(xin_aps, xin_regs, l, s, 0, 1, loader=loader, sbase=4, pair_mode=pair_mode, part=part)

            def load_cs(S, tile, slot):
                P.dma("sp", CSr[:, slot, :], mk(cstab[S], tile * 128 * 64, [[64, 128], [1, 64]]), ch_cs[slot], writes=[reg(f"CS{slot}")])
                return reg(f"CS{slot}")

            cs_ctr = [0]

            first_layer_loaded = [False]
            for s in range(NS):
                S = seq_lens[s]
                NB = S // TB
                NTIL = S // 128
                for l in range(L):
                    src = xs[s] if l == 0 else ys[s]
                    misc_extra = eb_guard if not first_layer_loaded[0] else []
                    first_layer_loaded[0] = True
                    P.dma("sp", G12[:, 0, :], mk(grow, (l * NS + s) * 2 * D, [[0, 128], [1, D]]), ch_misc, reads=[reg("grow")], writes=[reg("G12")], extra=misc_extra)
                    P.dma("sp", G12[:, 1, :], mk(grow, (l * NS + s) * 2 * D + D, [[0, 128], [1, D]]), ch_misc, reads=[reg("grow")], writes=[reg("G12")])
                    for i, k in enumerate(("ln_mix_g", "ln_mix_b", "ln_ff_g", "ln_ff_b")):
                        P.dma("sp", LNV[:, i, :], mk(lnv[k], l * D, [[0, 128], [1, D]]), ch_misc, writes=[reg("LNV")])
                    P.dma("sp", QKG[:, :], mk(qkg, l * 640, [[0, 128], [1, 640]]), ch_misc, writes=[reg("QKG")])

                    P.switch_mode()
                    for blk in range(NB):
                        ckpt('p1x')
                        if blk == 0:
                            stage_A(src, s, l, blk, part=1)
                        stage_A(src, s, l, blk, part=2)
                        if blk + 1 < NB:
                            stage_A(src, s, l, blk + 1, part=1)
                        ckpt('p1a')
                        s_kv = wget(l, "kvA", 0)
                        s_vb = [wget(l, "vB", 0), wget(l, "vB", 1)]
                        for pair in ((0, 1), (2, 3)):
                            pb_, pcs = {}, {}
                            for t in pair:
                                tile = blk * NT + t
                                b = acq()
                                pb_[t] = b
                                proj_tok([s_kv], 8, 256, 0, 256, t, b)
                                csl = cs_ctr[0] % 2
                                cs_ctr[0] += 1
                                pcs[t] = (csl, load_cs(S, tile, csl))
                                rv = reg(f"vAug{tile}")
                                P.op("act", lambda e, b=b, tile=tile: e.activation(
                                    out=mk(vAug, tile * 256, [[(SMAX // 128) * 256, 128], [192, 2], [1, 64]]),
                                    in_=mk(banks[b], 128, [[512, 128], [64, 2], [1, 64]]), func=AF.Copy),
                                    reads=[bank_r[b]], writes=[rv], extra=[reg("vAugAll").w])
                            ckpt('p1b')
                            interleave([qk_norm_rope(pb_[t], 128, 2, 512, pcs[t][0], pcs[t][1], QR[:, t % 2, 0:128], reg(f"QR{t % 2}"), t % 2)
                                        for t in pair])
                            for t in pair:
                                tile = blk * NT + t
                                kq = t % 2
                                rqr = reg(f"QR{kq}")
                                rel(pb_[t])
                                b2 = acq()
                                P.op("pe", lambda e, b2=b2: e.transpose(banks_bf[b2][:, 0:128], QR[:, kq, 0:128], ident[:]),
                                     reads=[rqr, r_ident], writes=[bank_r[b2]])
                                P.op("dve", lambda e, b2=b2, tile=tile: e.tensor_copy(out=kAT[:, tile * 128:(tile + 1) * 128], in_=banks_bf[b2][:, 0:128]),
                                     reads=[bank_r[b2]], writes=[reg(f"kAT{tile}")])
                                rel(b2)
                            ckpt('p1c')
                            for t in pair:
                                tile = blk * NT + t
                                b3 = acq()
                                proj_tok(s_vb, 4, 512, 0, 512, t, b3)
                                vsl = tile % 2
                                rvs = reg(f"vBst{vsl}")
                                P.op("act", lambda e, b3=b3, vsl=vsl: e.activation(
                                    out=mk(vBst, vsl * 520, [[2 * 520, 128], [65, 8], [1, 64]]),
                                    in_=mk(banks[b3], 0, [[512, 128], [64, 8], [1, 64]]), func=AF.Copy),
                                    reads=[bank_r[b3]], writes=[rvs])
                                rel(b3)
                                P.dma("pool", mk(vbs, tile * 128 * 520, [[520, 128], [1, 520]]), vBst[:, vsl, :], ch_kst[vsl],
                                      reads=[rvs], writes=[reg(f"vbs{tile}")])
                        wdone(s_kv)
                        for x in s_vb:
                            wdone(x)
                        ckpt('p1d')
                        rks = reg("kBst")
                        for i in range(2):
                            sl = wget(l, "kB", i)
                            for cc in range(2):
                                hp = 2 * i + cc
                                b = acq()
                                proj_feat(sl, cc, b, lambda kc: uT[:, kc, :], uT_regs)
                                P.op("dve", lambda e, b=b, hp=hp: e.tensor_copy(out=abf(hp * TB, [[1, TB]]), in_=banks[b][:, :]),
                                     reads=[bank_r[b]], writes=[rks], arena=True)
                                rel(b)
                            wdone(sl)
                        P.dma("pool", mk(kbs, blk * TB, [[4 * SMAX, 128], [SMAX, 4], [1, TB]]), abf(0, [[TB, 4], [1, TB]]), store_chan(),
                              reads=[rks], writes=[reg(f"kbs{blk}")], arena=True)

                    ckpt('p1')
                    kb_slot_of = {}
                    kb_state = {"next": 0}

                    def na_window(T):
                        return min(max(T - 2, 0), NTIL - 5)

                    def ensure_kv(upto):
                        while kb_state["next"] <= upto:
                            c = kb_state["next"]
                            slot = c % 6
                            P.dma("sp", kBr[:, :, slot, :], mk(kbs, c * 128, [[4 * SMAX, 128], [SMAX, 4], [1, 128]]), ch_kb[slot],
                                  reads=[reg(f"kbs{c // NT}")], writes=[reg(f"kBr{slot}")])
                            P.dma("sp", vBr[:, slot, :], mk(vbs, c * 128 * 520, [[520, 128], [1, 520]]), ch_vb[slot],
                                  reads=[reg(f"vbs{c}")], writes=[reg(f"vBr{slot}")])
                            kb_state["next"] += 1

                    eb_ctr = [0]
                    for blk in range(NB):
                        P.suffix = f'@{blk}'
                        ckpt('S')
                        if blk == 0:
                            stage_A(src, s, l, blk)
                        P.switch_mode()
                        load_x(src, s, blk)
                        ckpt('A')
                        s_qa = [wget(l, "qA", 0), wget(l, "qA", 1)]
                        qbanks = []
                        for t in range(NT):
                            b = acq()
                            proj_tok(s_qa, 4, 512, 0, 512, t, b)
                            qbanks.append(b)
                        for x in s_qa:
                            wdone(x)
                        for i in range(2):
                            sl = wget(l, "qB", i)
                            for cc in range(2):
                                hp = 2 * i + cc
                                b = acq()
                                proj_feat(sl, cc, b, lambda kc: uT[:, kc, :], uT_regs)
                                P.op("dve", lambda e, b=b, hp=hp: e.tensor_copy(out=abf(A_QBT + hp * 512, [[1, 512]]), in_=banks[b][:, :]),
                                     reads=[bank_r[b]], writes=[reg(f"qBT{hp}")], arena=True)
                                rel(b)
                            wdone(sl)
                        for pair in ((0, 1), (2, 3)):
                            gens = []
                            for t in pair:
                                tile = blk * NT + t
                                csl = cs_ctr[0] % 2
                                cs_ctr[0] += 1
                                csreg = load_cs(S, tile, csl)
                                kq = t % 2
                                gens.append(qk_norm_rope(qbanks[t], 512, 8, 0, csl, csreg, QR[:, kq, 0:512], reg(f"QR{kq}"), kq))
                            interleave(gens)
                            for t in pair:
                                kq = t % 2
                                rqr = reg(f"QR{kq}")
                                rel(qbanks[t])
                                b2 = acq()
                                for c in range(4):
                                    P.op("pe", lambda e, b2=b2, c=c: e.transpose(banks_bf[b2][:, c * 128:(c + 1) * 128], QR[:, kq, c * 128:(c + 1) * 128], ident[:]),
                                         reads=[rqr, r_ident], writes=[bank_r[b2]] if c in (0, 3) else [], signal=(c == 3))
                                P.op("act", lambda e, b2=b2, t=t: e.activation(
                                    out=abf(A_QAT + t * 128, [[512, 4], [1, 128]]),
                                    in_=mk(banks_bf[b2], 0, [[1024, 128], [128, 4], [1, 128]]), func=AF.Copy),
                                    reads=[bank_r[b2]], writes=[reg(f"qAT{c}") for c in range(4)], arena=True)
                                rel(b2)
                        ckpt('NA')
                        na_st = {}
                        na_obk = {}
                        na_units = [(t, h) for t in range(NT) for h in range(8)]

                        def na_qk(t, h, u):
                            T = blk * NT + t
                            c0 = na_window(T)
                            cfg = 0 if T == 0 else 1 if T == 1 else 3 if T == NTIL - 2 else 4 if T == NTIL - 1 else 2
                            if h == 0:
                                ensure_kv(c0 + 4)
                            hp, side = h // 2, h % 2
                            esl = eb_ctr[0] % 3
                            eb_ctr[0] += 1
                            off = ((l * 5 + cfg) * 8 + h) * 128 * 640
                            P.dma("sp", EBr[:, esl, :], mk(ebs, off, [[640, 128], [1, 640]]), ch_eb[esl], writes=[reg(f"EBr{esl}")])
                            sa, sb2 = acq(), acq()
                            rhs = mk(arena, side * 64 * ARENA + A_QBT + hp * 512 + t * 128, [[ARENA, 64], [1, 128]])
                            for j in range(5):
                                slot = (c0 + j) % 6
                                lhsT = mk(kBr, side * 64 * (4 * 6 * 128) + (hp * 6 + slot) * 128, [[4 * 6 * 128, 64], [1, 128]])
                                o = banks[sa][:, j * 128:(j + 1) * 128] if j < 4 else banks[sb2][:, 0:128]
                                P.op("pe", lambda e, o=o, lhsT=lhsT, rhs=rhs: e.matmul(o, lhsT=lhsT, rhs=rhs, start=True, stop=True),
                                     reads=[reg(f"kBr{slot}"), reg(f"qBT{hp}")],
                                     writes=[bank_r[sa]] if j in (0, 3) else [bank_r[sb2]] if j == 4 else [],
                                     signal=(j >= 3), arena=True)
                            psl = u % 3
                            rp = reg(f"pB{psl}")
                            P.op("act", lambda e: e.activation(out=abf(A_PB + psl * 640, [[1, 512]]), in_=banks[sa][:, :], func=AF.Exp, scale=0.125),
                                 reads=[bank_r[sa]], writes=[rp], arena=True)
                            P.op("act", lambda e: e.activation(out=abf(A_PB + psl * 640 + 512, [[1, 128]]), in_=banks[sb2][:, 0:128], func=AF.Exp, scale=0.125),
                                 reads=[bank_r[sb2]], writes=[rp], arena=True)
                            rel(sa)
                            rel(sb2)
                            P.op("dve", lambda e: e.tensor_tensor(out=abf(A_PB + psl * 640, [[1, 640]]), in0=abf(A_PB + psl * 640, [[1, 640]]),
                                                                  in1=EBr[:, esl, :], op=ALU.mult),
                                 reads=[rp, reg(f"EBr{esl}")], writes=[rp], arena=True)
                            na_st[(t, h)] = (psl, c0)

                        def na_pv(t, h):
                            psl, c0 = na_st[(t, h)]
                            rp = reg(f"pB{psl}")
                            if h == 0:
                                na_obk[t] = [acq(), acq()]
                            obk = na_obk[t]
                            bk = obk[h // 4]
                            for j in range(5):
                                slot = (c0 + j) % 6
                                lhsT = abf(A_PB + psl * 640 + j * 128, [[1, 128]])
                                rhs2 = mk(vBr, slot * 520 + h * 65, [[6 * 520, 128], [1, 65]])
                                o = banks[bk][:, (h % 4) * 65:(h % 4) * 65 + 65]
                                P.op("pe", lambda e, o=o, lhsT=lhsT, rhs2=rhs2, j=j: e.matmul(o, lhsT=lhsT, rhs=rhs2, start=(j == 0), stop=(j == 4)),
                                     reads=[rp, reg(f"vBr{slot}")], writes=[bank_r[bk]] if j in (0, 4) else [], signal=(j == 4), arena=True)
                            if h != 7:
                                return
                            osl = t % 2
                            rob = reg(f"ob{osl}")
                            for g in range(2):
                                bk = obk[g]
                                rb_ap = mk(arena, A_RECB + osl * 32 + g * 8, [[ARENA, 128], [1, 8]]).bitcast(F32)
                                P.op("dve", lambda e, bk=bk, rb_ap=rb_ap: e.reciprocal(out=rb_ap, in_=mk(banks[bk], 64, [[512, 128], [65, 4]])),
                                     reads=[bank_r[bk]], writes=[reg(f"recB{osl}")], arena=True)
                                P.op("dve", lambda e, bk=bk, rb_ap=rb_ap, g=g, osl=osl: e.tensor_tensor(
                                    out=abf(A_OB + osl * 512 + g * 256, [[64, 4], [1, 64]]),
                                    in0=mk(banks[bk], 0, [[512, 128], [65, 4], [1, 64]]),
                                    in1=mk(rb_ap.tensor, rb_ap.offset, [[rb_ap.ap[0][0], 128], [1, 4], [0, 64]]), op=ALU.mult),
                                    reads=[bank_r[bk], reg(f"recB{osl}")], writes=[rob], arena=True)
                                rel(bk)
                            b2 = acq()
                            for hp in range(4):
                                P.op("pe", lambda e, b2=b2, hp=hp, osl=osl: e.transpose(banks_bf[b2][:, hp * 128:(hp + 1) * 128],
                                                                                        abf(A_OB + osl * 512 + hp * 128, [[1, 128]]), ident[:]),
                                     reads=[rob, r_ident], writes=[bank_r[b2]] if hp in (0, 3) else [], signal=(hp == 3), arena=True)
                            P.op("dve", lambda e, b2=b2, t=t: e.tensor_copy(out=abf(A_OBT + t * 128, [[512, 4], [1, 128]]),
                                                                            in_=mk(banks_bf[b2], 0, [[1024, 128], [128, 4], [1, 128]])),
                                 reads=[bank_r[b2]], writes=[reg(f"obT{hp}") for hp in range(4)], arena=True)
                            rel(b2)

                        na_pend = []
                        for u in range(len(na_units)):
                            na_qk(na_units[u][0], na_units[u][1], u)
                            if len(na_pend) == 2:
                                na_pv(*na_pend.pop(0))
                            na_pend.append(na_units[u])
                        while na_pend:
                            na_pv(*na_pend.pop(0))

                        ckpt('GA')
                        NK = NTIL
                        for c in range(4):
                            ob_ = [acq(), acq()]
                            prev = None
                            pa_ctr = 0

                            def pv(prev):
                                kt, slots_ = prev
                                for side in range(2):
                                    lhsT = mk(vAug, kt * 256 + side * 128, [[(SMAX // 128) * 256, 128], [1, 128]])
                                    rhs = abf(A_PA + slots_[side] * 512, [[1, 512]])
                                    last = (kt == NK - 1)
                                    P.op("pe", lambda e, o=banks[ob_[side]][:, :], lhsT=lhsT, rhs=rhs, kt=kt:
                                         e.matmul(o, lhsT=lhsT, rhs=rhs, start=(kt == 0), stop=(kt == NK - 1)),
                                         reads=[reg(f"vAug{kt}"), reg(f"pA{slots_[side]}")],
                                         writes=[bank_r[ob_[side]]] if (kt == 0 or last) else [], signal=True, arena=True)

                            pend = []
                            for kt in range(NK):
                                sb_ = [acq(), acq()]
                                slots_ = [(kt % 3) * 2, (kt % 3) * 2 + 1]
                                for side in range(2):
                                    lhsT = kAT[side * 64:(side + 1) * 64, kt * 128:(kt + 1) * 128]
                                    rhs = mk(arena, side * 64 * ARENA + A_QAT + c * 512, [[ARENA, 64], [1, 512]])
                                    P.op("pe", lambda e, o=banks[sb_[side]][:, :], lhsT=lhsT, rhs=rhs:
                                         e.matmul(o, lhsT=lhsT, rhs=rhs, start=True, stop=True),
                                         reads=[reg(f"kAT{kt}"), reg(f"qAT{c}")], writes=[bank_r[sb_[side]]], arena=True)
                                if len(pend) == 2:
                                    pv(pend.pop(0))
                                for side in range(2):
                                    P.op("act", lambda e, i=banks[sb_[side]][:, :], o=abf(A_PA + slots_[side] * 512, [[1, 512]]):
                                         e.activation(out=o, in_=i, func=AF.Exp, scale=0.125),
                                         reads=[bank_r[sb_[side]]], writes=[reg(f"pA{slots_[side]}")], arena=True)
                                    rel(sb_[side])
                                pend.append((kt, slots_))
                            while pend:
                                pv(pend.pop(0))
                            for side in range(2):
                                b = ob_[side]
                                lo, hi = (0, 64) if side == 0 else (64, 128)
                                dlo, dhi = (64, 128) if side == 0 else (0, 64)
                                rr = reg(f"rec{side}")
                                rec_f = mk(arena, lo * ARENA + A_REC + side * 1024, [[ARENA, 64], [1, 1024]]).bitcast(F32)
                                P.op("dve", lambda e, b=b, rec_f=rec_f, dlo=dlo, dhi=dhi: e.reciprocal(out=rec_f, in_=banks[b][dlo:dhi, :]),
                                     reads=[bank_r[b]], writes=[rr], arena=True)
                                P.op("dve", lambda e, b=b, rec_f=rec_f, lo=lo, hi=hi, c=c:
                                     e.tensor_tensor(out=mk(arena, lo * ARENA + A_AOT + c * 512, [[ARENA, 64], [1, 512]]),
                                                     in0=banks[b][lo:hi, :], in1=rec_f, op=ALU.mult),
                                     reads=[bank_r[b], rr], writes=[reg(f"aoT{c}")], arena=True)
                                rel(b)

                        ckpt('D')
                        for g in range(2):
                            sl_ga = [None, None]
                            sl_gb = [None, None]
                            sl_ga[0] = wget(l, "ga", 2 * g)
                            sl_gb[0] = wget(l, "gb", 2 * g)
                            sl_oa = wget(l, "oa", g)
                            sl_ob = wget(l, "ob", g)
                            for q in range(4):
                                n = 4 * g + q
                                if q == 2:
                                    sl_ga[1] = wget(l, "ga", 2 * g + 1)
                                    sl_gb[1] = wget(l, "gb", 2 * g + 1)
                                bga, bgb, boa, bob = acq(), acq(), acq(), acq()
                                proj_feat(sl_ga[q // 2], q % 2, bga, lambda kc: uT[:, kc, :], uT_regs)
                                proj_feat(sl_gb[q // 2], q % 2, bgb, lambda kc: uT[:, kc, :], uT_regs)
                                proj_feat(sl_oa, q, boa, lambda kc: abf(A_AOT + kc * 512, [[1, 512]]), [reg(f"aoT{c}") for c in range(4)], nk=4, ncols=512)
                                proj_feat(sl_ob, q, bob, lambda kc: abf(A_OBT + kc * 512, [[1, 512]]), [reg(f"obT{c}") for c in range(4)], nk=4, ncols=512)
                                if q == 1:
                                    wdone(sl_ga[0])
                                    wdone(sl_gb[0])
                                ta = abf(A_TA + q * 512, [[1, 512]])
                                tb_ = abf(A_TB_ + q * 512, [[1, 512]])
                                P.op("act", lambda e, ta=ta, bga=bga: e.activation(out=ta, in_=banks[bga][:, :], func=AF.Tanh, scale=0.5),
                                     reads=[bank_r[bga]], writes=[reg(f"ta{q}")], arena=True)
                                P.op("act", lambda e, tb_=tb_, bgb=bgb: e.activation(out=tb_, in_=banks[bgb][:, :], func=AF.Tanh, scale=0.5),
                                     reads=[bank_r[bgb]], writes=[reg(f"tb{q}")], arena=True)
                                rel(bga)
                                rel(bgb)
                                t1 = abf(A_T12, [[1, 1024]]).bitcast(F32)
                                t2 = abf(A_T12 + 1024, [[1, 1024]]).bitcast(F32)
                                P.op("dve", lambda e, t1=t1, ta=ta, boa=boa: e.scalar_tensor_tensor(out=t1, in0=ta, scalar=1.0, in1=banks[boa][:, :], op0=ALU.add, op1=ALU.mult),
                                     reads=[reg(f"ta{q}"), bank_r[boa]], writes=[reg("t1")], arena=True)
                                P.op("dve", lambda e, t2=t2, tb_=tb_, bob=bob: e.scalar_tensor_tensor(out=t2, in0=tb_, scalar=1.0, in1=banks[bob][:, :], op0=ALU.add, op1=ALU.mult),
                                     reads=[reg(f"tb{q}"), bank_r[bob]], writes=[reg("t2")], arena=True)
                                rel(boa)
                                rel(bob)
                                P.op("pool", lambda e, t1=t1, t2=t2, n=n: e.tensor_tensor(out=abf(A_MIT + n * 512, [[1, 512]]), in0=t1, in1=t2, op=ALU.add),
                                     reads=[reg("t1"), reg("t2")], writes=[reg(f"miT{n}")], arena=True)
                            wdone(sl_ga[1])
                            wdone(sl_gb[1])
                            wdone(sl_oa)
                            wdone(sl_ob)

                        ckpt('E')
                        halves = []
                        for hf in range(2):
                            sl = [wget(l, "wo", 2 * hf), wget(l, "wo", 2 * hf + 1)]
                            bs = [acq() for _ in range(NT)]
                            for t in range(NT):
                                for kc in range(KC):
                                    slot = sl[kc // 4]
                                    rhs = wap(slot, 4, 512, kc % 4, 0, 512)
                                    lhsT = abf(A_MIT + kc * 512 + t * 128, [[1, 128]])
                                    P.op("pe", lambda e, o=banks[bs[t]][:, :], lhsT=lhsT, rhs=rhs, kc=kc: e.matmul(o, lhsT=lhsT, rhs=rhs, start=(kc == 0), stop=(kc == KC - 1)),
                                         reads=[wslot_r[slot], reg(f"miT{kc}")], writes=[bank_r[bs[t]]] if kc in (0, KC - 1) else [], signal=(kc == KC - 1), arena=True)
                            for x in sl:
                                wdone(x)
                            halves.append(bs)
                        for pair in ((0, 1), (2, 3)):
                            interleave([post_ln(t, [halves[0][t], halves[1][t]], 0, 0, 1, t % 2) for t in pair])
                            for t in pair:
                                rel(halves[0][t])
                                rel(halves[1][t])

                        ckpt('F')
                        ln_to_uT(xt_aps, xt_regs, l, s, 2, 3)
                        P.switch_mode()
                        if blk + 1 < NB:
                            P.suffix = f'@{blk + 1}'
                            ckpt('S')
                            stage_A(src, s, l, blk + 1, part=1)
                            P.suffix = f'@{blk}'
                        ckpt('G')
                        for i in range(16):
                            sl = wget(l, "f1", i)
                            for cc in range(2):
                                f = 2 * i + cc
                                b = acq()
                                proj_feat(sl, cc, b, lambda kc: uT[:, kc, :], uT_regs)
                                rs_ = f % 2
                                rr = reg(f"r{rs_}")
                                P.op("act", lambda e, b=b, rs_=rs_: e.activation(out=abf(A_R + rs_ * 512, [[1, 512]]), in_=banks[b][:, :], func=AF.Relu),
                                     reads=[bank_r[b]], writes=[rr], arena=True)
                                rel(b)
                                P.op("pool", lambda e, rs_=rs_, f=f: e.tensor_tensor(out=abf(A_HT + f * 512, [[1, 512]]), in0=abf(A_R + rs_ * 512, [[1, 512]]),
                                                                                     in1=abf(A_R + rs_ * 512, [[1, 512]]), op=ALU.mult),
                                     reads=[rr], writes=[reg(f"hT{f}")], arena=True)
                            wdone(sl)
                        ckpt('H')
                        halves = []
                        for hf in range(2):
                            bs = [acq() for _ in range(NT)]
                            for g in range(8):
                                sl = wget(l, "f2", 8 * hf + g)
                                for t in range(NT):
                                    for k4 in range(4):
                                        f = 4 * g + k4
                                        rhs = wap(sl, 4, 512, k4, 0, 512)
                                        lhsT = abf(A_HT + f * 512 + t * 128, [[1, 128]])
                                        first, last = (f == 0), (f == 31)
                                        P.op("pe", lambda e, o=banks[bs[t]][:, :], lhsT=lhsT, rhs=rhs, first=first, last=last:
                                             e.matmul(o, lhsT=lhsT, rhs=rhs, start=first, stop=last),
                                             reads=[wslot_r[sl], reg(f"hT{f}")], writes=[bank_r[bs[t]]] if (first or last) else [],
                                             signal=(last or k4 == 3 and t == NT - 1), arena=True)
                                wdone(sl)
                            halves.append(bs)
                            if hf == 0 and blk + 1 < NB:
                                P.suffix = f'@{blk + 1}'
                                ckpt('S')
                                stage_A(src, s, l, blk + 1, pair_mode=True, part=2)
                                P.suffix = f'@{blk}'
                                ckpt('H')
                        for pair in ((0, 1), (2, 3)):
                            interleave([post_ln(t, [halves[0][t], halves[1][t]], 1, 2, 3, t % 2) for t in pair])
                            for t in pair:
                                rel(halves[0][t])
                                rel(halves[1][t])
                                row0 = blk * TB + t * 128
                                P.dma("pool", mk(ys[s], row0 * D, [[D, 128], [1, D]]), xt[:, t, :], store_chan(),
                                      reads=[xt_regs[t]], writes=[reg(f"y{s}_{blk * NT + t}")])

        try:
            body()
        except _Stop:
            pass

        fin = [c.last for c in ch_st] + [c.last for c in ch_kst]
        P.op("pool", lambda e: e.memset(negh[:, 0:1], -0.5), extra=fin)
        for eng in ("pe", "act", "dve", "pool"):
            if P.pending[eng]:
                if stop_at is None:
                    raise RuntimeError(f"pending tokens on {eng}")
                for p in P.pending[eng]:
                    p.val = P.cnt[eng]
                P.pending[eng] = []
        P.check()

        with nc.Block() as block:
            @block.sync
            def _(e):
                P.emit("sp", e)

            @block.gpsimd
            def _(e):
                P.emit("pool", e)

            @block.tensor
            def _(e):
                P.emit("pe", e)

            @block.scalar
            def _(e):
                P.emit("act", e)

            @block.vector
            def _(e):
                P.emit("dve", e)
    stats = {e: len(P.ops[e]) for e in ENGS}
    return nc, stats


def _deint():
    return np.concatenate([np.arange(0, 64, 2), np.arange(1, 64, 2)])


HEAD_ORDER_A = [0, 4, 1, 5, 2, 6, 3, 7]


def w_in_perm():
    perm = np.arange(NIN)
    di = _deint()
    qa = np.concatenate([h * 64 + di for h in HEAD_ORDER_A])
    ka = np.concatenate([C_KA + h * 64 + di for h in range(2)])
    perm[0:512] = qa
    perm[C_KA:C_KA + 128] = ka
    return perm


def na_bias_index():
    idx = np.full((5, 128, 5, 128), 465, np.int64)
    a = np.arange(128) // 64
    kc = np.arange(128) % 64
    e = np.arange(128) // 64
    c = np.arange(128) % 64
    cs = np.clip(c - 8, 0, 48)
    for cfg in range(5):
        off = cfg
        rs_rel = {0: -e, 1: -2 - e, 2: -4 + 0 * e, 3: -4 - e, 4: -6 - e}[cfg]
        for j in range(5):
            dr = 2 * (j - off) + a[:, None] - e[None, :]
            vrow = (dr >= rs_rel[None, :]) & (dr <= rs_rel[None, :] + 7)
            vcol = (kc[:, None] >= cs[None, :]) & (kc[:, None] <= cs[None, :] + 15)
            dc = kc[:, None] - c[None, :]
            val = (dr + 7) * 31 + (dc + 15)
            ok = vrow & vcol
            idx[cfg, :, j, :] = np.where(ok, val, 465)
    return idx.reshape(5, 128, 640)


def rope_table(S):
    t = np.arange(S)
    inv = 1.0 / (10000.0 ** (np.arange(16) * 2.0 / 32))
    ang = np.concatenate([(t // GRID_W)[:, None] * inv[None, :], (t % GRID_W)[:, None] * inv[None, :]], -1)
    return np.concatenate([np.cos(ang), np.sin(ang)], -1).astype(np.float32)


def prep_shared(inp, depth):
    f = lambda a: np.ascontiguousarray(np.asarray(a, dtype=np.float32))
    L = depth
    perm = w_in_perm()
    di = _deint()
    out = {}
    out["w_ada"] = f(inp["w_ada"][:L])
    out["b_ada"] = f(inp["b_ada"][:L])
    out["b_ada_pm"] = f(np.asarray(inp["b_ada"][:L]).reshape(L, 48, 128).transpose(2, 0, 1).reshape(128, L * 48))
    out["w_in"] = f(np.asarray(inp["w_in"][:L])[:, :, perm])
    qg = np.asarray(inp["q_norm_a"][:L])[:, di]
    kg = np.asarray(inp["k_norm_a"][:L])[:, di]
    out["qkg"] = f(np.concatenate([np.tile(qg, (1, 8)), np.tile(kg, (1, 2))], 1))
    rows = np.concatenate([h * 64 + np.arange(64) for h in HEAD_ORDER_A])
    out["w_out_a"] = f(np.asarray(inp["w_out_a"][:L])[:, rows, :])
    out["w_out_b"] = f(inp["w_out_b"][:L])
    out["w_out"] = f(inp["w_out"][:L])
    out["w_ff1"] = f(inp["w_ff1"][:L])
    out["w_ff2"] = f(inp["w_ff2"][:L])
    for k in ("ln_mix_g", "ln_mix_b", "ln_ff_g", "ln_ff_b"):
        out[k] = f(inp[k][:L])
    rpb = np.asarray(inp["rpb_b"][:L], dtype=np.float32).reshape(L, 8, 465)
    rpb_pad = np.concatenate([rpb, np.full((L, 8, 1), NEG_FILL, np.float32)], -1)
    idx = na_bias_index()
    out["bt"] = f(rpb_pad[:, :, idx].transpose(0, 2, 1, 3, 4))
    return out


_PROG_CACHE = {}


def run_cores(seq_lens, depth, core_inputs, stop_at=None):
    key = (tuple(seq_lens), depth, stop_at)
    if key not in _PROG_CACHE:
        _PROG_CACHE[key] = build_program(list(seq_lens), depth, stop_at)
    nc, stats = _PROG_CACHE[key]
    res = run_bass_kernel_spmd(nc, core_inputs, core_ids=list(range(len(core_inputs))))
    return res


def kernel(x_prompt, x_sample, c_prompt, c_sample, w_ada, b_ada, w_in, q_norm_a, k_norm_a, rpb_b,
           w_out_a, w_out_b, w_out, ln_mix_g, ln_mix_b, w_ff1, w_ff2, ln_ff_g, ln_ff_b):
    inp = dict(w_ada=w_ada, b_ada=b_ada, w_in=w_in, q_norm_a=q_norm_a, k_norm_a=k_norm_a, rpb_b=rpb_b,
               w_out_a=w_out_a, w_out_b=w_out_b, w_out=w_out, ln_mix_g=ln_mix_g, ln_mix_b=ln_mix_b,
               w_ff1=w_ff1, w_ff2=w_ff2, ln_ff_g=ln_ff_g, ln_ff_b=ln_ff_b)
    depth = 4
    ncores = 8
    x_prompt = np.asarray(x_prompt, dtype=np.float32)
    x_sample = np.asarray(x_sample, dtype=np.float32)
    c_prompt = np.asarray(c_prompt, dtype=np.float32)
    c_sample = np.asarray(c_sample, dtype=np.float32)
    shared = prep_shared(inp, depth)
    SP, SS = x_prompt.shape[1], x_sample.shape[1]
    npp = x_prompt.shape[0] // ncores
    nsp = x_sample.shape[0] // ncores
    seq_lens = [SP] * npp + [SS] * nsp
    for S in set(seq_lens):
        shared[f"cs{S}"] = rope_table(S)
    core_inputs = []
    for i in range(ncores):
        m = dict(shared)
        cs = []
        for j in range(npp):
            m[f"x{j}"] = np.ascontiguousarray(x_prompt[i * npp + j])
            cs.append(c_prompt[i * npp + j])
        for j in range(nsp):
            m[f"x{npp + j}"] = np.ascontiguousarray(x_sample[i * nsp + j])
            cs.append(c_sample[i * nsp + j])
        c = np.stack(cs, 0)
        NS = len(seq_lens)
        m["cT"] = np.ascontiguousarray(c.T.reshape(KC, 128, NS).transpose(1, 0, 2).reshape(128, KC * NS))
        core_inputs.append(m)
    res = run_cores(seq_lens, depth, core_inputs)
    yp = np.empty_like(x_prompt)
    ysm = np.empty_like(x_sample)
    for i in range(ncores):
        r = res.results[i]
        for j in range(npp):
            yp[i * npp + j] = r[f"y{j}"]
        for j in range(nsp):
            ysm[i * nsp + j] = r[f"y{npp + j}"]
    return (yp, ysm)
```
